# Optimizing a Trainium2 kernel written in Bass

```python
import jax, jax.numpy as jnp
from jax import lax
import numpy as np

D_MODEL = 1024
BATCH = 2
SEQ = 8192
DEPTH = 1

GLA_HEADS = 4
GLA_DK = 128
GLA_DV = 256
GLA_KW = GLA_HEADS * GLA_DK
GLA_VW = GLA_HEADS * GLA_DV
GATE_RANK = 16
GATE_NORMALIZER = 16.0
GLA_CHUNK = 64
CONV_WIDTH = 1024
CONV_GROUPS = 8
CONV_K = 3
PLE_DIM = 256
EPS = 1e-6

SPLITS = [GLA_KW, GLA_KW, GLA_VW, GLA_VW, GATE_RANK,
          CONV_WIDTH, CONV_WIDTH, CONV_WIDTH, CONV_WIDTH,
          D_MODEL, D_MODEL]
IN_COLS = sum(SPLITS)

kernel_name = "hybrid_gla_shortconv_gated_merge"


def rms_norm(x, g):
    xf = x.astype(jnp.float32)
    y = xf * lax.rsqrt(jnp.mean(xf * xf, axis=-1, keepdims=True) + EPS)
    return (y * g.astype(jnp.float32)).astype(x.dtype)


def gla_chunked(q, k, v, gk):
    bsz, s, h, dk = q.shape
    dv = v.shape[-1]
    n = s // GLA_CHUNK

    def to_chunks(t):
        return t.astype(jnp.float32).reshape(bsz, n, GLA_CHUNK, h, t.shape[-1]).transpose(1, 0, 3, 2, 4)

    q, k, v, gk = to_chunks(q) * (dk ** -0.5), to_chunks(k), to_chunks(v), to_chunks(gk)
    b = jnp.cumsum(gk, axis=3)
    b_last = b[:, :, :, -1:, :]
    q_in = q * jnp.exp(b)
    k_in = k * jnp.exp(-b)
    k_dec = k * jnp.exp(b_last - b)
    causal = jnp.tril(jnp.ones((GLA_CHUNK, GLA_CHUNK), dtype=bool))
    a = jnp.where(causal, jnp.einsum('nbhid,nbhjd->nbhij', q_in, k_in), 0.0)
    o_intra = jnp.einsum('nbhij,nbhjv->nbhiv', a, v)
    decay = jnp.exp(b_last[:, :, :, 0, :])

    def step(state, xs):
        q_n, k_n, v_n, d_n = xs
        o_n = jnp.einsum('bhcd,bhdv->bhcv', q_n, state)
        state = d_n[..., None] * state + jnp.einsum('bhcd,bhcv->bhdv', k_n, v_n)
        return state, o_n

    state0 = jnp.zeros((bsz, h, dk, dv), jnp.float32)
    _, o_inter = lax.scan(step, state0, (q_in, k_dec, v, decay))
    o = o_intra + o_inter
    return o.transpose(1, 0, 3, 2, 4).reshape(bsz, s, h, dv)


def causal_dwconv(u, w):
    s = u.shape[1]
    u_pad = jnp.pad(u, ((0, 0), (CONV_K - 1, 0), (0, 0)))
    y = w[0] * u_pad[:, 0:s, :]
    for j in range(1, CONV_K):
        y = y + w[j] * u_pad[:, j:j + s, :]
    return y


def setup_inputs(seed: int = 0) -> dict:
    key = jax.random.key(seed)
    ks = jax.random.split(key, 20)
    nrm = lambda k, shape, scale: jax.random.normal(k, shape, jnp.float32) * scale
    gain = lambda k, shape: 1.0 + 0.05 * jax.random.normal(k, shape, jnp.float32)
    return {
        "x": nrm(ks[0], (BATCH, SEQ, D_MODEL), 1.0),
        "p": nrm(ks[1], (DEPTH, BATCH, SEQ, PLE_DIM), 1.0),
        "norm_mix_g": gain(ks[2], (DEPTH, D_MODEL)),
        "w_in": nrm(ks[3], (DEPTH, D_MODEL, IN_COLS), D_MODEL ** -0.5),
        "b_merge": nrm(ks[4], (DEPTH, 2 * D_MODEL), 0.02),
        "w_gk2": nrm(ks[5], (DEPTH, GATE_RANK, GLA_KW), GATE_RANK ** -0.5),
        "b_gk": nrm(ks[6], (DEPTH, GLA_KW), 0.02),
        "gla_norm_g": gain(ks[7], (DEPTH, GLA_DV)),
        "conv_w": nrm(ks[8], (DEPTH, CONV_K, CONV_WIDTH), CONV_K ** -0.5),
        "w_branch_a": nrm(ks[9], (DEPTH, GLA_VW, D_MODEL), GLA_VW ** -0.5),
        "w_branch_c": nrm(ks[10], (DEPTH, CONV_WIDTH, D_MODEL), CONV_WIDTH ** -0.5),
        "w_out": nrm(ks[11], (DEPTH, D_MODEL, D_MODEL), D_MODEL ** -0.5),
        "norm_ple_g": gain(ks[12], (DEPTH, D_MODEL)),
        "w_ple_gate": nrm(ks[13], (DEPTH, D_MODEL, D_MODEL), D_MODEL ** -0.5),
        "w_ple_proj": nrm(ks[14], (DEPTH, PLE_DIM, D_MODEL), PLE_DIM ** -0.5),
        "norm_final_g": gain(ks[15], (D_MODEL,)),
    }


def reference(x, p, norm_mix_g, w_in, b_merge, w_gk2, b_gk, gla_norm_g, conv_w,
              w_branch_a, w_branch_c, w_out, norm_ple_g, w_ple_gate, w_ple_proj,
              norm_final_g):
    bsz, s, _ = x.shape
    idx = np.cumsum(SPLITS)[:-1].tolist()
    for i in range(DEPTH):
        h = rms_norm(x, norm_mix_g[i])
        proj = jnp.einsum('bsd,dc->bsc', h, w_in[i])
        (q, k, v, z_a, gk_low, cb, cc, xc, z_c, g_a, g_c) = jnp.split(proj, idx, axis=-1)

        gk = jax.nn.log_sigmoid((jnp.einsum('bsr,rk->bsk', gk_low, w_gk2[i]) + b_gk[i]).astype(jnp.float32)) / GATE_NORMALIZER
        o = gla_chunked(q.reshape(bsz, s, GLA_HEADS, GLA_DK),
                        k.reshape(bsz, s, GLA_HEADS, GLA_DK),
                        v.reshape(bsz, s, GLA_HEADS, GLA_DV),
                        gk.reshape(bsz, s, GLA_HEADS, GLA_DK))
        o = rms_norm(o, gla_norm_g[i]).reshape(bsz, s, GLA_VW).astype(x.dtype)
        y_a = jnp.einsum('bsv,vd->bsd', o * jax.nn.silu(z_a), w_branch_a[i])

        u = causal_dwconv(cc * xc, conv_w[i])
        y_c = jnp.einsum('bsc,cd->bsd', (cb * u) * jax.nn.silu(z_c), w_branch_c[i])

        b_ga, b_gc = b_merge[i][:D_MODEL], b_merge[i][D_MODEL:]
        merged = jax.nn.sigmoid(g_a + b_ga) * y_a + jax.nn.sigmoid(g_c + b_gc) * y_c
        x = x + jnp.einsum('bsd,de->bse', merged, w_out[i])

        ple_gate = jax.nn.sigmoid(jnp.einsum('bsd,de->bse', rms_norm(x, norm_ple_g[i]), w_ple_gate[i]))
        x = x + ple_gate * jnp.einsum('bsq,qd->bsd', p[i], w_ple_proj[i])
    return rms_norm(x, norm_final_g)
```

```python
import contextlib
import numpy as np
import concourse.bass as bass
import concourse.mybir as mybir
from concourse.bass_utils import run_bass_kernel_spmd

F32 = mybir.dt.float32
BF16 = mybir.dt.bfloat16
AF = mybir.ActivationFunctionType
ALU = mybir.AluOpType

P = 128
D = 1024
KC = 8
TB = 512
NT = 4
SEQ = 8192
NCORES = 8
SEG = 2048
NOWN = SEG // TB
NPRE = (SEQ - SEG) // TB
IN_COLS = 9232
C_Q, C_K, C_V, C_ZA, C_GL, C_CB, C_CC, C_XC, C_ZC, C_GA, C_GC = (
    0, 512, 1024, 2048, 3072, 3088, 4112, 5136, 6160, 7184, 8208)
EPS = 1e-6
NWSLOT = 6
NTMP = 10
CM_W = 6 * P + 512 + 3
COMPUTE = ("pe", "act", "dve", "pool")


class Sched:
    def __init__(self, nc):
        self.nc = nc
        self.ops = []
        self.lastw = {}
        self.readers = {}
        self.dma_count = {}
        self.final_waits = []
        self.total_keys = set()

    def add(self, eng, fn, reads=(), writes=(), dma=None, out_dma=False, total=False):
        i = len(self.ops)
        deps = {}
        for k in reads:
            w = self.lastw.get(k)
            if w is not None:
                deps[w] = "raw"
        for k in writes:
            w = self.lastw.get(k)
            if w is not None and w not in deps:
                deps[w] = "waw"
            for r in self.readers.get(k, ()):
                if r not in deps:
                    deps[r] = "war"
        for k in reads:
            self.readers.setdefault(k, []).append(i)
        for k in writes:
            self.lastw[k] = i
            self.readers[k] = []
        deps.pop(i, None)
        op = dict(eng=eng, fn=fn, deps=deps, dma=dma, need_inc=False)
        if dma is not None:
            c = self.dma_count.get(dma, 0) + 1
            self.dma_count[dma] = c
            op["dma_val"] = 16 * c
            if total:
                self.total_keys.add(dma)
            if out_dma and dma not in self.final_waits:
                self.final_waits.append(dma)
        self.ops.append(op)
        return i

    def emit(self):
        nc = self.nc
        ops = self.ops
        for op in ops:
            keep = {}
            for d, kind in op["deps"].items():
                p = ops[d]
                if p["dma"] is None and op["dma"] is None and p["eng"] == op["eng"]:
                    if op["eng"] == "pe" or kind != "raw":
                        continue
                keep[d] = kind
            latest = {}
            for d in list(keep):
                p = ops[d]
                if p["dma"] is None:
                    e_ = p["eng"]
                    if e_ in latest:
                        lo_ = min(latest[e_], d)
                        latest[e_] = max(latest[e_], d)
                        keep.pop(lo_)
                    else:
                        latest[e_] = d
            op["deps"] = keep
            for d in keep:
                ops[d]["need_inc"] = True
        cnt = {e: 0 for e in COMPUTE}
        for op in ops:
            if op["dma"] is None and op["need_inc"]:
                cnt[op["eng"]] += 1
                op["ms"] = cnt[op["eng"]]
        with contextlib.ExitStack() as stack:
            sems = {e: stack.enter_context(nc.semaphore("s_" + e)) for e in COMPUTE}
            dsem = {}
            for n, k in enumerate(self.dma_count):
                dsem[k] = stack.enter_context(nc.semaphore("d%d" % n))
            block = stack.enter_context(nc.Block())

            def run(engname, handle):
                waited = {}
                for op in ops:
                    if op["eng"] != engname:
                        continue
                    need = {}
                    for d in op["deps"]:
                        p = ops[d]
                        if p["dma"] is not None:
                            s, v = dsem[p["dma"]], p["dma_val"]
                            if p["dma"] in self.total_keys:
                                v = 16 * self.dma_count[p["dma"]]
                        else:
                            s, v = sems[p["eng"]], p["ms"]
                        if need.get(id(s), (None, 0))[1] < v:
                            need[id(s)] = (s, v)
                    for key, (s, v) in need.items():
                        if waited.get(key, 0) >= v:
                            continue
                        handle.wait_ge(s, v)
                        waited[key] = v
                    inst = op["fn"](handle)
                    if op["dma"] is not None:
                        inst.then_inc(dsem[op["dma"]], 16)
                    elif op["need_inc"]:
                        inst.then_inc(sems[op["eng"]], 1)
                if engname == "sp":
                    for k in self.final_waits:
                        handle.wait_ge(dsem[k], 16 * self.dma_count[k])

            @block.sync
            def _(h):
                run("sp", h)

            @block.scalar
            def _(h):
                run("act", h)

            @block.vector
            def _(h):
                run("dve", h)

            @block.gpsimd
            def _(h):
                run("pool", h)

            @block.tensor
            def _(h):
                run("pe", h)


def build_program(npre=NPRE, nown=NOWN):
    nc = bass.Bass("TRN2", target_bir_lowering=False)
    dr = lambda name, shape, kind="ExternalInput", dt=F32: nc.dram_tensor(name, shape, dt, kind=kind).ap()
    x_own = dr("x_own", [nown * TB, D])
    x_pre = dr("x_pre", [max(npre, 1) * TB, D])
    x_prev = dr("x_prev", [P, D])
    p_own = dr("p_own", [nown * TB, 256])
    w_in = dr("w_in", [D, IN_COLS])
    w_gk2 = dr("w_gk2", [16, 512])
    w_a = dr("w_a", [D, D])
    w_c = dr("w_c", [D, D])
    w_o = dr("w_o", [D, D])
    w_pg = dr("w_pg", [D, D])
    w_pe = dr("w_pe", [256, D])
    rowp = dr("rowp", [P, 3 * D + 512])
    colp = dr("colp", [P, 42])
    cmat = dr("cmat", [P, CM_W])
    out = dr("out", [nown * TB, D], kind="ExternalOutput")
    NGRP = 26
    s_all = dr("s_all", [NGRP, P, KC * 512], kind="Internal", dt=BF16)
    s_in, s_a, s_c, s_o, s_pg = "s_in", "s_a", "s_c", "s_o", "s_pg"
    gmap = {}

    S = Sched(nc)
    with contextlib.ExitStack() as st:
        sb = lambda name, shape, dt=F32: st.enter_context(nc.sbuf_tensor(name, shape, dt))
        rowp_t = sb("rowp_t", [P, 3 * D])
        colp_t = sb("colp_t", [P, 42])
        cm32 = sb("cm32", [P, P + 512])
        cmb = sb("cmb", [P, 4 * P + 3], BF16)
        identb = sb("identb", [P, P], BF16)
        onesb = sb("onesb", [P, P], BF16)
        wst = sb("wst", [P, 512])
        wcat = sb("wcat", [P, 512], BF16)
        wgl3 = sb("wgl3", [P, KC, P], BF16)
        wpe_t = sb("wpe_t", [P, 2, D], BF16)
        wslot = [sb("wslot%d" % i, [P, KC, 512], BF16) for i in range(NWSLOT)]
        xtok = [sb("xtok0", [P, NT, D])]
        htok = sb("htok", [P, NT, D], BF16)
        hT = sb("hT", [P, KC, TB], BF16)
        junk = sb("junk", [P, D], BF16)
        stat = sb("stat", [P, 16])
        glcat = sb("glcat", [P, TB], BF16)
        sph = sb("sph", [P, NT, 512], BF16)
        spl = sb("spl", [P, NT, 512], BF16)
        kdec = sb("kdec", [P, NT, 512], BF16)
        vm = sb("vm", [P, 8, 512], BF16)
        qk = sb("qk", [P, 8, 512], BF16)
        dec = sb("dec", [P, 32])
        at_t = sb("at_t", [P, NT, 512], BF16)
        s32 = sb("s32", [P, 4, 256])
        big = sb("big", [P, 8 * 1024], BF16)
        sbf = big[:, 0:4096].rearrange("p (c n) -> p c n", c=4)
        sqb = sb("sqb", [P, 2, TB], BF16)
        ozT = sb("ozT", [P, KC, TB], BF16)
        utmp = [sb("utmp%d" % i, [P, TB + 2]) for i in range(2)]
        ucar = [sb("ucar%d" % i, [P, KC, 2]) for i in range(2)]
        ptok = [sb("ptok0", [P, NT, 256], BF16)]
        xstage = sb("xstage", [P, 2, D])
        pT = sb("pT", [P, 2, TB], BF16)
        _ot = big[:, 4096:8192].bitcast(F32)
        otile = [_ot[:, i * D:(i + 1) * D] for i in range(2)]
        tmps = [sb("tmp%d" % i, [P, TB]) for i in range(NTMP)]
        tctr = [0]

        def newtmp():
            i = tctr[0] % NTMP
            tctr[0] += 1
            return tmps[i], ("tmp", i)

        banks = [st.enter_context(nc.psum_tensor("bank%d" % i, [P, 512], F32)) for i in range(8)]
        bctr = [0]

        def newbank():
            i = bctr[0] % 8
            bctr[0] += 1
            return banks[i], ("bank", i)

        g_mix = rowp_t[:, 0:D]
        g_ple = rowp_t[:, D:2 * D]
        g_fin = rowp_t[:, 2 * D:3 * D]
        mask4 = cm32[:, P:P + 512]
        tri_incl = cmb[:, 0:P]
        tri_strict = cmb[:, P:2 * P]
        chunkind = cmb[:, 2 * P:2 * P + 2]
        negcol = cmb[:, 2 * P + 2:2 * P + 3]
        tri_strict128 = cmb[:, 2 * P + 3:3 * P + 3]
        allneg = cmb[:, 3 * P + 3:4 * P + 3]
        ident32 = cm32[:, 0:P]
        Mst = big[:].bitcast(F32).rearrange("p (h f) -> p h f", h=4)
        xt = xtok[0]
        xk = "xtok0"
        xkt = lambda t: ("xt", t)

        S.add("sp", lambda e: e.dma_start(out=rowp_t[:], in_=rowp[:, 0:3 * D]), writes=["rowp"], dma="const", total=True)
        S.add("sp", lambda e: e.dma_start(out=colp_t[:], in_=colp), writes=["colp"], dma="const", total=True)
        S.add("sp", lambda e: e.dma_start(out=cm32[:, 0:P], in_=cmat[:, 0:P]), writes=["cm32a"], dma="const", total=True)
        S.add("sp", lambda e: e.dma_start(out=cm32[:, P:P + 512], in_=cmat[:, 4 * P:4 * P + 512]), writes=["cm32"], dma="const", total=True)
        S.add("pool", lambda e: e.dma_start(out=cmb[:, 0:2 * P], in_=cmat[:, 2 * P:4 * P]), writes=["cmb"], dma="const2", total=True)
        S.add("pool", lambda e: e.dma_start(out=cmb[:, 2 * P:4 * P + 3], in_=cmat[:, 4 * P + 512:CM_W]), writes=["cmb2"], dma="const2", total=True)
        S.add("dve", lambda e: e.memset(wst[:], 0.0), writes=["wst"])
        for g0 in (0, 32, 64):
            S.add("sp", lambda e, g0=g0: e.dma_start(out=wst[g0:g0 + 16, :], in_=w_gk2), reads=["wst"], writes=[("wstw", g0)], dma="const3", total=True)
        for g0 in (0, 64):
            S.add("sp", lambda e, g0=g0: e.dma_start(out=wst[g0 + 16:g0 + 17, :], in_=rowp[0:1, 3 * D:3 * D + 512]), reads=["wst"], writes=[("wstb", g0)], dma="const3", total=True)
        wst_keys = [("wstw", 0), ("wstw", 32), ("wstw", 64), ("wstb", 0), ("wstb", 64)]
        S.add("dve", lambda e: e.tensor_copy(out=wcat[:], in_=wst[:]), reads=wst_keys, writes=["wcat"])
        S.add("dve", lambda e: e.tensor_tensor(out=wcat[64:96, :], in0=wst[64:96, :], in1=wcat[64:96, :], op=ALU.subtract), reads=wst_keys + ["wcat"], writes=["wcat"])
        S.add("dve", lambda e: e.memset(glcat[0:32, :], 1.0), writes=["glcat"])
        S.add("dve", lambda e: e.memset(glcat[32:64, :], 0.0), writes=["glcat"])
        S.add("dve", lambda e: e.memset(glcat[64:128, :], 1.0), writes=["glcat"])
        S.add("dve", lambda e: e.memset(wgl3[:], 0.0), writes=["wgl0"])
        S.add("dve", lambda e: e.memset(Mst, 0.0), writes=["Mst"])
        S.add("pool", lambda e: e.dma_start(out=identb[:], in_=cmat[:, 0:P]), writes=["identb"], dma="const2", total=True)
        S.add("pool", lambda e: e.dma_start(out=onesb[:], in_=cmat[:, P:2 * P]), writes=["onesb"], dma="const2", total=True)
        wview = lambda w, c0, n: w[:, c0:c0 + n].rearrange("(kc p) n -> p kc n", p=P)
        for g0 in (0, 32, 64):
            S.add("pool", lambda e, g0=g0: e.dma_start(out=wgl3[:, :, g0:g0 + 16], in_=wview(w_in, C_GL, 16)), reads=["wgl0"], writes=[("wgl", g0)], dma="const4", total=True)
        S.add("pool", lambda e: e.dma_start(out=wpe_t[:], in_=w_pe.rearrange("(kc p) n -> p kc n", p=P)), writes=["wpe"], dma="const2", total=True)

        conv_keys = {"A": [], "B": []}
        conv_todo = []

        def conv_piece(dst, src, c0, grp):
            for r0 in (0, 512):
                key = ("wconv", grp, len(conv_keys[grp]))
                conv_keys[grp].append(key)
                gid = gmap.setdefault((dst, c0), len(gmap))
                k0 = r0 // P
                conv_todo.append((lambda e, r0=r0, gid=gid, k0=k0: e.dma_start(
                    out=s_all[gid][:, k0 * 512:(k0 + 4) * 512].rearrange("p (kc n) -> p kc n", kc=4),
                    in_=src[r0:r0 + 512, c0:c0 + 512].rearrange("(kc p) n -> p kc n", p=P)), key, grp))
        grpA_cols = (C_V, C_V + 512, C_CC, C_CC + 512, C_XC, C_XC + 512, C_K)
        for c0 in grpA_cols:
            conv_piece(s_in, w_in, c0, "A")
        for c0 in [C_Q, C_ZA, C_ZA + 512, C_CB, C_CB + 512, C_ZC, C_ZC + 512] + [C_GA + 512 * i for i in range(4)]:
            conv_piece(s_in, w_in, c0, "B")
        for dst, src in ((s_a, w_a), (s_c, w_c), (s_o, w_o), (s_pg, w_pg)):
            for c0 in (0, 512):
                conv_piece(dst, src, c0, "B")

        def emit_conv(n):
            for _ in range(min(n, len(conv_todo))):
                fn, key, grp = conv_todo.pop(0)
                S.add("pool", fn, writes=[key], dma="wconv" + grp, total=True)

        S.add("dve", lambda e: e.memset(s32[:], 0.0), writes=["s32"])

        wctr = [0]

        def load_group(src, c0):
            seq = wctr[0]
            i = seq % NWSLOT
            wctr[0] += 1
            key = ("wslot", i)
            grp = "A" if (src == s_in and c0 in grpA_cols) else "B"
            gid = gmap[(src, c0)]
            S.add("sp", lambda e: e.dma_start(out=wslot[i][:], in_=s_all[gid].rearrange("p (kc n) -> p kc n", kc=KC)),
                  reads=conv_keys[grp], writes=[key], dma=key)
            return (wslot[i], key, seq)

        def load_group_direct(src32, c0):
            seq = wctr[0]
            i = seq % NWSLOT
            wctr[0] += 1
            key = ("wslot", i)
            S.add("pool", lambda e: e.dma_start(out=wslot[i][:], in_=src32[:, c0:c0 + 512].rearrange("(kc p) n -> p kc n", p=P)),
                  writes=[key], dma=("wslot_sw", i))
            return (wslot[i], key, seq)

        def wchk(w):
            assert wctr[0] - w[2] <= NWSLOT, "weight slot reused while still needed"
            return w[0], w[1]

        def mm(out_ap, lhsT, rhs, start, stop, reads, bkey):
            S.add("pe", lambda e: e.matmul(out_ap, lhsT=lhsT, rhs=rhs, start=start, stop=stop), reads=reads, writes=[bkey])

        def fm_chunk(w, j0, m, act_t, akeys, ntok):
            wt, wkey = wchk(w)
            bank, bkey = newbank()
            for kc in range(KC):
                mm(bank[0:m, 0:ntok], wt[:, kc, j0:j0 + m], act_t[:, kc, 0:ntok], kc == 0, kc == KC - 1, [wkey] + akeys, bkey)
            return bank, bkey

        def tm_bank(wt, wkey, c0, act_t, akey, t, nk=KC):
            bank, bkey = newbank()
            for kc in range(nk):
                mm(bank[:, :], act_t[:, kc, t * P:(t + 1) * P], wt[:, kc, c0:c0 + 512], kc == 0, kc == nk - 1, [wkey, akey], bkey)
            return bank, bkey

        def act(out_ap, in_ap, func, reads, writes, **kw):
            S.add("act", lambda e: e.activation(out=out_ap, in_=in_ap, func=func, **kw), reads=reads, writes=writes)

        def acopy(out_ap, in_ap, reads, writes):
            S.add("act", lambda e: e.copy(out=out_ap, in_=in_ap), reads=reads, writes=writes)

        def tt(eng, out_ap, a, b, op, reads, writes):
            S.add(eng, lambda e: e.tensor_tensor(out=out_ap, in0=a, in1=b, op=op), reads=reads, writes=writes)

        def stt(eng, out_ap, a, scalar, b, op0, op1, reads, writes):
            S.add(eng, lambda e: e.scalar_tensor_tensor(out=out_ap, in0=a, scalar=scalar, in1=b, op0=op0, op1=op1), reads=reads, writes=writes)

        def rstd_from_ms(ms_ap, tmp_ap, out_ap, reads, writes):
            act(tmp_ap, ms_ap, AF.Ln, reads, [("lnstat", writes[0])], bias=EPS, scale=1.0)
            act(out_ap, tmp_ap, AF.Exp, [("lnstat", writes[0])], writes, scale=-0.5)

        def norm_and_transpose(src_tile, skey, nt, gain, stat_c0, ht=None, hkey="htok", dT=None, dkey="hT"):
            ht = htok if ht is None else ht
            dT = hT if dT is None else dT
            sk = skey if callable(skey) else (lambda t: skey)
            for t in range(nt):
                S.add("act", lambda e, t=t: e.activation(out=junk[:], in_=src_tile[:, t, :], func=AF.Square, scale=1.0 / 32.0,
                                                        accum_out=stat[:, stat_c0 + t:stat_c0 + t + 1]),
                      reads=[sk(t)], writes=["junk", ("ms", t)])
                rstd_from_ms(stat[:, stat_c0 + t:stat_c0 + t + 1], stat[:, 8 + t:9 + t], stat[:, 12 + t:13 + t], [("ms", t)], [("rstd", t)])
            for t in range(nt):
                hs = t
                stt("dve", ht[:, hs, :], src_tile[:, t, :], stat[:, 12 + t:13 + t], gain, ALU.mult, ALU.mult,
                    [sk(t), ("rstd", t), "rowp"], [(hkey, hs)])
                bank, bkey = newbank()
                bv = bank[:].bitcast(BF16)
                for c in range(KC):
                    S.add("pe", lambda e, hs=hs, c=c, bv=bv: e.transpose(out=bv[:, c * P:(c + 1) * P], in_=ht[:, hs, c * P:(c + 1) * P], identity=identb[:]),
                          reads=[(hkey, hs), "identb"], writes=[bkey])
                src3 = bv.rearrange("p (c n) -> p c n", c=KC)
                extra = [("ozT", c) for c in range(KC)] if dkey == "n1T" else []
                if t % 2 == 0:
                    acopy(dT[:, :, t * P:(t + 1) * P], src3, [], [bkey, (dkey, t)] + extra)
                else:
                    S.add("dve", lambda e, t=t, src3=src3: e.tensor_copy(out=dT[:, :, t * P:(t + 1) * P], in_=src3), reads=[], writes=[bkey, (dkey, t)] + extra)

        hT_keys = [("hT", t) for t in range(NT)]

        CMB = ["cmb", "cmb2"]

        def gate_softplus(actT, akeys, sh, sl, skey):
            bank, bkey = newbank()
            wglk = [("wgl", 0), ("wgl", 32), ("wgl", 64)]
            for kc in range(KC):
                mm(bank[:, 0:TB], wgl3[:, kc, :], actT[:, kc, :], kc == 0, kc == KC - 1, wglk + akeys, bkey)
            acopy(glcat[0:16, :], bank[0:16, 0:TB], [], [bkey, "glcat"])
            acopy(glcat[64:80, :], bank[64:80, 0:TB], [], [bkey, "glcat"])
            S.add("dve", lambda e, bank=bank: e.tensor_copy(out=glcat[32:48, :], in_=bank[32:48, 0:TB]), reads=[], writes=[bkey, "glcat"])
            tt("dve", glcat[32:48, :], bank[32:48, 0:TB], glcat[32:48, :], ALU.subtract, ["glcat"], [bkey, "glcat"])
            for t in range(NT):
                bank, bkey = newbank()
                cs = slice(t * P, (t + 1) * P)
                mm(bank[:, :], glcat[:, cs], wcat[:, :], True, True, ["glcat", "wcat"], bkey)
                et, ekey = newtmp()
                st_, stkey = newtmp()
                act(et[:], bank[:, :], AF.Exp, [], [bkey, ekey], scale=-1.0)
                act(st_[:], et[:], AF.Ln, [ekey], [stkey], bias=1.0, scale=1.0)
                S.add("pool", lambda e, t=t, st_=st_: e.tensor_copy(out=sh[:, t, :], in_=st_[:]), reads=[stkey], writes=[(skey + "h", t)])
                tt("pool", sl[:, t, :], st_[:], sh[:, t, :], ALU.subtract, [stkey, (skey + "h", t)], [(skey + "l", t)])

        def spmm(out_ap, use_sp_as_lhsT, sh, sl, skey, t, sp_cols, other, start, stop, bkey):
            for i, (sx, sfx) in enumerate(((sh, "h"), (sl, "l"))):
                a_ = sx[:, t, sp_cols]
                if use_sp_as_lhsT:
                    mm(out_ap, a_, other, start and i == 0, stop and i == 1, [(skey + sfx, t)] + CMB, bkey)
                else:
                    mm(out_ap, other, a_, start and i == 0, stop and i == 1, [(skey + sfx, t)] + CMB, bkey)

        def gate_and_state(with_sbf):
            wkg = load_group(s_in, C_K)
            wvg = [load_group(s_in, C_V), load_group(s_in, C_V + 512)]
            bankd, bdkey = newbank()
            for h in range(4):
                for t in range(NT):
                    spmm(bankd[:, h * 4 + t:h * 4 + t + 1], True, sph, spl, "sp", t, slice(h * P, (h + 1) * P), negcol, True, True, bdkey)
            act(dec[:, 0:16], bankd[:, 0:16], AF.Exp, [], [bdkey, "dec"])
            for t in range(NT):
                bank, bkey = newbank()
                spmm(bank[:, :], False, sph, spl, "sp", t, slice(0, 512), tri_strict, True, True, bkey)
                ed, edkey = newtmp()
                act(ed[:], bank[:, :], AF.Exp, [], [bkey, edkey])
                wt, wkey = wchk(wkg)
                kb, kbkey = tm_bank(wt, wkey, 0, hT, ("hT", t), t)
                tt("dve", kdec[:, t, :], kb[:, :], ed[:], ALU.mult, [edkey], [kbkey, ("kdec", t)])
                for half in range(2):
                    wt, wkey = wchk(wvg[half])
                    vb, vbkey = tm_bank(wt, wkey, 0, hT, ("hT", t), t)
                    ch = 2 * t + half
                    if half == 0:
                        acopy(vm[:, ch, :], vb[:, :], [], [vbkey, ("vm", ch)])
                    else:
                        S.add("dve", lambda e, ch=ch, vb=vb: e.tensor_copy(out=vm[:, ch, :], in_=vb[:, :]), reads=[], writes=[vbkey, ("vm", ch)])
            s32f = s32[:].rearrange("p h v -> p (h v)")
            if with_sbf:
                acopy(sbf[:, 0, :], s32f, ["s32"], [("sbf", 0), "Mst"])
            for c in range(NT):
                t = c
                for hp in range(2):
                    bank, bkey = newbank()
                    for hh in range(2):
                        h = hp * 2 + hh
                        mm(bank[:, hh * 256:(hh + 1) * 256], kdec[:, t, h * P:(h + 1) * P],
                           vm[:, 2 * t + h // 2, (h % 2) * 256:(h % 2 + 1) * 256],
                           True, True, [("kdec", t), ("vm", 2 * t + h // 2)], bkey)
                    for hh in range(2):
                        h = hp * 2 + hh
                        stt("dve", s32[:, h, :], s32[:, h, :], dec[:, h * 4 + c:h * 4 + c + 1], bank[:, hh * 256:(hh + 1) * 256], ALU.mult, ALU.add,
                            ["s32", "dec"], [bkey, "s32"])
                if with_sbf and c < NT - 1:
                    acopy(sbf[:, c + 1, :], s32f, ["s32"], [("sbf", c + 1), "Mst"])
            return wkg

        htokB = qk[:].rearrange("p c n -> p (c n)").rearrange("p (t d) -> p t d", t=NT)
        psets = [dict(ht=htok, hk="htok", dT=hT, dk="hT", sh=sph, sl=spl, sk="sp"),
                 dict(ht=htokB, hk="htokB", dT=ozT, dk="hTB", sh=vm[:, 0:4, :], sl=vm[:, 4:8, :], sk="spB")]

        def pre_L(pb):
            for t in range(NT):
                S.add("sp", lambda e, pb=pb, t=t: e.dma_start(out=xt[:, t, :], in_=x_pre[pb * TB + t * P:pb * TB + (t + 1) * P, :]),
                      writes=[xkt(t)], dma=("xload", t))
            for t in range(NT):
                S.add("act", lambda e, t=t: e.activation(out=junk[:], in_=xt[:, t, :], func=AF.Square, scale=1.0 / 32.0, accum_out=stat[:, t:t + 1]),
                      reads=[xkt(t)], writes=["junk", ("ms", t)])
                rstd_from_ms(stat[:, t:t + 1], stat[:, 8 + t:9 + t], stat[:, 12 + t:13 + t], [("ms", t)], [("rstd", t)])

        def pre_H(pb, t):
            ps = psets[pb % 2]
            stt("dve", ps["ht"][:, t, :], xt[:, t, :], stat[:, 12 + t:13 + t], g_mix, ALU.mult, ALU.mult, [xkt(t), ("rstd", t), "rowp"], [(ps["hk"], t)])

        def pre_T(pb):
            ps = psets[pb % 2]
            for t in range(NT):
                bank, bkey = newbank()
                bv = bank[:].bitcast(BF16)
                for c in range(KC):
                    S.add("pe", lambda e, t=t, c=c, bv=bv, ps=ps: e.transpose(out=bv[:, c * P:(c + 1) * P], in_=ps["ht"][:, t, c * P:(c + 1) * P], identity=identb[:]),
                          reads=[(ps["hk"], t), "identb"], writes=[bkey])
                src3 = bv.rearrange("p (c n) -> p c n", c=KC)
                if t % 2 == 0:
                    acopy(ps["dT"][:, :, t * P:(t + 1) * P], src3, [], [bkey, (ps["dk"], t)])
                else:
                    S.add("dve", lambda e, t=t, src3=src3, ps=ps: e.tensor_copy(out=ps["dT"][:, :, t * P:(t + 1) * P], in_=src3), reads=[], writes=[bkey, (ps["dk"], t)])
            dkeys = [(ps["dk"], t) for t in range(NT)]
            gate_softplus(ps["dT"], dkeys, ps["sh"], ps["sl"], ps["sk"])

        def pre_B(pb, after_all=None):
            ps = psets[pb % 2]
            sh, sl, sk_ = ps["sh"], ps["sl"], ps["sk"]
            wkg = wk_pre
            bankd, bdkey = newbank()
            for h in range(4):
                for t in range(NT):
                    spmm(bankd[:, h:h + 1], True, sh, sl, sk_, t, slice(h * P, (h + 1) * P), negcol, t == 0, t == NT - 1, bdkey)
            act(dec[:, 0:4], bankd[:, 0:4], AF.Exp, [], [bdkey, "dec"])
            eds = []
            for t in range(NT):
                bank, bkey = newbank()
                spmm(bank[:, :], False, sh, sl, sk_, t, slice(0, 512), tri_strict128, True, t == NT - 1, bkey)
                for u in range(t + 1, NT):
                    spmm(bank[:, :], False, sh, sl, sk_, u, slice(0, 512), allneg, False, u == NT - 1, bkey)
                ed, edkey = newtmp()
                act(ed[:], bank[:, :], AF.Exp, [], [bkey, edkey])
                eds.append((ed, edkey))
            for t in range(NT):
                wt, wkey = wchk(wkg)
                kb, kbkey = tm_bank(wt, wkey, 0, ps["dT"], (ps["dk"], t), t)
                ed, edkey = eds[t]
                tt("dve", kdec[:, t, :], kb[:, :], ed[:], ALU.mult, [edkey], [kbkey, ("kdec", t)])
            groups = [(h, half) for h in range(4) for half in range(2)]
            for bi in range(2):
                gl = groups[bi * 4:(bi + 1) * 4]
                bks = [newbank() for _ in gl]
                for t in range(NT - 1):
                    for (h, half), (bank, bkey) in zip(gl, bks):
                        mm(bank[:, :], kdec[:, t, h * P:(h + 1) * P], ps["ht"][:, t, half * 512:(half + 1) * 512], t == 0, False, [("kdec", t), (ps["hk"], t)], bkey)
                if bi == 1 and after_all is not None:
                    after_all(range(NT - 1))
                t = NT - 1
                for (h, half), (bank, bkey) in zip(gl, bks):
                    mm(bank[:, :], kdec[:, t, h * P:(h + 1) * P], ps["ht"][:, t, half * 512:(half + 1) * 512], False, True, [("kdec", t), (ps["hk"], t)], bkey)
                for (h, half), (bank, bkey) in zip(gl, bks):
                    stt("dve", Mst[:, h, half * 512:(half + 1) * 512], Mst[:, h, half * 512:(half + 1) * 512], dec[:, h:h + 1], bank[:, :],
                        ALU.mult, ALU.add, ["Mst", "dec"], [bkey, "Mst"])
            if after_all is not None:
                after_all([NT - 1])

        wk_pre = load_group_direct(w_in, C_K) if npre > 0 else None
        for pb in range(min(2, npre)):
            pre_L(pb)
            for t in range(NT):
                pre_H(pb, t)
            pre_T(pb)
        for pb in range(npre):
            nb = pb + 2
            if nb < npre:
                pre_L(nb)
                pre_B(pb, after_all=lambda ts, nb=nb: [pre_H(nb, t) for t in ts])
                pre_T(nb)
            else:
                pre_B(pb)
            emit_conv(4)
        emit_conv(len(conv_todo))
        if npre > 1:
            fence_r = [(k, t) for k in ("htokB", "hTB", "spBh", "spBl") for t in range(NT)]
            fence_w = [(k, c) for k in ("qk", "ozT", "vm") for c in range(KC)]
            S.add("dve", lambda e: e.memset(stat[:, 0:1], 0.0), reads=fence_r, writes=fence_w + [("ms", 0)])
        if npre > 0:
            wvg = [load_group(s_in, C_V), load_group(s_in, C_V + 512)]
            for h in range(4):
                tb = [newbank(), newbank()]
                for c in range(KC):
                    bnk, bk_ = tb[c // 4]
                    S.add("pe", lambda e, h=h, c=c, bnk=bnk: e.transpose(out=bnk[:, (c % 4) * P:(c % 4 + 1) * P], in_=Mst[:, h, c * P:(c + 1) * P], identity=ident32),
                          reads=["Mst", "cm32a"], writes=[bk_])
                hi_, hik = newtmp()
                lo_, lok = newtmp()
                hib = hi_[:].bitcast(BF16)
                lob = lo_[:].bitcast(BF16)
                for g2 in range(2):
                    bnk, bk_ = tb[g2]
                    acopy(hib[:, g2 * 512:(g2 + 1) * 512], bnk[:, :], [], [bk_, hik])
                    tt("dve", lob[:, g2 * 512:(g2 + 1) * 512], bnk[:, :], hib[:, g2 * 512:(g2 + 1) * 512], ALU.subtract, [hik], [bk_, lok])
                wt, wkey = wchk(wvg[h // 2])
                sbk, sbkk = newbank()
                n = 0
                for src_, skey_ in ((hib, hik), (lob, lok)):
                    for c in range(KC):
                        mm(sbk[:, 0:256], src_[:, c * P:(c + 1) * P], wt[:, c, (h % 2) * 256:(h % 2 + 1) * 256], n == 0, n == 2 * KC - 1, [skey_, wkey], sbkk)
                        n += 1
                acopy(s32[:, h, :], sbk[:, 0:256], [], [sbkk, "s32"])

        xp = xstage[:, 0, :]
        S.add("sp", lambda e: e.dma_start(out=xp, in_=x_prev), writes=[("xs", 0)], dma="xprev")
        xp3 = xstage[:, 0:1, :]
        norm_and_transpose(xp3, ("xs", 0), 1, g_mix, 0)
        for half in range(2):
            wcc = load_group(s_in, C_CC + half * 512)
            wxc = load_group(s_in, C_XC + half * 512)
            for cc_ in range(4):
                c = half * 4 + cc_
                b1, k1 = fm_chunk(wcc, cc_ * P, P, hT, [("hT", 0)], P)
                b2, k2 = fm_chunk(wxc, cc_ * P, P, hT, [("hT", 0)], P)
                tm_, tk_ = newtmp()
                acopy(tm_[:, 0:2], b1[:, P - 2:P], [], [k1, tk_])
                tt("dve", ucar[0][:, c, :], b2[:, P - 2:P], tm_[:, 0:2], ALU.mult, [tk_], [k2, ("uc", 0, c)])

        stg = [xstage[:, 0, :], xstage[:, 1, :],
               kdec[:].rearrange("p t n -> p (t n)").bitcast(F32),
               at_t[:].rearrange("p t n -> p (t n)").bitcast(F32)]
        stg_keys = [[("xs", 0)], [("xs", 1)], [("kdec", t_) for t_ in range(NT)], [("at", t_) for t_ in range(NT)]]

        def stage_load(blk, t):
            S.add("sp", lambda e, blk=blk, t=t: e.dma_start(out=stg[t], in_=x_own[blk * TB + t * P:blk * TB + (t + 1) * P, :]),
                  writes=stg_keys[t], dma=("xsload", t))

        def front1(blk):
            for t in range(NT):
                S.add("act", lambda e, t=t: e.activation(out=junk[:], in_=stg[t], func=AF.Square, scale=1.0 / 32.0, accum_out=stat[:, t:t + 1]),
                      reads=stg_keys[t], writes=["junk", ("ms", t)])
                rstd_from_ms(stat[:, t:t + 1], stat[:, 8 + t:9 + t], stat[:, 12 + t:13 + t], [("ms", t)], [("rstd", t)])
                stt("dve", htok[:, t, :], stg[t], stat[:, 12 + t:13 + t], g_mix, ALU.mult, ALU.mult, stg_keys[t] + [("rstd", t), "rowp"], [("htok", t)])

        def front2(blk, after_tile=None):
            for t in range(NT):
                bank, bkey = newbank()
                bv = bank[:].bitcast(BF16)
                for c in range(KC):
                    S.add("pe", lambda e, t=t, c=c, bv=bv: e.transpose(out=bv[:, c * P:(c + 1) * P], in_=htok[:, t, c * P:(c + 1) * P], identity=identb[:]),
                          reads=[("htok", t), "identb"], writes=[bkey])
                src3 = bv.rearrange("p (c n) -> p c n", c=KC)
                if t % 2 == 0:
                    acopy(hT[:, :, t * P:(t + 1) * P], src3, [], [bkey, ("hT", t)])
                else:
                    S.add("dve", lambda e, t=t, src3=src3: e.tensor_copy(out=hT[:, :, t * P:(t + 1) * P], in_=src3), reads=[], writes=[bkey, ("hT", t)])
                if after_tile is not None:
                    after_tile(t)
            gate_softplus(hT, hT_keys, sph, spl, "sp")

        def xt_load(blk):
            for t in range(NT):
                S.add("sp", lambda e, blk=blk, t=t: e.dma_start(out=xt[:, t, :], in_=x_own[blk * TB + t * P:blk * TB + (t + 1) * P, :]),
                      writes=[xkt(t)], dma=("xload", t))

        for t_ in range(NT):
            stage_load(0, t_)
        front1(0)
        front2(0)
        for blk in range(nown):
            slot = 0
            S.add("pool", lambda e, blk=blk: e.dma_start(out=ptok[0][:], in_=p_own[blk * TB:(blk + 1) * TB, :].rearrange("(t p) d -> p t d", p=P)),
                  writes=[("ptok", 0)], dma=("pload", 0))
            wkg = gate_and_state(True)

            wqg = load_group(s_in, C_Q)
            for h in range(4):
                bank, bkey = newbank()
                for t in range(NT):
                    spmm(bank[:, t * P:(t + 1) * P], True, sph, spl, "sp", t, slice(h * P, (h + 1) * P), tri_incl, True, True, bkey)
                eq_, eqk = newtmp()
                ek_, ekk = newtmp()
                act(eq_[:], bank[:, :], AF.Exp, [], [bkey, eqk])
                act(ek_[:], bank[:, :], AF.Exp, [], [bkey, ekk], scale=-1.0)
                qb, qbk = fm_chunk(wqg, h * P, P, hT, hT_keys, TB)
                stt("dve", qk[:, h, :], qb[:, :], 128.0 ** -0.5, eq_[:], ALU.mult, ALU.mult, [eqk], [qbk, ("qk", h)])
                kb, kbk = fm_chunk(wkg, h * P, P, hT, hT_keys, TB)
                tt("dve", qk[:, 4 + h, :], kb[:, :], ek_[:], ALU.mult, [ekk], [kbk, ("qk", 4 + h)])
            for t in range(NT):
                bank, bkey = newbank()
                for h in range(4):
                    mm(bank[:, h * P:(h + 1) * P], qk[:, 4 + h, t * P:(t + 1) * P], qk[:, h, t * P:(t + 1) * P], True, True, [("qk", 4 + h), ("qk", h)], bkey)
                tt("dve", at_t[:, t, :], bank[:, :], mask4, ALU.mult, ["cm32"], [bkey, ("at", t)])
            for h in range(4):
                if h % 2 == 0:
                    wzg = load_group(s_in, C_ZA + (h // 2) * 512)
                held = []
                for vh in range(2):
                    oc = h * 2 + vh
                    ob, ok = newbank()
                    for t in range(NT):
                        vch = 2 * t + h // 2
                        v0 = (h % 2) * 256 + vh * P
                        mm(ob[:, t * P:(t + 1) * P], vm[:, vch, v0:v0 + P], at_t[:, t, h * P:(h + 1) * P], True, False, [("vm", vch), ("at", t)], ok)
                        mm(ob[:, t * P:(t + 1) * P], sbf[:, t, h * 256 + vh * P:h * 256 + (vh + 1) * P],
                           qk[:, h, t * P:(t + 1) * P], False, True, [("sbf", t), ("qk", h)], ok)
                    act(sqb[:, vh, :], ob[:, :], AF.Square, [], [ok, ("sqb", vh)])
                    zb, zk = fm_chunk(wzg, (oc % 4) * P, P, hT, hT_keys, TB)
                    sz_, szk = newtmp()
                    act(sz_[:], zb[:, :], AF.Silu, [], [zk, szk])
                    held.append((ob, ok, sz_, szk))
                sbk, sbkk = newbank()
                mm(sbk[:, :], onesb[:], sqb[:, 0, :], True, False, ["onesb", ("sqb", 0)], sbkk)
                mm(sbk[:, :], onesb[:], sqb[:, 1, :], False, True, ["onesb", ("sqb", 1)], sbkk)
                ln_, lnk = newtmp()
                rs_, rsk = newtmp()
                act(ln_[:], sbk[:, :], AF.Ln, [], [sbkk, lnk], bias=EPS, scale=1.0)
                act(rs_[:], ln_[:], AF.Exp, [lnk], [rsk], scale=-0.5)
                for vh in range(2):
                    oc = h * 2 + vh
                    ob, ok, sz_, szk = held[vh]
                    tt("pool", sz_[:], sz_[:], rs_[:], ALU.mult, [szk, rsk], [szk])
                    stt("dve", ozT[:, oc, :], ob[:, :], colp_t[:, 40 + vh:41 + vh], sz_[:], ALU.mult, ALU.mult, [szk, "colp"], [ok, ("ozT", oc)] + [("n1T", t_) for t_ in range(NT)])
            us, uns = blk % 2, (blk + 1) % 2
            for half in range(2):
                wcc = load_group(s_in, C_CC + half * 512)
                wxc = load_group(s_in, C_XC + half * 512)
                wcb = load_group(s_in, C_CB + half * 512)
                wzc = load_group(s_in, C_ZC + half * 512)
                if half == 1 and blk + 1 < nown:
                    for t_ in range(NT):
                        stage_load(blk + 1, t_)
                for cc_ in range(4):
                    c = half * 4 + cc_
                    i2 = c % 2
                    ut = utmp[i2]
                    ukey = ("utmp", i2)
                    b1, k1 = fm_chunk(wcc, cc_ * P, P, hT, hT_keys, TB)
                    b2, k2 = fm_chunk(wxc, cc_ * P, P, hT, hT_keys, TB)
                    cs_, csk = newtmp()
                    acopy(cs_[:], b1[:, :], [], [k1, csk])
                    S.add("pool", lambda e, c=c, ut=ut, us=us: e.tensor_copy(out=ut[:, 0:2], in_=ucar[us][:, c, :]), reads=[("uc", us, c)], writes=[ukey])
                    tt("dve", ut[:, 2:TB + 2], b2[:, :], cs_[:], ALU.mult, [csk], [k2, ukey])
                    S.add("pool", lambda e, c=c, ut=ut, uns=uns: e.tensor_copy(out=ucar[uns][:, c, :], in_=ut[:, TB:TB + 2]), reads=[ukey], writes=[("uc", uns, c)])
                    cw = lambda k, c=c: colp_t[:, 16 + c * 3 + k:17 + c * 3 + k]
                    y_, yk = newtmp()
                    S.add("act", lambda e, ut=ut, y_=y_, cw=cw: e.mul(out=y_[:], in_=ut[:, 2:TB + 2], mul=cw(2)), reads=[ukey, "colp"], writes=[yk])
                    for kk in (1, 0):
                        y2_, y2k = newtmp()
                        S.add("act", lambda e, ut=ut, y2_=y2_, cw=cw, kk=kk: e.mul(out=y2_[:], in_=ut[:, kk:TB + kk], mul=cw(kk)), reads=[ukey, "colp"], writes=[y2k])
                        tt("pool", y_[:], y_[:], y2_[:], ALU.add, [yk, y2k], [yk])
                    b3, k3 = fm_chunk(wcb, cc_ * P, P, hT, hT_keys, TB)
                    b4, k4 = fm_chunk(wzc, cc_ * P, P, hT, hT_keys, TB)
                    tt("dve", y_[:], b3[:, :], y_[:], ALU.mult, [yk], [k3, yk])
                    sc_, sck = newtmp()
                    act(sc_[:], b4[:, :], AF.Silu, [], [k4, sck])
                    tt("pool", qk[:, c, :], y_[:], sc_[:], ALU.mult, [yk, sck], [("qk", c)])
            if blk + 1 < nown:
                front1(blk + 1)
            oz_keys = [("ozT", c) for c in range(KC)]
            cz_keys = [("qk", c) for c in range(KC)]
            for half in range(2):
                wag = load_group(s_a, half * 512)
                wgag = load_group(s_in, C_GA + half * 512)
                wcg = load_group(s_c, half * 512)
                wgcg = load_group(s_in, C_GC + half * 512)
                if half == 1:
                    xt_load(blk)
                for ee in range(4):
                    e_ = half * 4 + ee
                    ba, kba = fm_chunk(wag, ee * P, P, ozT, oz_keys, TB)
                    bg, kbg = fm_chunk(wgag, ee * P, P, hT, hT_keys, TB)
                    bc, kbc = fm_chunk(wcg, ee * P, P, qk, cz_keys, TB)
                    bh, kbh = fm_chunk(wgcg, ee * P, P, hT, hT_keys, TB)
                    sa_, sak = newtmp()
                    sc_, sck = newtmp()
                    act(sa_[:], bg[:, :], AF.Sigmoid, ["colp"], [kbg, sak], bias=colp_t[:, e_:e_ + 1], scale=1.0)
                    act(sc_[:], bh[:, :], AF.Sigmoid, ["colp"], [kbh, sck], bias=colp_t[:, 8 + e_:9 + e_], scale=1.0)
                    tt("dve", sa_[:], ba[:, :], sa_[:], ALU.mult, [sak], [kba, sak])
                    tt("dve", sc_[:], bc[:, :], sc_[:], ALU.mult, [sck], [kbc, sck])
                    tt("pool", vm[:, e_, :], sa_[:], sc_[:], ALU.add, [sak, sck], [("vm", e_)])
            m_keys = [("vm", c) for c in range(KC)]
            for half in range(2):
                wog = load_group(s_o, half * 512)
                for t in range(NT):
                    wt, wkey = wchk(wog)
                    bank, bkey = newbank()
                    for kc in range(KC):
                        mm(bank[:, :], vm[:, kc, t * P:(t + 1) * P], wt[:, kc, :], kc == 0, kc == KC - 1, [wkey] + m_keys, bkey)
                    tt("dve", xt[:, t, half * 512:(half + 1) * 512], bank[:, :], xt[:, t, half * 512:(half + 1) * 512], ALU.add, [xkt(t)], [bkey, xkt(t)])
            for t in range(NT):
                S.add("act", lambda e, t=t: e.activation(out=junk[:], in_=xt[:, t, :], func=AF.Square, scale=1.0 / 32.0, accum_out=stat[:, 4 + t:5 + t]),
                      reads=[xkt(t)], writes=["junk", ("ms", t)])
                rstd_from_ms(stat[:, 4 + t:5 + t], stat[:, 8 + t:9 + t], stat[:, 12 + t:13 + t], [("ms", t)], [("rstd", t)])

            def np_scale(t):
                stt("dve", htok[:, t, :], xt[:, t, :], stat[:, 12 + t:13 + t], g_ple, ALU.mult, ALU.mult, [xkt(t), ("rstd", t), "rowp"], [("htok", t)])
            if blk + 1 < nown:
                front2(blk + 1, after_tile=np_scale)
            else:
                for t in range(NT):
                    np_scale(t)
            n1_extra = [("ozT", c) for c in range(KC)]
            for t in range(NT):
                bank, bkey = newbank()
                bv = bank[:].bitcast(BF16)
                for c in range(KC):
                    S.add("pe", lambda e, t=t, c=c, bv=bv: e.transpose(out=bv[:, c * P:(c + 1) * P], in_=htok[:, t, c * P:(c + 1) * P], identity=identb[:]),
                          reads=[("htok", t), "identb"], writes=[bkey])
                src3 = bv.rearrange("p (c n) -> p c n", c=KC)
                if t % 2 == 0:
                    acopy(ozT[:, :, t * P:(t + 1) * P], src3, [], [bkey, ("n1T", t)] + n1_extra)
                else:
                    S.add("dve", lambda e, t=t, src3=src3: e.tensor_copy(out=ozT[:, :, t * P:(t + 1) * P], in_=src3), reads=[], writes=[bkey, ("n1T", t)] + n1_extra)
            for t in range(NT):
                bank, bkey = newbank()
                bv = bank[:].bitcast(BF16)
                for c in range(2):
                    S.add("pe", lambda e, t=t, c=c, bv=bv, slot=slot: e.transpose(out=bv[:, c * P:(c + 1) * P], in_=ptok[slot][:, t, c * P:(c + 1) * P], identity=identb[:]),
                          reads=[("ptok", slot), "identb"], writes=[bkey])
                acopy(pT[:, :, t * P:(t + 1) * P], bv[:, 0:2 * P].rearrange("p (c n) -> p c n", c=2), [], [bkey, ("pT", t)])
            for half in range(2):
                wgg = load_group(s_pg, half * 512)
                for t in range(NT):
                    wt, wkey = wchk(wgg)
                    bg, kbg = newbank()
                    for kc in range(KC):
                        mm(bg[:, :], ozT[:, kc, t * P:(t + 1) * P], wt[:, kc, :], kc == 0, kc == KC - 1, [wkey, ("n1T", t)], kbg)
                    bp, kbp = newbank()
                    for kc in range(2):
                        mm(bp[:, :], pT[:, kc, t * P:(t + 1) * P], wpe_t[:, kc, half * 512:(half + 1) * 512], kc == 0, kc == 1, ["wpe", ("pT", t)], kbp)
                    sg_, sgk = newtmp()
                    act(sg_[:], bg[:, :], AF.Sigmoid, [], [kbg, sgk])
                    tt("dve", sg_[:], bp[:, :], sg_[:], ALU.mult, [sgk], [kbp, sgk])
                    tt("pool", xt[:, t, half * 512:(half + 1) * 512], xt[:, t, half * 512:(half + 1) * 512], sg_[:], ALU.add, [xkt(t), sgk], [xkt(t)])
            for t in range(NT):
                i2 = t % 2
                S.add("act", lambda e, t=t: e.activation(out=junk[:], in_=xt[:, t, :], func=AF.Square, scale=1.0 / 32.0, accum_out=stat[:, t:t + 1]),
                      reads=[xkt(t)], writes=["junk", ("ms", t)])
                rstd_from_ms(stat[:, t:t + 1], stat[:, 8 + t:9 + t], stat[:, 12 + t:13 + t], [("ms", t)], [("rstd", t)])
                S.add("act", lambda e, t=t, i2=i2: e.mul(out=otile[i2][:], in_=xt[:, t, :], mul=stat[:, 12 + t:13 + t]),
                      reads=[xkt(t), ("rstd", t)], writes=[("otile", i2)])
                tt("pool", otile[i2][:], otile[i2][:], g_fin, ALU.mult, [("otile", i2), "rowp"], [("otile", i2)])
                S.add("pool", lambda e, t=t, i2=i2, blk=blk: e.dma_start(out=out[blk * TB + t * P:blk * TB + (t + 1) * P, :], in_=otile[i2][:]),
                      reads=[("otile", i2)], dma=("ostore", i2), out_dma=True)
        S.emit()
    return nc


def _consts():
    c = np.zeros((P, CM_W), np.float32)
    s = np.arange(P)[:, None]
    t = np.arange(P)[None, :]
    same = (s // P) == (t // P)
    c[:, 0:P] = np.eye(P, dtype=np.float32)
    c[:, P:2 * P] = 1.0 / 256.0
    c[:, 2 * P:3 * P] = np.where(same & (s <= t), -1.0 / 16.0, 0.0)
    c[:, 3 * P:4 * P] = np.where(same & (s > t), -1.0 / 16.0, 0.0)
    c[:, 4 * P:4 * P + 512] = np.tile(np.where(same & (s <= t), 1.0, 0.0).astype(np.float32), (1, 4))
    o = 4 * P + 512
    c[:, o] = np.where(np.arange(P) < 64, -1.0 / 16.0, 0.0)
    c[:, o + 1] = np.where(np.arange(P) >= 64, -1.0 / 16.0, 0.0)
    c[:, o + 2] = -1.0 / 16.0
    c[:, o + 3:o + 3 + P] = np.where(s > t, -1.0 / 16.0, 0.0)
    c[:, o + 3 + P:o + 3 + 2 * P] = -1.0 / 16.0
    return c


_NC_CACHE = {}


def kernel(x, p, norm_mix_g, w_in, b_merge, w_gk2, b_gk, gla_norm_g, conv_w, w_branch_a, w_branch_c, w_out,
           norm_ple_g, w_ple_gate, w_ple_proj, norm_final_g):
    f = lambda a: np.ascontiguousarray(np.asarray(a, dtype=np.float32))
    x, p = f(x), f(p)
    if "nc" not in _NC_CACHE:
        _NC_CACHE["nc"] = build_program()
    nc = _NC_CACHE["nc"]
    rowp = np.concatenate([f(norm_mix_g)[0], f(norm_ple_g)[0], f(norm_final_g), f(b_gk)[0]])[None, :].repeat(P, 0)
    colp = np.zeros((P, 42), np.float32)
    bm = f(b_merge)[0]
    colp[:, 0:8] = bm[:D].reshape(8, P).T
    colp[:, 8:16] = bm[D:].reshape(8, P).T
    cw = f(conv_w)[0]
    colp[:, 16:40] = cw.reshape(3, 8, P).transpose(2, 1, 0).reshape(P, 24)
    colp[:, 40:42] = f(gla_norm_g)[0].reshape(2, P).T
    cm = _consts()
    shared = {
        "w_in": f(w_in)[0], "w_gk2": f(w_gk2)[0], "w_a": f(w_branch_a)[0], "w_c": f(w_branch_c)[0],
        "w_o": f(w_out)[0], "w_pg": f(w_ple_gate)[0], "w_pe": f(w_ple_proj)[0],
        "rowp": np.ascontiguousarray(rowp), "colp": colp, "cmat": cm,
    }
    in_maps = []
    for c in range(NCORES):
        b, j = c // 4, c % 4
        s0 = j * SEG
        xpre = np.zeros((NPRE * TB, D), np.float32)
        if s0 > 0:
            xpre[NPRE * TB - s0:] = x[b, 0:s0]
        xprev = np.zeros((P, D), np.float32)
        if s0 > 0:
            xprev[:] = x[b, s0 - P:s0]
        m = dict(shared)
        m.update({"x_own": np.ascontiguousarray(x[b, s0:s0 + SEG]), "x_pre": xpre, "x_prev": xprev,
                  "p_own": np.ascontiguousarray(p[0, b, s0:s0 + SEG])})
        in_maps.append(m)
    res = run_bass_kernel_spmd(nc, in_maps, core_ids=list(range(NCORES)))
    outp = np.zeros((2, SEQ, D), np.float32)
    for c in range(NCORES):
        b, j = c // 4, c % 4
        outp[b, j * SEG:(j + 1) * SEG] = np.asarray(res.results[c]["out"], dtype=np.float32)
    return outp
```

```python
import contextlib
import numpy as np
import concourse.bass as bass
import concourse.mybir as mybir
from concourse.bass_utils import run_bass_kernel_spmd

F32 = mybir.dt.float32
BF16 = mybir.dt.bfloat16
AF = mybir.ActivationFunctionType
ALU = mybir.AluOpType

P = 128
D = 1024
KC = 8
TB = 512
NT = 4
SEQ = 8192
NCORES = 8
SEG = 2048
NOWN = SEG // TB
NPRE = (SEQ - SEG) // TB
IN_COLS = 9232
C_Q, C_K, C_V, C_ZA, C_GL, C_CB, C_CC, C_XC, C_ZC, C_GA, C_GC = (
    0, 512, 1024, 2048, 3072, 3088, 4112, 5136, 6160, 7184, 8208)
EPS = 1e-6
NWSLOT = 6
NTMP = 10
CM_W = 6 * P + 512 + 3
COMPUTE = ("pe", "act", "dve", "pool")


class Sched:
    def __init__(self, nc):
        self.nc = nc
        self.ops = []
        self.lastw = {}
        self.readers = {}
        self.dma_count = {}
        self.final_waits = []
        self.total_keys = set()

    def add(self, eng, fn, reads=(), writes=(), dma=None, out_dma=False, total=False):
        i = len(self.ops)
        deps = {}
        for k in reads:
            w = self.lastw.get(k)
            if w is not None:
                deps[w] = "raw"
        for k in writes:
            w = self.lastw.get(k)
            if w is not None and w not in deps:
                deps[w] = "waw"
            for r in self.readers.get(k, ()):
                if r not in deps:
                    deps[r] = "war"
        for k in reads:
            self.readers.setdefault(k, []).append(i)
        for k in writes:
            self.lastw[k] = i
            self.readers[k] = []
        deps.pop(i, None)
        op = dict(eng=eng, fn=fn, deps=deps, dma=dma, need_inc=False)
        if dma is not None:
            c = self.dma_count.get(dma, 0) + 1
            self.dma_count[dma] = c
            op["dma_val"] = 16 * c
            if total:
                self.total_keys.add(dma)
            if out_dma and dma not in self.final_waits:
                self.final_waits.append(dma)
        self.ops.append(op)
        return i

    def emit(self):
        nc = self.nc
        ops = self.ops
        for op in ops:
            keep = {}
            for d, kind in op["deps"].items():
                p = ops[d]
                if p["dma"] is None and op["dma"] is None and p["eng"] == op["eng"]:
                    if op["eng"] == "pe" or kind != "raw":
                        continue
                keep[d] = kind
            latest = {}
            for d in list(keep):
                p = ops[d]
                if p["dma"] is None:
                    e_ = p["eng"]
                    if e_ in latest:
                        lo_ = min(latest[e_], d)
                        latest[e_] = max(latest[e_], d)
                        keep.pop(lo_)
                    else:
                        latest[e_] = d
            op["deps"] = keep
            for d in keep:
                ops[d]["need_inc"] = True
        cnt = {e: 0 for e in COMPUTE}
        for op in ops:
            if op["dma"] is None and op["need_inc"]:
                cnt[op["eng"]] += 1
                op["ms"] = cnt[op["eng"]]
        with contextlib.ExitStack() as stack:
            sems = {e: stack.enter_context(nc.semaphore("s_" + e)) for e in COMPUTE}
            dsem = {}
            for n, k in enumerate(self.dma_count):
                dsem[k] = stack.enter_context(nc.semaphore("d%d" % n))
            block = stack.enter_context(nc.Block())

            def run(engname, handle):
                waited = {}
                for op in ops:
                    if op["eng"] != engname:
                        continue
                    need = {}
                    for d in op["deps"]:
                        p = ops[d]
                        if p["dma"] is not None:
                            s, v = dsem[p["dma"]], p["dma_val"]
                            if p["dma"] in self.total_keys:
                                v = 16 * self.dma_count[p["dma"]]
                        else:
                            s, v = sems[p["eng"]], p["ms"]
                        if need.get(id(s), (None, 0))[1] < v:
                            need[id(s)] = (s, v)
                    for key, (s, v) in need.items():
                        if waited.get(key, 0) >= v:
                            continue
                        handle.wait_ge(s, v)
                        waited[key] = v
                    inst = op["fn"](handle)
                    if op["dma"] is not None:
                        inst.then_inc(dsem[op["dma"]], 16)
                    elif op["need_inc"]:
                        inst.then_inc(sems[op["eng"]], 1)
                if engname == "sp":
                    for k in self.final_waits:
                        handle.wait_ge(dsem[k], 16 * self.dma_count[k])

            @block.sync
            def _(h):
                run("sp", h)

            @block.scalar
            def _(h):
                run("act", h)

            @block.vector
            def _(h):
                run("dve", h)

            @block.gpsimd
            def _(h):
                run("pool", h)

            @block.tensor
            def _(h):
                run("pe", h)


def build_program(npre=NPRE, nown=NOWN):
    nc = bass.Bass("TRN2", target_bir_lowering=False)
    dr = lambda name, shape, kind="ExternalInput", dt=F32: nc.dram_tensor(name, shape, dt, kind=kind).ap()
    x_own = dr("x_own", [nown * TB, D])
    x_pre = dr("x_pre", [max(npre, 1) * TB, D])
    x_prev = dr("x_prev", [P, D])
    p_own = dr("p_own", [nown * TB, 256])
    w_in = dr("w_in", [D, IN_COLS])
    w_gk2 = dr("w_gk2", [16, 512])
    w_a = dr("w_a", [D, D])
    w_c = dr("w_c", [D, D])
    w_o = dr("w_o", [D, D])
    w_pg = dr("w_pg", [D, D])
    w_pe = dr("w_pe", [256, D])
    rowp = dr("rowp", [P, 3 * D + 512])
    colp = dr("colp", [P, 42])
    cmat = dr("cmat", [P, CM_W])
    out = dr("out", [nown * TB, D], kind="ExternalOutput")
    NGRP = 26
    s_all = dr("s_all", [NGRP, P, KC * 512], kind="Internal", dt=BF16)
    s_in, s_a, s_c, s_o, s_pg = "s_in", "s_a", "s_c", "s_o", "s_pg"
    gmap = {}

    S = Sched(nc)
    with contextlib.ExitStack() as st:
        sb = lambda name, shape, dt=F32: st.enter_context(nc.sbuf_tensor(name, shape, dt))
        rowp_t = sb("rowp_t", [P, 3 * D])
        colp_t = sb("colp_t", [P, 42])
        cm32 = sb("cm32", [P, P + 512])
        cmb = sb("cmb", [P, 4 * P + 3], BF16)
        identb = sb("identb", [P, P], BF16)
        onesb = sb("onesb", [P, P], BF16)
        wst = sb("wst", [P, 512])
        wcat = sb("wcat", [P, 512], BF16)
        wgl3 = sb("wgl3", [P, KC, P], BF16)
        wpe_t = sb("wpe_t", [P, 2, D], BF16)
        wslot = [sb("wslot%d" % i, [P, KC, 512], BF16) for i in range(NWSLOT)]
        xtok = [sb("xtok0", [P, NT, D])]
        htok = sb("htok", [P, NT, D], BF16)
        hT = sb("hT", [P, KC, TB], BF16)
        junk = sb("junk", [P, D], BF16)
        stat = sb("stat", [P, 16])
        glcat = sb("glcat", [P, TB], BF16)
        sph = sb("sph", [P, NT, 512], BF16)
        spl = sb("spl", [P, NT, 512], BF16)
        kdec = sb("kdec", [P, NT, 512], BF16)
        vm = sb("vm", [P, 8, 512], BF16)
        qk = sb("qk", [P, 8, 512], BF16)
        dec = sb("dec", [P, 32])
        at_t = sb("at_t", [P, NT, 512], BF16)
        s32 = sb("s32", [P, 4, 256])
        big = sb("big", [P, 8 * 1024], BF16)
        sbf = big[:, 0:4096].rearrange("p (c n) -> p c n", c=4)
        sqb = sb("sqb", [P, 2, TB], BF16)
        ozT = sb("ozT", [P, KC, TB], BF16)
        utmp = [sb("utmp%d" % i, [P, TB + 2]) for i in range(2)]
        ucar = [sb("ucar%d" % i, [P, KC, 2]) for i in range(2)]
        ptok = [sb("ptok0", [P, NT, 256], BF16)]
        xstage = sb("xstage", [P, 2, D])
        pT = sb("pT", [P, 2, TB], BF16)
        _ot = big[:, 4096:8192].bitcast(F32)
        otile = [_ot[:, i * D:(i + 1) * D] for i in range(2)]
        tmps = [sb("tmp%d" % i, [P, TB]) for i in range(NTMP)]
        tctr = [0]

        def newtmp():
            i = tctr[0] % NTMP
            tctr[0] += 1
            return tmps[i], ("tmp", i)

        banks = [st.enter_context(nc.psum_tensor("bank%d" % i, [P, 512], F32)) for i in range(8)]
        bctr = [0]

        def newbank():
            i = bctr[0] % 8
            bctr[0] += 1
            return banks[i], ("bank", i)

        g_mix = rowp_t[:, 0:D]
        g_ple = rowp_t[:, D:2 * D]
        g_fin = rowp_t[:, 2 * D:3 * D]
        mask4 = cm32[:, P:P + 512]
        tri_incl = cmb[:, 0:P]
        tri_strict = cmb[:, P:2 * P]
        chunkind = cmb[:, 2 * P:2 * P + 2]
        negcol = cmb[:, 2 * P + 2:2 * P + 3]
        tri_strict128 = cmb[:, 2 * P + 3:3 * P + 3]
        allneg = cmb[:, 3 * P + 3:4 * P + 3]
        ident32 = cm32[:, 0:P]
        Mst = big[:].bitcast(F32).rearrange("p (h f) -> p h f", h=4)
        xt = xtok[0]
        xk = "xtok0"
        xkt = lambda t: ("xt", t)

        if npre > 0:
            for t_ in range(NT):
                S.add("sp", lambda e, t_=t_: e.dma_start(out=xt[:, t_, :], in_=x_pre[t_ * P:(t_ + 1) * P, :]), writes=[xkt(t_)], dma=("xload", t_))
        S.add("sp", lambda e: e.dma_start(out=rowp_t[:], in_=rowp[:, 0:3 * D]), writes=["rowp"], dma="const", total=True)
        S.add("sp", lambda e: e.dma_start(out=colp_t[:], in_=colp), writes=["colp"], dma="const", total=True)
        S.add("sp", lambda e: e.dma_start(out=cm32[:, 0:P], in_=cmat[:, 0:P]), writes=["cm32a"], dma="const", total=True)
        S.add("sp", lambda e: e.dma_start(out=cm32[:, P:P + 512], in_=cmat[:, 4 * P:4 * P + 512]), writes=["cm32"], dma="const", total=True)
        S.add("pool", lambda e: e.dma_start(out=cmb[:, 0:2 * P], in_=cmat[:, 2 * P:4 * P]), writes=["cmb"], dma="const2", total=True)
        S.add("pool", lambda e: e.dma_start(out=cmb[:, 2 * P:4 * P + 3], in_=cmat[:, 4 * P + 512:CM_W]), writes=["cmb2"], dma="const2", total=True)
        S.add("dve", lambda e: e.memset(wst[:], 0.0), writes=["wst"])
        for g0 in (0, 32, 64):
            S.add("sp", lambda e, g0=g0: e.dma_start(out=wst[g0:g0 + 16, :], in_=w_gk2), reads=["wst"], writes=[("wstw", g0)], dma="const3", total=True)
        for g0 in (0, 64):
            S.add("sp", lambda e, g0=g0: e.dma_start(out=wst[g0 + 16:g0 + 17, :], in_=rowp[0:1, 3 * D:3 * D + 512]), reads=["wst"], writes=[("wstb", g0)], dma="const3", total=True)
        wst_keys = [("wstw", 0), ("wstw", 32), ("wstw", 64), ("wstb", 0), ("wstb", 64)]
        S.add("dve", lambda e: e.tensor_copy(out=wcat[:], in_=wst[:]), reads=wst_keys, writes=["wcat"])
        S.add("dve", lambda e: e.tensor_tensor(out=wcat[64:96, :], in0=wst[64:96, :], in1=wcat[64:96, :], op=ALU.subtract), reads=wst_keys + ["wcat"], writes=["wcat"])
        S.add("dve", lambda e: e.memset(glcat[0:32, :], 1.0), writes=["glcat"])
        S.add("dve", lambda e: e.memset(glcat[32:64, :], 0.0), writes=["glcat"])
        S.add("dve", lambda e: e.memset(glcat[64:128, :], 1.0), writes=["glcat"])
        S.add("dve", lambda e: e.memset(wgl3[:], 0.0), writes=["wgl0"])
        S.add("dve", lambda e: e.memset(Mst, 0.0), writes=["Mst"])
        S.add("pool", lambda e: e.dma_start(out=identb[:], in_=cmat[:, 0:P]), writes=["identb"], dma="const2", total=True)
        S.add("pool", lambda e: e.dma_start(out=onesb[:], in_=cmat[:, P:2 * P]), writes=["onesb"], dma="const2", total=True)
        wview = lambda w, c0, n: w[:, c0:c0 + n].rearrange("(kc p) n -> p kc n", p=P)
        for g0 in (0, 32, 64):
            S.add("pool", lambda e, g0=g0: e.dma_start(out=wgl3[:, :, g0:g0 + 16], in_=wview(w_in, C_GL, 16)), reads=["wgl0"], writes=[("wgl", g0)], dma="const4", total=True)
        S.add("pool", lambda e: e.dma_start(out=wpe_t[:], in_=w_pe.rearrange("(kc p) n -> p kc n", p=P)), writes=["wpe"], dma="const2", total=True)

        conv_keys = {"A": [], "B": []}
        conv_todo = []

        def conv_piece(dst, src, c0, grp):
            for r0 in (0, 512):
                key = ("wconv", grp, len(conv_keys[grp]))
                conv_keys[grp].append(key)
                gid = gmap.setdefault((dst, c0), len(gmap))
                k0 = r0 // P
                conv_todo.append((lambda e, r0=r0, gid=gid, k0=k0: e.dma_start(
                    out=s_all[gid][:, k0 * 512:(k0 + 4) * 512].rearrange("p (kc n) -> p kc n", kc=4),
                    in_=src[r0:r0 + 512, c0:c0 + 512].rearrange("(kc p) n -> p kc n", p=P)), key, grp))
        grpA_cols = (C_V, C_V + 512, C_CC, C_CC + 512, C_XC, C_XC + 512, C_K)
        for c0 in grpA_cols:
            conv_piece(s_in, w_in, c0, "A")
        for c0 in [C_Q, C_ZA, C_ZA + 512, C_CB, C_CB + 512, C_ZC, C_ZC + 512] + [C_GA + 512 * i for i in range(4)]:
            conv_piece(s_in, w_in, c0, "B")
        for dst, src in ((s_a, w_a), (s_c, w_c), (s_o, w_o), (s_pg, w_pg)):
            for c0 in (0, 512):
                conv_piece(dst, src, c0, "B")

        def emit_conv(n):
            for _ in range(min(n, len(conv_todo))):
                fn, key, grp = conv_todo.pop(0)
                S.add("pool", fn, writes=[key], dma="wconv" + grp, total=True)

        S.add("dve", lambda e: e.memset(s32[:], 0.0), writes=["s32"])

        wctr = [0]

        def load_group(src, c0):
            seq = wctr[0]
            i = seq % NWSLOT
            wctr[0] += 1
            key = ("wslot", i)
            grp = "A" if (src == s_in and c0 in grpA_cols) else "B"
            gid = gmap[(src, c0)]
            S.add("sp", lambda e: e.dma_start(out=wslot[i][:], in_=s_all[gid].rearrange("p (kc n) -> p kc n", kc=KC)),
                  reads=conv_keys[grp], writes=[key], dma=key)
            return (wslot[i], key, seq)

        def load_group_direct(src32, c0):
            seq = wctr[0]
            i = seq % NWSLOT
            wctr[0] += 1
            key = ("wslot", i)
            S.add("pool", lambda e: e.dma_start(out=wslot[i][:], in_=src32[:, c0:c0 + 512].rearrange("(kc p) n -> p kc n", p=P)),
                  writes=[key], dma=("wslot_sw", i))
            return (wslot[i], key, seq)

        def wchk(w):
            assert wctr[0] - w[2] <= NWSLOT, "weight slot reused while still needed"
            return w[0], w[1]

        def mm(out_ap, lhsT, rhs, start, stop, reads, bkey):
            S.add("pe", lambda e: e.matmul(out_ap, lhsT=lhsT, rhs=rhs, start=start, stop=stop), reads=reads, writes=[bkey])

        def fm_chunk(w, j0, m, act_t, akeys, ntok):
            wt, wkey = wchk(w)
            bank, bkey = newbank()
            for kc in range(KC):
                mm(bank[0:m, 0:ntok], wt[:, kc, j0:j0 + m], act_t[:, kc, 0:ntok], kc == 0, kc == KC - 1, [wkey] + akeys, bkey)
            return bank, bkey

        def tm_bank(wt, wkey, c0, act_t, akey, t, nk=KC):
            bank, bkey = newbank()
            for kc in range(nk):
                mm(bank[:, :], act_t[:, kc, t * P:(t + 1) * P], wt[:, kc, c0:c0 + 512], kc == 0, kc == nk - 1, [wkey, akey], bkey)
            return bank, bkey

        def act(out_ap, in_ap, func, reads, writes, **kw):
            S.add("act", lambda e: e.activation(out=out_ap, in_=in_ap, func=func, **kw), reads=reads, writes=writes)

        def acopy(out_ap, in_ap, reads, writes):
            S.add("act", lambda e: e.copy(out=out_ap, in_=in_ap), reads=reads, writes=writes)

        def tt(eng, out_ap, a, b, op, reads, writes):
            S.add(eng, lambda e: e.tensor_tensor(out=out_ap, in0=a, in1=b, op=op), reads=reads, writes=writes)

        def stt(eng, out_ap, a, scalar, b, op0, op1, reads, writes):
            S.add(eng, lambda e: e.scalar_tensor_tensor(out=out_ap, in0=a, scalar=scalar, in1=b, op0=op0, op1=op1), reads=reads, writes=writes)

        def rstd_from_ms(ms_ap, tmp_ap, out_ap, reads, writes):
            act(tmp_ap, ms_ap, AF.Ln, reads, [("lnstat", writes[0])], bias=EPS, scale=1.0)
            act(out_ap, tmp_ap, AF.Exp, [("lnstat", writes[0])], writes, scale=-0.5)

        def norm_and_transpose(src_tile, skey, nt, gain, stat_c0, ht=None, hkey="htok", dT=None, dkey="hT"):
            ht = htok if ht is None else ht
            dT = hT if dT is None else dT
            sk = skey if callable(skey) else (lambda t: skey)
            for t in range(nt):
                S.add("act", lambda e, t=t: e.activation(out=junk[:], in_=src_tile[:, t, :], func=AF.Square, scale=1.0 / 32.0,
                                                        accum_out=stat[:, stat_c0 + t:stat_c0 + t + 1]),
                      reads=[sk(t)], writes=["junk", ("ms", t)])
                rstd_from_ms(stat[:, stat_c0 + t:stat_c0 + t + 1], stat[:, 8 + t:9 + t], stat[:, 12 + t:13 + t], [("ms", t)], [("rstd", t)])
            for t in range(nt):
                hs = t
                stt("dve", ht[:, hs, :], src_tile[:, t, :], stat[:, 12 + t:13 + t], gain, ALU.mult, ALU.mult,
                    [sk(t), ("rstd", t), "rowp"], [(hkey, hs)])
                bank, bkey = newbank()
                bv = bank[:].bitcast(BF16)
                for c in range(KC):
                    S.add("pe", lambda e, hs=hs, c=c, bv=bv: e.transpose(out=bv[:, c * P:(c + 1) * P], in_=ht[:, hs, c * P:(c + 1) * P], identity=identb[:]),
                          reads=[(hkey, hs), "identb"], writes=[bkey])
                src3 = bv.rearrange("p (c n) -> p c n", c=KC)
                extra = [("ozT", c) for c in range(KC)] if dkey == "n1T" else []
                if t % 2 == 0:
                    acopy(dT[:, :, t * P:(t + 1) * P], src3, [], [bkey, (dkey, t)] + extra)
                else:
                    S.add("dve", lambda e, t=t, src3=src3: e.tensor_copy(out=dT[:, :, t * P:(t + 1) * P], in_=src3), reads=[], writes=[bkey, (dkey, t)] + extra)

        hT_keys = [("hT", t) for t in range(NT)]

        CMB = ["cmb", "cmb2"]

        def gate_softplus(actT, akeys, sh, sl, skey):
            bank, bkey = newbank()
            wglk = [("wgl", 0), ("wgl", 32), ("wgl", 64)]
            for kc in range(KC):
                mm(bank[:, 0:TB], wgl3[:, kc, :], actT[:, kc, :], kc == 0, kc == KC - 1, wglk + akeys, bkey)
            acopy(glcat[0:16, :], bank[0:16, 0:TB], [], [bkey, "glcat"])
            acopy(glcat[64:80, :], bank[64:80, 0:TB], [], [bkey, "glcat"])
            S.add("dve", lambda e, bank=bank: e.tensor_copy(out=glcat[32:48, :], in_=bank[32:48, 0:TB]), reads=[], writes=[bkey, "glcat"])
            tt("dve", glcat[32:48, :], bank[32:48, 0:TB], glcat[32:48, :], ALU.subtract, ["glcat"], [bkey, "glcat"])
            for t in range(NT):
                bank, bkey = newbank()
                cs = slice(t * P, (t + 1) * P)
                mm(bank[:, :], glcat[:, cs], wcat[:, :], True, True, ["glcat", "wcat"], bkey)
                et, ekey = newtmp()
                st_, stkey = newtmp()
                act(et[:], bank[:, :], AF.Exp, [], [bkey, ekey], scale=-1.0)
                act(st_[:], et[:], AF.Ln, [ekey], [stkey], bias=1.0, scale=1.0)
                S.add("pool", lambda e, t=t, st_=st_: e.tensor_copy(out=sh[:, t, :], in_=st_[:]), reads=[stkey], writes=[(skey + "h", t)])
                tt("pool", sl[:, t, :], st_[:], sh[:, t, :], ALU.subtract, [stkey, (skey + "h", t)], [(skey + "l", t)])

        def spmm(out_ap, use_sp_as_lhsT, sh, sl, skey, t, sp_cols, other, start, stop, bkey):
            for i, (sx, sfx) in enumerate(((sh, "h"), (sl, "l"))):
                a_ = sx[:, t, sp_cols]
                if use_sp_as_lhsT:
                    mm(out_ap, a_, other, start and i == 0, stop and i == 1, [(skey + sfx, t)] + CMB, bkey)
                else:
                    mm(out_ap, other, a_, start and i == 0, stop and i == 1, [(skey + sfx, t)] + CMB, bkey)

        def gate_and_state(with_sbf):
            wkg = load_group(s_in, C_K)
            wvg = [load_group(s_in, C_V), load_group(s_in, C_V + 512)]
            bankd, bdkey = newbank()
            for h in range(4):
                for t in range(NT):
                    spmm(bankd[:, h * 4 + t:h * 4 + t + 1], True, sph, spl, "sp", t, slice(h * P, (h + 1) * P), negcol, True, True, bdkey)
            act(dec[:, 0:16], bankd[:, 0:16], AF.Exp, [], [bdkey, "dec"])
            for t in range(NT):
                bank, bkey = newbank()
                spmm(bank[:, :], False, sph, spl, "sp", t, slice(0, 512), tri_strict, True, True, bkey)
                ed, edkey = newtmp()
                act(ed[:], bank[:, :], AF.Exp, [], [bkey, edkey])
                wt, wkey = wchk(wkg)
                kb, kbkey = tm_bank(wt, wkey, 0, hT, ("hT", t), t)
                tt("dve", kdec[:, t, :], kb[:, :], ed[:], ALU.mult, [edkey], [kbkey, ("kdec", t)])
                for half in range(2):
                    wt, wkey = wchk(wvg[half])
                    vb, vbkey = tm_bank(wt, wkey, 0, hT, ("hT", t), t)
                    ch = 2 * t + half
                    if half == 0:
                        acopy(vm[:, ch, :], vb[:, :], [], [vbkey, ("vm", ch)])
                    else:
                        S.add("dve", lambda e, ch=ch, vb=vb: e.tensor_copy(out=vm[:, ch, :], in_=vb[:, :]), reads=[], writes=[vbkey, ("vm", ch)])
            s32f = s32[:].rearrange("p h v -> p (h v)")
            if with_sbf:
                acopy(sbf[:, 0, :], s32f, ["s32"], [("sbf", 0), "Mst"])
            for c in range(NT):
                t = c
                for hp in range(2):
                    bank, bkey = newbank()
                    for hh in range(2):
                        h = hp * 2 + hh
                        mm(bank[:, hh * 256:(hh + 1) * 256], kdec[:, t, h * P:(h + 1) * P],
                           vm[:, 2 * t + h // 2, (h % 2) * 256:(h % 2 + 1) * 256],
                           True, True, [("kdec", t), ("vm", 2 * t + h // 2)], bkey)
                    for hh in range(2):
                        h = hp * 2 + hh
                        stt("dve", s32[:, h, :], s32[:, h, :], dec[:, h * 4 + c:h * 4 + c + 1], bank[:, hh * 256:(hh + 1) * 256], ALU.mult, ALU.add,
                            ["s32", "dec"], [bkey, "s32"])
                if with_sbf and c < NT - 1:
                    acopy(sbf[:, c + 1, :], s32f, ["s32"], [("sbf", c + 1), "Mst"])
            return wkg

        htokB = qk[:].rearrange("p c n -> p (c n)").rearrange("p (t d) -> p t d", t=NT)
        psets = [dict(ht=htok, hk="htok", dT=hT, dk="hT", sh=sph, sl=spl, sk="sp"),
                 dict(ht=htokB, hk="htokB", dT=ozT, dk="hTB", sh=vm[:, 0:4, :], sl=vm[:, 4:8, :], sk="spB")]

        def pre_L(pb, skip_load=False):
            for t in range(0 if not skip_load else NT, NT):
                S.add("sp", lambda e, pb=pb, t=t: e.dma_start(out=xt[:, t, :], in_=x_pre[pb * TB + t * P:pb * TB + (t + 1) * P, :]),
                      writes=[xkt(t)], dma=("xload", t))
            for t in range(NT):
                S.add("act", lambda e, t=t: e.activation(out=junk[:], in_=xt[:, t, :], func=AF.Square, scale=1.0 / 32.0, accum_out=stat[:, t:t + 1]),
                      reads=[xkt(t)], writes=["junk", ("ms", t)])
                rstd_from_ms(stat[:, t:t + 1], stat[:, 8 + t:9 + t], stat[:, 12 + t:13 + t], [("ms", t)], [("rstd", t)])

        def pre_H(pb, t):
            ps = psets[pb % 2]
            stt("dve", ps["ht"][:, t, :], xt[:, t, :], stat[:, 12 + t:13 + t], g_mix, ALU.mult, ALU.mult, [xkt(t), ("rstd", t), "rowp"], [(ps["hk"], t)])

        def pre_T(pb):
            ps = psets[pb % 2]
            for t in range(NT):
                bank, bkey = newbank()
                bv = bank[:].bitcast(BF16)
                for c in range(KC):
                    S.add("pe", lambda e, t=t, c=c, bv=bv, ps=ps: e.transpose(out=bv[:, c * P:(c + 1) * P], in_=ps["ht"][:, t, c * P:(c + 1) * P], identity=identb[:]),
                          reads=[(ps["hk"], t), "identb"], writes=[bkey])
                src3 = bv.rearrange("p (c n) -> p c n", c=KC)
                if t % 2 == 0:
                    acopy(ps["dT"][:, :, t * P:(t + 1) * P], src3, [], [bkey, (ps["dk"], t)])
                else:
                    S.add("dve", lambda e, t=t, src3=src3, ps=ps: e.tensor_copy(out=ps["dT"][:, :, t * P:(t + 1) * P], in_=src3), reads=[], writes=[bkey, (ps["dk"], t)])
            dkeys = [(ps["dk"], t) for t in range(NT)]
            gate_softplus(ps["dT"], dkeys, ps["sh"], ps["sl"], ps["sk"])

        def pre_B(pb, after_all=None):
            ps = psets[pb % 2]
            sh, sl, sk_ = ps["sh"], ps["sl"], ps["sk"]
            wkg = wk_pre
            bankd, bdkey = newbank()
            for h in range(4):
                for t in range(NT):
                    spmm(bankd[:, h:h + 1], True, sh, sl, sk_, t, slice(h * P, (h + 1) * P), negcol, t == 0, t == NT - 1, bdkey)
            act(dec[:, 0:4], bankd[:, 0:4], AF.Exp, [], [bdkey, "dec"])
            eds = []
            for t in range(NT):
                bank, bkey = newbank()
                spmm(bank[:, :], False, sh, sl, sk_, t, slice(0, 512), tri_strict128, True, t == NT - 1, bkey)
                for u in range(t + 1, NT):
                    spmm(bank[:, :], False, sh, sl, sk_, u, slice(0, 512), allneg, False, u == NT - 1, bkey)
                ed, edkey = newtmp()
                act(ed[:], bank[:, :], AF.Exp, [], [bkey, edkey])
                eds.append((ed, edkey))
            for t in range(NT):
                wt, wkey = wchk(wkg)
                kb, kbkey = tm_bank(wt, wkey, 0, ps["dT"], (ps["dk"], t), t)
                ed, edkey = eds[t]
                tt("dve", kdec[:, t, :], kb[:, :], ed[:], ALU.mult, [edkey], [kbkey, ("kdec", t)])
            for h in range(4):
                for half in range(2):
                    bank, bkey = newbank()
                    for t in range(NT):
                        mm(bank[:, :], kdec[:, t, h * P:(h + 1) * P], ps["ht"][:, t, half * 512:(half + 1) * 512], t == 0, t == NT - 1, [("kdec", t), (ps["hk"], t)], bkey)
                    stt("dve", Mst[:, h, half * 512:(half + 1) * 512], Mst[:, h, half * 512:(half + 1) * 512], dec[:, h:h + 1], bank[:, :],
                        ALU.mult, ALU.add, ["Mst", "dec"], [bkey, "Mst"])
            if after_all is not None:
                after_all()

        wk_pre = load_group_direct(w_in, C_K) if npre > 0 else None
        if npre > 0:
            pre_L(0, skip_load=True)
            for t in range(NT):
                pre_H(0, t)
            if npre > 1:
                pre_L(1)
            pre_T(0)
            if npre > 1:
                for t in range(NT):
                    pre_H(1, t)
                pre_T(1)
        for pb in range(npre):
            nb = pb + 2
            if nb < npre:
                pre_L(nb)
                pre_B(pb, after_all=lambda nb=nb: [pre_H(nb, t) for t in range(NT)])
                pre_T(nb)
            else:
                pre_B(pb)
            emit_conv(4)
        emit_conv(len(conv_todo))
        if npre > 1:
            fence_r = [(k, t) for k in ("htokB", "hTB", "spBh", "spBl") for t in range(NT)]
            fence_w = [(k, c) for k in ("qk", "ozT", "vm") for c in range(KC)]
            S.add("dve", lambda e: e.memset(stat[:, 0:1], 0.0), reads=fence_r, writes=fence_w + [("ms", 0)])
        if npre > 0:
            wvg = [load_group(s_in, C_V), load_group(s_in, C_V + 512)]
            for h in range(4):
                tb = [newbank(), newbank()]
                for c in range(KC):
                    bnk, bk_ = tb[c // 4]
                    S.add("pe", lambda e, h=h, c=c, bnk=bnk: e.transpose(out=bnk[:, (c % 4) * P:(c % 4 + 1) * P], in_=Mst[:, h, c * P:(c + 1) * P], identity=ident32),
                          reads=["Mst", "cm32a"], writes=[bk_])
                hi_, hik = newtmp()
                lo_, lok = newtmp()
                hib = hi_[:].bitcast(BF16)
                lob = lo_[:].bitcast(BF16)
                for g2 in range(2):
                    bnk, bk_ = tb[g2]
                    acopy(hib[:, g2 * 512:(g2 + 1) * 512], bnk[:, :], [], [bk_, hik])
                    tt("dve", lob[:, g2 * 512:(g2 + 1) * 512], bnk[:, :], hib[:, g2 * 512:(g2 + 1) * 512], ALU.subtract, [hik], [bk_, lok])
                wt, wkey = wchk(wvg[h // 2])
                sbk, sbkk = newbank()
                n = 0
                for src_, skey_ in ((hib, hik), (lob, lok)):
                    for c in range(KC):
                        mm(sbk[:, 0:256], src_[:, c * P:(c + 1) * P], wt[:, c, (h % 2) * 256:(h % 2 + 1) * 256], n == 0, n == 2 * KC - 1, [skey_, wkey], sbkk)
                        n += 1
                acopy(s32[:, h, :], sbk[:, 0:256], [], [sbkk, "s32"])

        xp = xstage[:, 0, :]
        S.add("sp", lambda e: e.dma_start(out=xp, in_=x_prev), writes=[("xs", 0)], dma="xprev")
        xp3 = xstage[:, 0:1, :]
        norm_and_transpose(xp3, ("xs", 0), 1, g_mix, 0)
        for half in range(2):
            wcc = load_group(s_in, C_CC + half * 512)
            wxc = load_group(s_in, C_XC + half * 512)
            for cc_ in range(4):
                c = half * 4 + cc_
                b1, k1 = fm_chunk(wcc, cc_ * P, P, hT, [("hT", 0)], P)
                b2, k2 = fm_chunk(wxc, cc_ * P, P, hT, [("hT", 0)], P)
                tm_, tk_ = newtmp()
                acopy(tm_[:, 0:2], b1[:, P - 2:P], [], [k1, tk_])
                tt("dve", ucar[0][:, c, :], b2[:, P - 2:P], tm_[:, 0:2], ALU.mult, [tk_], [k2, ("uc", 0, c)])

        stg = [xstage[:, 0, :], xstage[:, 1, :],
               kdec[:].rearrange("p t n -> p (t n)").bitcast(F32),
               at_t[:].rearrange("p t n -> p (t n)").bitcast(F32)]
        stg_keys = [[("xs", 0)], [("xs", 1)], [("kdec", t_) for t_ in range(NT)], [("at", t_) for t_ in range(NT)]]

        def stage_load(blk, t):
            S.add("sp", lambda e, blk=blk, t=t: e.dma_start(out=stg[t], in_=x_own[blk * TB + t * P:blk * TB + (t + 1) * P, :]),
                  writes=stg_keys[t], dma=("xsload", t))

        def front1(blk):
            for t in range(NT):
                S.add("act", lambda e, t=t: e.activation(out=junk[:], in_=stg[t], func=AF.Square, scale=1.0 / 32.0, accum_out=stat[:, t:t + 1]),
                      reads=stg_keys[t], writes=["junk", ("ms", t)])
                rstd_from_ms(stat[:, t:t + 1], stat[:, 8 + t:9 + t], stat[:, 12 + t:13 + t], [("ms", t)], [("rstd", t)])
                stt("dve", htok[:, t, :], stg[t], stat[:, 12 + t:13 + t], g_mix, ALU.mult, ALU.mult, stg_keys[t] + [("rstd", t), "rowp"], [("htok", t)])

        def front2(blk, after_tile=None):
            for t in range(NT):
                bank, bkey = newbank()
                bv = bank[:].bitcast(BF16)
                for c in range(KC):
                    S.add("pe", lambda e, t=t, c=c, bv=bv: e.transpose(out=bv[:, c * P:(c + 1) * P], in_=htok[:, t, c * P:(c + 1) * P], identity=identb[:]),
                          reads=[("htok", t), "identb"], writes=[bkey])
                src3 = bv.rearrange("p (c n) -> p c n", c=KC)
                if t % 2 == 0:
                    acopy(hT[:, :, t * P:(t + 1) * P], src3, [], [bkey, ("hT", t)])
                else:
                    S.add("dve", lambda e, t=t, src3=src3: e.tensor_copy(out=hT[:, :, t * P:(t + 1) * P], in_=src3), reads=[], writes=[bkey, ("hT", t)])
                if after_tile is not None:
                    after_tile(t)
            gate_softplus(hT, hT_keys, sph, spl, "sp")

        def xt_load(blk):
            for t in range(NT):
                S.add("sp", lambda e, blk=blk, t=t: e.dma_start(out=xt[:, t, :], in_=x_own[blk * TB + t * P:blk * TB + (t + 1) * P, :]),
                      writes=[xkt(t)], dma=("xload", t))

        for t_ in range(NT):
            stage_load(0, t_)
        front1(0)
        front2(0)
        for blk in range(nown):
            slot = 0
            S.add("pool", lambda e, blk=blk: e.dma_start(out=ptok[0][:], in_=p_own[blk * TB:(blk + 1) * TB, :].rearrange("(t p) d -> p t d", p=P)),
                  writes=[("ptok", 0)], dma=("pload", 0))
            wkg = gate_and_state(True)

            wqg = load_group(s_in, C_Q)
            for h in range(4):
                bank, bkey = newbank()
                for t in range(NT):
                    spmm(bank[:, t * P:(t + 1) * P], True, sph, spl, "sp", t, slice(h * P, (h + 1) * P), tri_incl, True, True, bkey)
                eq_, eqk = newtmp()
                ek_, ekk = newtmp()
                act(eq_[:], bank[:, :], AF.Exp, [], [bkey, eqk])
                act(ek_[:], bank[:, :], AF.Exp, [], [bkey, ekk], scale=-1.0)
                qb, qbk = fm_chunk(wqg, h * P, P, hT, hT_keys, TB)
                stt("dve", qk[:, h, :], qb[:, :], 128.0 ** -0.5, eq_[:], ALU.mult, ALU.mult, [eqk], [qbk, ("qk", h)])
                kb, kbk = fm_chunk(wkg, h * P, P, hT, hT_keys, TB)
                tt("dve", qk[:, 4 + h, :], kb[:, :], ek_[:], ALU.mult, [ekk], [kbk, ("qk", 4 + h)])
            for t in range(NT):
                bank, bkey = newbank()
                for h in range(4):
                    mm(bank[:, h * P:(h + 1) * P], qk[:, 4 + h, t * P:(t + 1) * P], qk[:, h, t * P:(t + 1) * P], True, True, [("qk", 4 + h), ("qk", h)], bkey)
                tt("dve", at_t[:, t, :], bank[:, :], mask4, ALU.mult, ["cm32"], [bkey, ("at", t)])
            for h in range(4):
                if h % 2 == 0:
                    wzg = load_group(s_in, C_ZA + (h // 2) * 512)
                held = []
                for vh in range(2):
                    oc = h * 2 + vh
                    ob, ok = newbank()
                    for t in range(NT):
                        vch = 2 * t + h // 2
                        v0 = (h % 2) * 256 + vh * P
                        mm(ob[:, t * P:(t + 1) * P], vm[:, vch, v0:v0 + P], at_t[:, t, h * P:(h + 1) * P], True, False, [("vm", vch), ("at", t)], ok)
                        mm(ob[:, t * P:(t + 1) * P], sbf[:, t, h * 256 + vh * P:h * 256 + (vh + 1) * P],
                           qk[:, h, t * P:(t + 1) * P], False, True, [("sbf", t), ("qk", h)], ok)
                    act(sqb[:, vh, :], ob[:, :], AF.Square, [], [ok, ("sqb", vh)])
                    zb, zk = fm_chunk(wzg, (oc % 4) * P, P, hT, hT_keys, TB)
                    sz_, szk = newtmp()
                    act(sz_[:], zb[:, :], AF.Silu, [], [zk, szk])
                    held.append((ob, ok, sz_, szk))
                sbk, sbkk = newbank()
                mm(sbk[:, :], onesb[:], sqb[:, 0, :], True, False, ["onesb", ("sqb", 0)], sbkk)
                mm(sbk[:, :], onesb[:], sqb[:, 1, :], False, True, ["onesb", ("sqb", 1)], sbkk)
                ln_, lnk = newtmp()
                rs_, rsk = newtmp()
                act(ln_[:], sbk[:, :], AF.Ln, [], [sbkk, lnk], bias=EPS, scale=1.0)
                act(rs_[:], ln_[:], AF.Exp, [lnk], [rsk], scale=-0.5)
                for vh in range(2):
                    oc = h * 2 + vh
                    ob, ok, sz_, szk = held[vh]
                    tt("pool", sz_[:], sz_[:], rs_[:], ALU.mult, [szk, rsk], [szk])
                    stt("dve", ozT[:, oc, :], ob[:, :], colp_t[:, 40 + vh:41 + vh], sz_[:], ALU.mult, ALU.mult, [szk, "colp"], [ok, ("ozT", oc)] + [("n1T", t_) for t_ in range(NT)])
            us, uns = blk % 2, (blk + 1) % 2
            for half in range(2):
                wcc = load_group(s_in, C_CC + half * 512)
                wxc = load_group(s_in, C_XC + half * 512)
                wcb = load_group(s_in, C_CB + half * 512)
                wzc = load_group(s_in, C_ZC + half * 512)
                if half == 1 and blk + 1 < nown:
                    for t_ in range(NT):
                        stage_load(blk + 1, t_)
                for cc_ in range(4):
                    c = half * 4 + cc_
                    i2 = c % 2
                    ut = utmp[i2]
                    ukey = ("utmp", i2)
                    b1, k1 = fm_chunk(wcc, cc_ * P, P, hT, hT_keys, TB)
                    b2, k2 = fm_chunk(wxc, cc_ * P, P, hT, hT_keys, TB)
                    cs_, csk = newtmp()
                    acopy(cs_[:], b1[:, :], [], [k1, csk])
                    S.add("pool", lambda e, c=c, ut=ut, us=us: e.tensor_copy(out=ut[:, 0:2], in_=ucar[us][:, c, :]), reads=[("uc", us, c)], writes=[ukey])
                    tt("dve", ut[:, 2:TB + 2], b2[:, :], cs_[:], ALU.mult, [csk], [k2, ukey])
                    S.add("pool", lambda e, c=c, ut=ut, uns=uns: e.tensor_copy(out=ucar[uns][:, c, :], in_=ut[:, TB:TB + 2]), reads=[ukey], writes=[("uc", uns, c)])
                    cw = lambda k, c=c: colp_t[:, 16 + c * 3 + k:17 + c * 3 + k]
                    y_, yk = newtmp()
                    S.add("act", lambda e, ut=ut, y_=y_, cw=cw: e.mul(out=y_[:], in_=ut[:, 2:TB + 2], mul=cw(2)), reads=[ukey, "colp"], writes=[yk])
                    for kk in (1, 0):
                        y2_, y2k = newtmp()
                        S.add("act", lambda e, ut=ut, y2_=y2_, cw=cw, kk=kk: e.mul(out=y2_[:], in_=ut[:, kk:TB + kk], mul=cw(kk)), reads=[ukey, "colp"], writes=[y2k])
                        tt("pool", y_[:], y_[:], y2_[:], ALU.add, [yk, y2k], [yk])
                    b3, k3 = fm_chunk(wcb, cc_ * P, P, hT, hT_keys, TB)
                    b4, k4 = fm_chunk(wzc, cc_ * P, P, hT, hT_keys, TB)
                    tt("dve", y_[:], b3[:, :], y_[:], ALU.mult, [yk], [k3, yk])
                    sc_, sck = newtmp()
                    act(sc_[:], b4[:, :], AF.Silu, [], [k4, sck])
                    tt("pool", qk[:, c, :], y_[:], sc_[:], ALU.mult, [yk, sck], [("qk", c)])
            if blk + 1 < nown:
                front1(blk + 1)
            oz_keys = [("ozT", c) for c in range(KC)]
            cz_keys = [("qk", c) for c in range(KC)]
            for half in range(2):
                wag = load_group(s_a, half * 512)
                wgag = load_group(s_in, C_GA + half * 512)
                wcg = load_group(s_c, half * 512)
                wgcg = load_group(s_in, C_GC + half * 512)
                if half == 1:
                    xt_load(blk)
                for ee in range(4):
                    e_ = half * 4 + ee
                    ba, kba = fm_chunk(wag, ee * P, P, ozT, oz_keys, TB)
                    bg, kbg = fm_chunk(wgag, ee * P, P, hT, hT_keys, TB)
                    bc, kbc = fm_chunk(wcg, ee * P, P, qk, cz_keys, TB)
                    bh, kbh = fm_chunk(wgcg, ee * P, P, hT, hT_keys, TB)
                    sa_, sak = newtmp()
                    sc_, sck = newtmp()
                    act(sa_[:], bg[:, :], AF.Sigmoid, ["colp"], [kbg, sak], bias=colp_t[:, e_:e_ + 1], scale=1.0)
                    act(sc_[:], bh[:, :], AF.Sigmoid, ["colp"], [kbh, sck], bias=colp_t[:, 8 + e_:9 + e_], scale=1.0)
                    tt("dve", sa_[:], ba[:, :], sa_[:], ALU.mult, [sak], [kba, sak])
                    tt("dve", sc_[:], bc[:, :], sc_[:], ALU.mult, [sck], [kbc, sck])
                    tt("pool", vm[:, e_, :], sa_[:], sc_[:], ALU.add, [sak, sck], [("vm", e_)])
            m_keys = [("vm", c) for c in range(KC)]
            for half in range(2):
                wog = load_group(s_o, half * 512)
                for t in range(NT):
                    wt, wkey = wchk(wog)
                    bank, bkey = newbank()
                    for kc in range(KC):
                        mm(bank[:, :], vm[:, kc, t * P:(t + 1) * P], wt[:, kc, :], kc == 0, kc == KC - 1, [wkey] + m_keys, bkey)
                    tt("dve", xt[:, t, half * 512:(half + 1) * 512], bank[:, :], xt[:, t, half * 512:(half + 1) * 512], ALU.add, [xkt(t)], [bkey, xkt(t)])
            for t in range(NT):
                S.add("act", lambda e, t=t: e.activation(out=junk[:], in_=xt[:, t, :], func=AF.Square, scale=1.0 / 32.0, accum_out=stat[:, 4 + t:5 + t]),
                      reads=[xkt(t)], writes=["junk", ("ms", t)])
                rstd_from_ms(stat[:, 4 + t:5 + t], stat[:, 8 + t:9 + t], stat[:, 12 + t:13 + t], [("ms", t)], [("rstd", t)])

            def np_scale(t):
                stt("dve", htok[:, t, :], xt[:, t, :], stat[:, 12 + t:13 + t], g_ple, ALU.mult, ALU.mult, [xkt(t), ("rstd", t), "rowp"], [("htok", t)])
            if blk + 1 < nown:
                front2(blk + 1, after_tile=np_scale)
            else:
                for t in range(NT):
                    np_scale(t)
            n1_extra = [("ozT", c) for c in range(KC)]
            for t in range(NT):
                bank, bkey = newbank()
                bv = bank[:].bitcast(BF16)
                for c in range(KC):
                    S.add("pe", lambda e, t=t, c=c, bv=bv: e.transpose(out=bv[:, c * P:(c + 1) * P], in_=htok[:, t, c * P:(c + 1) * P], identity=identb[:]),
                          reads=[("htok", t), "identb"], writes=[bkey])
                src3 = bv.rearrange("p (c n) -> p c n", c=KC)
                if t % 2 == 0:
                    acopy(ozT[:, :, t * P:(t + 1) * P], src3, [], [bkey, ("n1T", t)] + n1_extra)
                else:
                    S.add("dve", lambda e, t=t, src3=src3: e.tensor_copy(out=ozT[:, :, t * P:(t + 1) * P], in_=src3), reads=[], writes=[bkey, ("n1T", t)] + n1_extra)
            for t in range(NT):
                bank, bkey = newbank()
                bv = bank[:].bitcast(BF16)
                for c in range(2):
                    S.add("pe", lambda e, t=t, c=c, bv=bv, slot=slot: e.transpose(out=bv[:, c * P:(c + 1) * P], in_=ptok[slot][:, t, c * P:(c + 1) * P], identity=identb[:]),
                          reads=[("ptok", slot), "identb"], writes=[bkey])
                acopy(pT[:, :, t * P:(t + 1) * P], bv[:, 0:2 * P].rearrange("p (c n) -> p c n", c=2), [], [bkey, ("pT", t)])
            for half in range(2):
                wgg = load_group(s_pg, half * 512)
                for t in range(NT):
                    wt, wkey = wchk(wgg)
                    bg, kbg = newbank()
                    for kc in range(KC):
                        mm(bg[:, :], ozT[:, kc, t * P:(t + 1) * P], wt[:, kc, :], kc == 0, kc == KC - 1, [wkey, ("n1T", t)], kbg)
                    bp, kbp = newbank()
                    for kc in range(2):
                        mm(bp[:, :], pT[:, kc, t * P:(t + 1) * P], wpe_t[:, kc, half * 512:(half + 1) * 512], kc == 0, kc == 1, ["wpe", ("pT", t)], kbp)
                    sg_, sgk = newtmp()
                    act(sg_[:], bg[:, :], AF.Sigmoid, [], [kbg, sgk])
                    tt("dve", sg_[:], bp[:, :], sg_[:], ALU.mult, [sgk], [kbp, sgk])
                    tt("pool", xt[:, t, half * 512:(half + 1) * 512], xt[:, t, half * 512:(half + 1) * 512], sg_[:], ALU.add, [xkt(t), sgk], [xkt(t)])
            for t in range(NT):
                i2 = t % 2
                S.add("act", lambda e, t=t: e.activation(out=junk[:], in_=xt[:, t, :], func=AF.Square, scale=1.0 / 32.0, accum_out=stat[:, t:t + 1]),
                      reads=[xkt(t)], writes=["junk", ("ms", t)])
                rstd_from_ms(stat[:, t:t + 1], stat[:, 8 + t:9 + t], stat[:, 12 + t:13 + t], [("ms", t)], [("rstd", t)])
                S.add("act", lambda e, t=t, i2=i2: e.mul(out=otile[i2][:], in_=xt[:, t, :], mul=stat[:, 12 + t:13 + t]),
                      reads=[xkt(t), ("rstd", t)], writes=[("otile", i2)])
                tt("pool", otile[i2][:], otile[i2][:], g_fin, ALU.mult, [("otile", i2), "rowp"], [("otile", i2)])
                S.add("pool", lambda e, t=t, i2=i2, blk=blk: e.dma_start(out=out[blk * TB + t * P:blk * TB + (t + 1) * P, :], in_=otile[i2][:]),
                      reads=[("otile", i2)], dma=("ostore", i2), out_dma=True)
        S.emit()
    return nc


def _consts():
    c = np.zeros((P, CM_W), np.float32)
    s = np.arange(P)[:, None]
    t = np.arange(P)[None, :]
    same = (s // P) == (t // P)
    c[:, 0:P] = np.eye(P, dtype=np.float32)
    c[:, P:2 * P] = 1.0 / 256.0
    c[:, 2 * P:3 * P] = np.where(same & (s <= t), -1.0 / 16.0, 0.0)
    c[:, 3 * P:4 * P] = np.where(same & (s > t), -1.0 / 16.0, 0.0)
    c[:, 4 * P:4 * P + 512] = np.tile(np.where(same & (s <= t), 1.0, 0.0).astype(np.float32), (1, 4))
    o = 4 * P + 512
    c[:, o] = np.where(np.arange(P) < 64, -1.0 / 16.0, 0.0)
    c[:, o + 1] = np.where(np.arange(P) >= 64, -1.0 / 16.0, 0.0)
    c[:, o + 2] = -1.0 / 16.0
    c[:, o + 3:o + 3 + P] = np.where(s > t, -1.0 / 16.0, 0.0)
    c[:, o + 3 + P:o + 3 + 2 * P] = -1.0 / 16.0
    return c


_NC_CACHE = {}


def kernel(x, p, norm_mix_g, w_in, b_merge, w_gk2, b_gk, gla_norm_g, conv_w, w_branch_a, w_branch_c, w_out,
           norm_ple_g, w_ple_gate, w_ple_proj, norm_final_g):
    f = lambda a: np.ascontiguousarray(np.asarray(a, dtype=np.float32))
    x, p = f(x), f(p)
    if "nc" not in _NC_CACHE:
        _NC_CACHE["nc"] = build_program()
    nc = _NC_CACHE["nc"]
    rowp = np.concatenate([f(norm_mix_g)[0], f(norm_ple_g)[0], f(norm_final_g), f(b_gk)[0]])[None, :].repeat(P, 0)
    colp = np.zeros((P, 42), np.float32)
    bm = f(b_merge)[0]
    colp[:, 0:8] = bm[:D].reshape(8, P).T
    colp[:, 8:16] = bm[D:].reshape(8, P).T
    cw = f(conv_w)[0]
    colp[:, 16:40] = cw.reshape(3, 8, P).transpose(2, 1, 0).reshape(P, 24)
    colp[:, 40:42] = f(gla_norm_g)[0].reshape(2, P).T
    cm = _consts()
    shared = {
        "w_in": f(w_in)[0], "w_gk2": f(w_gk2)[0], "w_a": f(w_branch_a)[0], "w_c": f(w_branch_c)[0],
        "w_o": f(w_out)[0], "w_pg": f(w_ple_gate)[0], "w_pe": f(w_ple_proj)[0],
        "rowp": np.ascontiguousarray(rowp), "colp": colp, "cmat": cm,
    }
    in_maps = []
    for c in range(NCORES):
        b, j = c // 4, c % 4
        s0 = j * SEG
        xpre = np.zeros((NPRE * TB, D), np.float32)
        if s0 > 0:
            xpre[NPRE * TB - s0:] = x[b, 0:s0]
        xprev = np.zeros((P, D), np.float32)
        if s0 > 0:
            xprev[:] = x[b, s0 - P:s0]
        m = dict(shared)
        m.update({"x_own": np.ascontiguousarray(x[b, s0:s0 + SEG]), "x_pre": xpre, "x_prev": xprev,
                  "p_own": np.ascontiguousarray(p[0, b, s0:s0 + SEG])})
        in_maps.append(m)
    res = run_bass_kernel_spmd(nc, in_maps, core_ids=list(range(NCORES)))
    outp = np.zeros((2, SEQ, D), np.float32)
    for c in range(NCORES):
        b, j = c // 4, c % 4
        outp[b, j * SEG:(j + 1) * SEG] = np.asarray(res.results[c]["out"], dtype=np.float32)
    return outp
```

```python
import contextlib
import numpy as np
import concourse.bass as bass
import concourse.mybir as mybir
from concourse.bass_utils import run_bass_kernel_spmd

F32 = mybir.dt.float32
BF16 = mybir.dt.bfloat16
AF = mybir.ActivationFunctionType
ALU = mybir.AluOpType

P = 128
D = 1024
KC = 8
TB = 512
NT = 4
SEQ = 8192
NCORES = 8
SEG = 2048
NOWN = SEG // TB
NPRE = (SEQ - SEG) // TB
IN_COLS = 9232
C_Q, C_K, C_V, C_ZA, C_GL, C_CB, C_CC, C_XC, C_ZC, C_GA, C_GC = (
    0, 512, 1024, 2048, 3072, 3088, 4112, 5136, 6160, 7184, 8208)
EPS = 1e-6
NWSLOT = 6
NTMP = 10
CM_W = 6 * P + 512 + 3
COMPUTE = ("pe", "act", "dve", "pool")


class Sched:
    def __init__(self, nc):
        self.nc = nc
        self.ops = []
        self.lastw = {}
        self.readers = {}
        self.dma_count = {}
        self.final_waits = []
        self.total_keys = set()

    def add(self, eng, fn, reads=(), writes=(), dma=None, out_dma=False, total=False):
        i = len(self.ops)
        deps = {}
        for k in reads:
            w = self.lastw.get(k)
            if w is not None:
                deps[w] = "raw"
        for k in writes:
            w = self.lastw.get(k)
            if w is not None and w not in deps:
                deps[w] = "waw"
            for r in self.readers.get(k, ()):
                if r not in deps:
                    deps[r] = "war"
        for k in reads:
            self.readers.setdefault(k, []).append(i)
        for k in writes:
            self.lastw[k] = i
            self.readers[k] = []
        deps.pop(i, None)
        op = dict(eng=eng, fn=fn, deps=deps, dma=dma, need_inc=False)
        if dma is not None:
            c = self.dma_count.get(dma, 0) + 1
            self.dma_count[dma] = c
            op["dma_val"] = 16 * c
            if total:
                self.total_keys.add(dma)
            if out_dma and dma not in self.final_waits:
                self.final_waits.append(dma)
        self.ops.append(op)
        return i

    def emit(self):
        nc = self.nc
        ops = self.ops
        for op in ops:
            keep = {}
            for d, kind in op["deps"].items():
                p = ops[d]
                if p["dma"] is None and op["dma"] is None and p["eng"] == op["eng"]:
                    if op["eng"] == "pe" or kind != "raw":
                        continue
                keep[d] = kind
            latest = {}
            for d in list(keep):
                p = ops[d]
                if p["dma"] is None:
                    e_ = p["eng"]
                    if e_ in latest:
                        lo_ = min(latest[e_], d)
                        latest[e_] = max(latest[e_], d)
                        keep.pop(lo_)
                    else:
                        latest[e_] = d
            op["deps"] = keep
            for d in keep:
                ops[d]["need_inc"] = True
        cnt = {e: 0 for e in COMPUTE}
        for op in ops:
            if op["dma"] is None and op["need_inc"]:
                cnt[op["eng"]] += 1
                op["ms"] = cnt[op["eng"]]
        with contextlib.ExitStack() as stack:
            sems = {e: stack.enter_context(nc.semaphore("s_" + e)) for e in COMPUTE}
            dsem = {}
            for n, k in enumerate(self.dma_count):
                dsem[k] = stack.enter_context(nc.semaphore("d%d" % n))
            block = stack.enter_context(nc.Block())

            def run(engname, handle):
                waited = {}
                for op in ops:
                    if op["eng"] != engname:
                        continue
                    need = {}
                    for d in op["deps"]:
                        p = ops[d]
                        if p["dma"] is not None:
                            s, v = dsem[p["dma"]], p["dma_val"]
                            if p["dma"] in self.total_keys:
                                v = 16 * self.dma_count[p["dma"]]
                        else:
                            s, v = sems[p["eng"]], p["ms"]
                        if need.get(id(s), (None, 0))[1] < v:
                            need[id(s)] = (s, v)
                    for key, (s, v) in need.items():
                        if waited.get(key, 0) >= v:
                            continue
                        handle.wait_ge(s, v)
                        waited[key] = v
                    inst = op["fn"](handle)
                    if op["dma"] is not None:
                        inst.then_inc(dsem[op["dma"]], 16)
                    elif op["need_inc"]:
                        inst.then_inc(sems[op["eng"]], 1)
                if engname == "sp":
                    for k in self.final_waits:
                        handle.wait_ge(dsem[k], 16 * self.dma_count[k])

            @block.sync
            def _(h):
                run("sp", h)

            @block.scalar
            def _(h):
                run("act", h)

            @block.vector
            def _(h):
                run("dve", h)

            @block.gpsimd
            def _(h):
                run("pool", h)

            @block.tensor
            def _(h):
                run("pe", h)


def build_program(npre=NPRE, nown=NOWN):
    nc = bass.Bass("TRN2", target_bir_lowering=False)
    dr = lambda name, shape, kind="ExternalInput", dt=F32: nc.dram_tensor(name, shape, dt, kind=kind).ap()
    x_own = dr("x_own", [nown * TB, D])
    x_pre = dr("x_pre", [max(npre, 1) * TB, D])
    x_prev = dr("x_prev", [P, D])
    p_own = dr("p_own", [nown * TB, 256])
    w_in = dr("w_in", [D, IN_COLS])
    w_gk2 = dr("w_gk2", [16, 512])
    w_a = dr("w_a", [D, D])
    w_c = dr("w_c", [D, D])
    w_o = dr("w_o", [D, D])
    w_pg = dr("w_pg", [D, D])
    w_pe = dr("w_pe", [256, D])
    rowp = dr("rowp", [P, 3 * D + 512])
    colp = dr("colp", [P, 42])
    cmat = dr("cmat", [P, CM_W])
    out = dr("out", [nown * TB, D], kind="ExternalOutput")
    NGRP = 26
    s_all = dr("s_all", [NGRP, P, KC * 512], kind="Internal", dt=BF16)
    s_in, s_a, s_c, s_o, s_pg = "s_in", "s_a", "s_c", "s_o", "s_pg"
    gmap = {}

    S = Sched(nc)
    with contextlib.ExitStack() as st:
        sb = lambda name, shape, dt=F32: st.enter_context(nc.sbuf_tensor(name, shape, dt))
        rowp_t = sb("rowp_t", [P, 3 * D])
        colp_t = sb("colp_t", [P, 42])
        cm32 = sb("cm32", [P, P + 512])
        cmb = sb("cmb", [P, 4 * P + 3], BF16)
        identb = sb("identb", [P, P], BF16)
        onesb = sb("onesb", [P, P], BF16)
        wst = sb("wst", [P, 512])
        wcat = sb("wcat", [P, 512], BF16)
        wgl3 = sb("wgl3", [P, KC, P], BF16)
        wpe_t = sb("wpe_t", [P, 2, D], BF16)
        wslot = [sb("wslot%d" % i, [P, KC, 512], BF16) for i in range(NWSLOT)]
        xtok = [sb("xtok0", [P, NT, D])]
        htok = sb("htok", [P, NT, D], BF16)
        hT = sb("hT", [P, KC, TB], BF16)
        junk = sb("junk", [P, D], BF16)
        stat = sb("stat", [P, 16])
        glcat = sb("glcat", [P, TB], BF16)
        sph = sb("sph", [P, NT, 512], BF16)
        spl = sb("spl", [P, NT, 512], BF16)
        kdec = sb("kdec", [P, NT, 512], BF16)
        vm = sb("vm", [P, 8, 512], BF16)
        qk = sb("qk", [P, 8, 512], BF16)
        dec = sb("dec", [P, 32])
        at_t = sb("at_t", [P, NT, 512], BF16)
        s32 = sb("s32", [P, 4, 256])
        big = sb("big", [P, 8 * 1024], BF16)
        sbf = big[:, 0:4096].rearrange("p (c n) -> p c n", c=4)
        sqb = sb("sqb", [P, 2, TB], BF16)
        ozT = sb("ozT", [P, KC, TB], BF16)
        utmp = [sb("utmp%d" % i, [P, TB + 2]) for i in range(2)]
        ucar = [sb("ucar%d" % i, [P, KC, 2]) for i in range(2)]
        ptok = [sb("ptok0", [P, NT, 256], BF16)]
        xstage = sb("xstage", [P, 2, D])
        pT = sb("pT", [P, 2, TB], BF16)
        _ot = big[:, 4096:8192].bitcast(F32)
        otile = [_ot[:, i * D:(i + 1) * D] for i in range(2)]
        tmps = [sb("tmp%d" % i, [P, TB]) for i in range(NTMP)]
        tctr = [0]

        def newtmp():
            i = tctr[0] % NTMP
            tctr[0] += 1
            return tmps[i], ("tmp", i)

        banks = [st.enter_context(nc.psum_tensor("bank%d" % i, [P, 512], F32)) for i in range(8)]
        bctr = [0]

        def newbank():
            i = bctr[0] % 8
            bctr[0] += 1
            return banks[i], ("bank", i)

        g_mix = rowp_t[:, 0:D]
        g_ple = rowp_t[:, D:2 * D]
        g_fin = rowp_t[:, 2 * D:3 * D]
        mask4 = cm32[:, P:P + 512]
        tri_incl = cmb[:, 0:P]
        tri_strict = cmb[:, P:2 * P]
        chunkind = cmb[:, 2 * P:2 * P + 2]
        negcol = cmb[:, 2 * P + 2:2 * P + 3]
        tri_strict128 = cmb[:, 2 * P + 3:3 * P + 3]
        allneg = cmb[:, 3 * P + 3:4 * P + 3]
        ident32 = cm32[:, 0:P]
        Mst = big[:].bitcast(F32).rearrange("p (h f) -> p h f", h=4)
        xt = xtok[0]
        xk = "xtok0"
        xkt = lambda t: ("xt", t)

        S.add("sp", lambda e: e.dma_start(out=rowp_t[:], in_=rowp[:, 0:3 * D]), writes=["rowp"], dma="const", total=True)
        S.add("sp", lambda e: e.dma_start(out=colp_t[:], in_=colp), writes=["colp"], dma="const", total=True)
        S.add("sp", lambda e: e.dma_start(out=cm32[:, 0:P], in_=cmat[:, 0:P]), writes=["cm32a"], dma="const", total=True)
        S.add("sp", lambda e: e.dma_start(out=cm32[:, P:P + 512], in_=cmat[:, 4 * P:4 * P + 512]), writes=["cm32"], dma="const", total=True)
        S.add("pool", lambda e: e.dma_start(out=cmb[:, 0:2 * P], in_=cmat[:, 2 * P:4 * P]), writes=["cmb"], dma="const2", total=True)
        S.add("pool", lambda e: e.dma_start(out=cmb[:, 2 * P:4 * P + 3], in_=cmat[:, 4 * P + 512:CM_W]), writes=["cmb2"], dma="const2", total=True)
        S.add("dve", lambda e: e.memset(wst[:], 0.0), writes=["wst"])
        for g0 in (0, 32, 64):
            S.add("sp", lambda e, g0=g0: e.dma_start(out=wst[g0:g0 + 16, :], in_=w_gk2), reads=["wst"], writes=[("wstw", g0)], dma="const3", total=True)
        for g0 in (0, 64):
            S.add("sp", lambda e, g0=g0: e.dma_start(out=wst[g0 + 16:g0 + 17, :], in_=rowp[0:1, 3 * D:3 * D + 512]), reads=["wst"], writes=[("wstb", g0)], dma="const3", total=True)
        wst_keys = [("wstw", 0), ("wstw", 32), ("wstw", 64), ("wstb", 0), ("wstb", 64)]
        S.add("dve", lambda e: e.tensor_copy(out=wcat[:], in_=wst[:]), reads=wst_keys, writes=["wcat"])
        S.add("dve", lambda e: e.tensor_tensor(out=wcat[64:96, :], in0=wst[64:96, :], in1=wcat[64:96, :], op=ALU.subtract), reads=wst_keys + ["wcat"], writes=["wcat"])
        S.add("dve", lambda e: e.memset(glcat[0:32, :], 1.0), writes=["glcat"])
        S.add("dve", lambda e: e.memset(glcat[32:64, :], 0.0), writes=["glcat"])
        S.add("dve", lambda e: e.memset(glcat[64:128, :], 1.0), writes=["glcat"])
        S.add("dve", lambda e: e.memset(wgl3[:], 0.0), writes=["wgl0"])
        S.add("dve", lambda e: e.memset(Mst, 0.0), writes=["Mst"])
        S.add("pool", lambda e: e.dma_start(out=identb[:], in_=cmat[:, 0:P]), writes=["identb"], dma="const2", total=True)
        S.add("pool", lambda e: e.dma_start(out=onesb[:], in_=cmat[:, P:2 * P]), writes=["onesb"], dma="const2", total=True)
        wview = lambda w, c0, n: w[:, c0:c0 + n].rearrange("(kc p) n -> p kc n", p=P)
        for g0 in (0, 32, 64):
            S.add("pool", lambda e, g0=g0: e.dma_start(out=wgl3[:, :, g0:g0 + 16], in_=wview(w_in, C_GL, 16)), reads=["wgl0"], writes=[("wgl", g0)], dma="const4", total=True)
        S.add("pool", lambda e: e.dma_start(out=wpe_t[:], in_=w_pe.rearrange("(kc p) n -> p kc n", p=P)), writes=["wpe"], dma="const2", total=True)

        conv_keys = {"A": [], "B": []}
        conv_todo = []

        def conv_piece(dst, src, c0, grp):
            for r0 in (0, 512):
                key = ("wconv", grp, len(conv_keys[grp]))
                conv_keys[grp].append(key)
                gid = gmap.setdefault((dst, c0), len(gmap))
                k0 = r0 // P
                conv_todo.append((lambda e, r0=r0, gid=gid, k0=k0: e.dma_start(
                    out=s_all[gid][:, k0 * 512:(k0 + 4) * 512].rearrange("p (kc n) -> p kc n", kc=4),
                    in_=src[r0:r0 + 512, c0:c0 + 512].rearrange("(kc p) n -> p kc n", p=P)), key, grp))
        grpA_cols = (C_V, C_V + 512, C_CC, C_CC + 512, C_XC, C_XC + 512, C_K)
        for c0 in grpA_cols:
            conv_piece(s_in, w_in, c0, "A")
        for c0 in [C_Q, C_ZA, C_ZA + 512, C_CB, C_CB + 512, C_ZC, C_ZC + 512] + [C_GA + 512 * i for i in range(4)]:
            conv_piece(s_in, w_in, c0, "B")
        for dst, src in ((s_a, w_a), (s_c, w_c), (s_o, w_o), (s_pg, w_pg)):
            for c0 in (0, 512):
                conv_piece(dst, src, c0, "B")

        def emit_conv(n):
            for _ in range(min(n, len(conv_todo))):
                fn, key, grp = conv_todo.pop(0)
                S.add("pool", fn, writes=[key], dma="wconv" + grp, total=True)

        S.add("dve", lambda e: e.memset(s32[:], 0.0), writes=["s32"])

        wctr = [0]

        def load_group(src, c0):
            seq = wctr[0]
            i = seq % NWSLOT
            wctr[0] += 1
            key = ("wslot", i)
            grp = "A" if (src == s_in and c0 in grpA_cols) else "B"
            gid = gmap[(src, c0)]
            S.add("sp", lambda e: e.dma_start(out=wslot[i][:], in_=s_all[gid].rearrange("p (kc n) -> p kc n", kc=KC)),
                  reads=conv_keys[grp], writes=[key], dma=key)
            return (wslot[i], key, seq)

        def load_group_direct(src32, c0):
            seq = wctr[0]
            i = seq % NWSLOT
            wctr[0] += 1
            key = ("wslot", i)
            S.add("pool", lambda e: e.dma_start(out=wslot[i][:], in_=src32[:, c0:c0 + 512].rearrange("(kc p) n -> p kc n", p=P)),
                  writes=[key], dma=("wslot_sw", i))
            return (wslot[i], key, seq)

        def wchk(w):
            assert wctr[0] - w[2] <= NWSLOT, "weight slot reused while still needed"
            return w[0], w[1]

        def mm(out_ap, lhsT, rhs, start, stop, reads, bkey):
            S.add("pe", lambda e: e.matmul(out_ap, lhsT=lhsT, rhs=rhs, start=start, stop=stop), reads=reads, writes=[bkey])

        def fm_chunk(w, j0, m, act_t, akeys, ntok):
            wt, wkey = wchk(w)
            bank, bkey = newbank()
            for kc in range(KC):
                mm(bank[0:m, 0:ntok], wt[:, kc, j0:j0 + m], act_t[:, kc, 0:ntok], kc == 0, kc == KC - 1, [wkey] + akeys, bkey)
            return bank, bkey

        def tm_bank(wt, wkey, c0, act_t, akey, t, nk=KC):
            bank, bkey = newbank()
            for kc in range(nk):
                mm(bank[:, :], act_t[:, kc, t * P:(t + 1) * P], wt[:, kc, c0:c0 + 512], kc == 0, kc == nk - 1, [wkey, akey], bkey)
            return bank, bkey

        def act(out_ap, in_ap, func, reads, writes, **kw):
            S.add("act", lambda e: e.activation(out=out_ap, in_=in_ap, func=func, **kw), reads=reads, writes=writes)

        def acopy(out_ap, in_ap, reads, writes):
            S.add("act", lambda e: e.copy(out=out_ap, in_=in_ap), reads=reads, writes=writes)

        def tt(eng, out_ap, a, b, op, reads, writes):
            S.add(eng, lambda e: e.tensor_tensor(out=out_ap, in0=a, in1=b, op=op), reads=reads, writes=writes)

        def stt(eng, out_ap, a, scalar, b, op0, op1, reads, writes):
            S.add(eng, lambda e: e.scalar_tensor_tensor(out=out_ap, in0=a, scalar=scalar, in1=b, op0=op0, op1=op1), reads=reads, writes=writes)

        def rstd_from_ms(ms_ap, tmp_ap, out_ap, reads, writes):
            act(tmp_ap, ms_ap, AF.Ln, reads, [("lnstat", writes[0])], bias=EPS, scale=1.0)
            act(out_ap, tmp_ap, AF.Exp, [("lnstat", writes[0])], writes, scale=-0.5)

        def norm_and_transpose(src_tile, skey, nt, gain, stat_c0, ht=None, hkey="htok", dT=None, dkey="hT"):
            ht = htok if ht is None else ht
            dT = hT if dT is None else dT
            sk = skey if callable(skey) else (lambda t: skey)
            for t in range(nt):
                S.add("act", lambda e, t=t: e.activation(out=junk[:], in_=src_tile[:, t, :], func=AF.Square, scale=1.0 / 32.0,
                                                        accum_out=stat[:, stat_c0 + t:stat_c0 + t + 1]),
                      reads=[sk(t)], writes=["junk", ("ms", t)])
                rstd_from_ms(stat[:, stat_c0 + t:stat_c0 + t + 1], stat[:, 8 + t:9 + t], stat[:, 12 + t:13 + t], [("ms", t)], [("rstd", t)])
            for t in range(nt):
                hs = t
                stt("dve", ht[:, hs, :], src_tile[:, t, :], stat[:, 12 + t:13 + t], gain, ALU.mult, ALU.mult,
                    [sk(t), ("rstd", t), "rowp"], [(hkey, hs)])
                bank, bkey = newbank()
                bv = bank[:].bitcast(BF16)
                for c in range(KC):
                    S.add("pe", lambda e, hs=hs, c=c, bv=bv: e.transpose(out=bv[:, c * P:(c + 1) * P], in_=ht[:, hs, c * P:(c + 1) * P], identity=identb[:]),
                          reads=[(hkey, hs), "identb"], writes=[bkey])
                src3 = bv.rearrange("p (c n) -> p c n", c=KC)
                extra = [("ozT", c) for c in range(KC)] if dkey == "n1T" else []
                if t % 2 == 0:
                    acopy(dT[:, :, t * P:(t + 1) * P], src3, [], [bkey, (dkey, t)] + extra)
                else:
                    S.add("dve", lambda e, t=t, src3=src3: e.tensor_copy(out=dT[:, :, t * P:(t + 1) * P], in_=src3), reads=[], writes=[bkey, (dkey, t)] + extra)

        hT_keys = [("hT", t) for t in range(NT)]

        CMB = ["cmb", "cmb2"]

        def gate_softplus(actT, akeys, sh, sl, skey):
            bank, bkey = newbank()
            wglk = [("wgl", 0), ("wgl", 32), ("wgl", 64)]
            for kc in range(KC):
                mm(bank[:, 0:TB], wgl3[:, kc, :], actT[:, kc, :], kc == 0, kc == KC - 1, wglk + akeys, bkey)
            acopy(glcat[0:16, :], bank[0:16, 0:TB], [], [bkey, "glcat"])
            acopy(glcat[64:80, :], bank[64:80, 0:TB], [], [bkey, "glcat"])
            S.add("dve", lambda e, bank=bank: e.tensor_copy(out=glcat[32:48, :], in_=bank[32:48, 0:TB]), reads=[], writes=[bkey, "glcat"])
            tt("dve", glcat[32:48, :], bank[32:48, 0:TB], glcat[32:48, :], ALU.subtract, ["glcat"], [bkey, "glcat"])
            for t in range(NT):
                bank, bkey = newbank()
                cs = slice(t * P, (t + 1) * P)
                mm(bank[:, :], glcat[:, cs], wcat[:, :], True, True, ["glcat", "wcat"], bkey)
                et, ekey = newtmp()
                st_, stkey = newtmp()
                act(et[:], bank[:, :], AF.Exp, [], [bkey, ekey], scale=-1.0)
                act(st_[:], et[:], AF.Ln, [ekey], [stkey], bias=1.0, scale=1.0)
                S.add("pool", lambda e, t=t, st_=st_: e.tensor_copy(out=sh[:, t, :], in_=st_[:]), reads=[stkey], writes=[(skey + "h", t)])
                tt("pool", sl[:, t, :], st_[:], sh[:, t, :], ALU.subtract, [stkey, (skey + "h", t)], [(skey + "l", t)])

        def spmm(out_ap, use_sp_as_lhsT, sh, sl, skey, t, sp_cols, other, start, stop, bkey):
            for i, (sx, sfx) in enumerate(((sh, "h"), (sl, "l"))):
                a_ = sx[:, t, sp_cols]
                if use_sp_as_lhsT:
                    mm(out_ap, a_, other, start and i == 0, stop and i == 1, [(skey + sfx, t)] + CMB, bkey)
                else:
                    mm(out_ap, other, a_, start and i == 0, stop and i == 1, [(skey + sfx, t)] + CMB, bkey)

        def gate_and_state(with_sbf):
            wkg = load_group(s_in, C_K)
            wvg = [load_group(s_in, C_V), load_group(s_in, C_V + 512)]
            bankd, bdkey = newbank()
            for h in range(4):
                for t in range(NT):
                    spmm(bankd[:, h * 4 + t:h * 4 + t + 1], True, sph, spl, "sp", t, slice(h * P, (h + 1) * P), negcol, True, True, bdkey)
            act(dec[:, 0:16], bankd[:, 0:16], AF.Exp, [], [bdkey, "dec"])
            for t in range(NT):
                bank, bkey = newbank()
                spmm(bank[:, :], False, sph, spl, "sp", t, slice(0, 512), tri_strict, True, True, bkey)
                ed, edkey = newtmp()
                act(ed[:], bank[:, :], AF.Exp, [], [bkey, edkey])
                wt, wkey = wchk(wkg)
                kb, kbkey = tm_bank(wt, wkey, 0, hT, ("hT", t), t)
                tt("dve", kdec[:, t, :], kb[:, :], ed[:], ALU.mult, [edkey], [kbkey, ("kdec", t)])
                for half in range(2):
                    wt, wkey = wchk(wvg[half])
                    vb, vbkey = tm_bank(wt, wkey, 0, hT, ("hT", t), t)
                    ch = 2 * t + half
                    if half == 0:
                        acopy(vm[:, ch, :], vb[:, :], [], [vbkey, ("vm", ch)])
                    else:
                        S.add("dve", lambda e, ch=ch, vb=vb: e.tensor_copy(out=vm[:, ch, :], in_=vb[:, :]), reads=[], writes=[vbkey, ("vm", ch)])
            s32f = s32[:].rearrange("p h v -> p (h v)")
            if with_sbf:
                acopy(sbf[:, 0, :], s32f, ["s32"], [("sbf", 0), "Mst"])
            for c in range(NT):
                t = c
                for hp in range(2):
                    bank, bkey = newbank()
                    for hh in range(2):
                        h = hp * 2 + hh
                        mm(bank[:, hh * 256:(hh + 1) * 256], kdec[:, t, h * P:(h + 1) * P],
                           vm[:, 2 * t + h // 2, (h % 2) * 256:(h % 2 + 1) * 256],
                           True, True, [("kdec", t), ("vm", 2 * t + h // 2)], bkey)
                    for hh in range(2):
                        h = hp * 2 + hh
                        stt("dve", s32[:, h, :], s32[:, h, :], dec[:, h * 4 + c:h * 4 + c + 1], bank[:, hh * 256:(hh + 1) * 256], ALU.mult, ALU.add,
                            ["s32", "dec"], [bkey, "s32"])
                if with_sbf and c < NT - 1:
                    acopy(sbf[:, c + 1, :], s32f, ["s32"], [("sbf", c + 1), "Mst"])
            return wkg

        htokB = qk[:].rearrange("p c n -> p (c n)").rearrange("p (t d) -> p t d", t=NT)
        psets = [dict(ht=htok, hk="htok", dT=hT, dk="hT", sh=sph, sl=spl, sk="sp"),
                 dict(ht=htokB, hk="htokB", dT=ozT, dk="hTB", sh=vm[:, 0:4, :], sl=vm[:, 4:8, :], sk="spB")]

        def pre_L(pb):
            for t in range(NT):
                S.add("sp", lambda e, pb=pb, t=t: e.dma_start(out=xt[:, t, :], in_=x_pre[pb * TB + t * P:pb * TB + (t + 1) * P, :]),
                      writes=[xkt(t)], dma=("xload", t))
            for t in range(NT):
                S.add("act", lambda e, t=t: e.activation(out=junk[:], in_=xt[:, t, :], func=AF.Square, scale=1.0 / 32.0, accum_out=stat[:, t:t + 1]),
                      reads=[xkt(t)], writes=["junk", ("ms", t)])
                rstd_from_ms(stat[:, t:t + 1], stat[:, 8 + t:9 + t], stat[:, 12 + t:13 + t], [("ms", t)], [("rstd", t)])

        def pre_H(pb, t):
            ps = psets[pb % 2]
            stt("dve", ps["ht"][:, t, :], xt[:, t, :], stat[:, 12 + t:13 + t], g_mix, ALU.mult, ALU.mult, [xkt(t), ("rstd", t), "rowp"], [(ps["hk"], t)])

        def pre_T(pb):
            ps = psets[pb % 2]
            for t in range(NT):
                bank, bkey = newbank()
                bv = bank[:].bitcast(BF16)
                for c in range(KC):
                    S.add("pe", lambda e, t=t, c=c, bv=bv, ps=ps: e.transpose(out=bv[:, c * P:(c + 1) * P], in_=ps["ht"][:, t, c * P:(c + 1) * P], identity=identb[:]),
                          reads=[(ps["hk"], t), "identb"], writes=[bkey])
                src3 = bv.rearrange("p (c n) -> p c n", c=KC)
                if t % 2 == 0:
                    acopy(ps["dT"][:, :, t * P:(t + 1) * P], src3, [], [bkey, (ps["dk"], t)])
                else:
                    S.add("dve", lambda e, t=t, src3=src3, ps=ps: e.tensor_copy(out=ps["dT"][:, :, t * P:(t + 1) * P], in_=src3), reads=[], writes=[bkey, (ps["dk"], t)])
            dkeys = [(ps["dk"], t) for t in range(NT)]
            gate_softplus(ps["dT"], dkeys, ps["sh"], ps["sl"], ps["sk"])

        def pre_B(pb, after_all=None):
            ps = psets[pb % 2]
            sh, sl, sk_ = ps["sh"], ps["sl"], ps["sk"]
            wkg = wk_pre
            bankd, bdkey = newbank()
            for h in range(4):
                for t in range(NT):
                    spmm(bankd[:, h:h + 1], True, sh, sl, sk_, t, slice(h * P, (h + 1) * P), negcol, t == 0, t == NT - 1, bdkey)
            act(dec[:, 0:4], bankd[:, 0:4], AF.Exp, [], [bdkey, "dec"])
            eds = []
            for t in range(NT):
                bank, bkey = newbank()
                spmm(bank[:, :], False, sh, sl, sk_, t, slice(0, 512), tri_strict128, True, t == NT - 1, bkey)
                for u in range(t + 1, NT):
                    spmm(bank[:, :], False, sh, sl, sk_, u, slice(0, 512), allneg, False, u == NT - 1, bkey)
                ed, edkey = newtmp()
                act(ed[:], bank[:, :], AF.Exp, [], [bkey, edkey])
                eds.append((ed, edkey))
            for t in range(NT):
                wt, wkey = wchk(wkg)
                kb, kbkey = tm_bank(wt, wkey, 0, ps["dT"], (ps["dk"], t), t)
                ed, edkey = eds[t]
                tt("dve", kdec[:, t, :], kb[:, :], ed[:], ALU.mult, [edkey], [kbkey, ("kdec", t)])
            for h in range(4):
                for half in range(2):
                    bank, bkey = newbank()
                    for t in range(NT):
                        mm(bank[:, :], kdec[:, t, h * P:(h + 1) * P], ps["ht"][:, t, half * 512:(half + 1) * 512], t == 0, t == NT - 1, [("kdec", t), (ps["hk"], t)], bkey)
                    stt("dve", Mst[:, h, half * 512:(half + 1) * 512], Mst[:, h, half * 512:(half + 1) * 512], dec[:, h:h + 1], bank[:, :],
                        ALU.mult, ALU.add, ["Mst", "dec"], [bkey, "Mst"])
            if after_all is not None:
                after_all()

        wk_pre = load_group_direct(w_in, C_K) if npre > 0 else None
        for pb in range(min(2, npre)):
            pre_L(pb)
            for t in range(NT):
                pre_H(pb, t)
            pre_T(pb)
        for pb in range(npre):
            nb = pb + 2
            if nb < npre:
                pre_L(nb)
                pre_B(pb, after_all=lambda nb=nb: [pre_H(nb, t) for t in range(NT)])
                pre_T(nb)
            else:
                pre_B(pb)
            emit_conv(4)
        emit_conv(len(conv_todo))
        if npre > 1:
            fence_r = [(k, t) for k in ("htokB", "hTB", "spBh", "spBl") for t in range(NT)]
            fence_w = [(k, c) for k in ("qk", "ozT", "vm") for c in range(KC)]
            S.add("dve", lambda e: e.memset(stat[:, 0:1], 0.0), reads=fence_r, writes=fence_w + [("ms", 0)])
        if npre > 0:
            wvg = [load_group(s_in, C_V), load_group(s_in, C_V + 512)]
            for h in range(4):
                tb = [newbank(), newbank()]
                for c in range(KC):
                    bnk, bk_ = tb[c // 4]
                    S.add("pe", lambda e, h=h, c=c, bnk=bnk: e.transpose(out=bnk[:, (c % 4) * P:(c % 4 + 1) * P], in_=Mst[:, h, c * P:(c + 1) * P], identity=ident32),
                          reads=["Mst", "cm32a"], writes=[bk_])
                hi_, hik = newtmp()
                lo_, lok = newtmp()
                hib = hi_[:].bitcast(BF16)
                lob = lo_[:].bitcast(BF16)
                for g2 in range(2):
                    bnk, bk_ = tb[g2]
                    acopy(hib[:, g2 * 512:(g2 + 1) * 512], bnk[:, :], [], [bk_, hik])
                    tt("dve", lob[:, g2 * 512:(g2 + 1) * 512], bnk[:, :], hib[:, g2 * 512:(g2 + 1) * 512], ALU.subtract, [hik], [bk_, lok])
                wt, wkey = wchk(wvg[h // 2])
                sbk, sbkk = newbank()
                n = 0
                for src_, skey_ in ((hib, hik), (lob, lok)):
                    for c in range(KC):
                        mm(sbk[:, 0:256], src_[:, c * P:(c + 1) * P], wt[:, c, (h % 2) * 256:(h % 2 + 1) * 256], n == 0, n == 2 * KC - 1, [skey_, wkey], sbkk)
                        n += 1
                acopy(s32[:, h, :], sbk[:, 0:256], [], [sbkk, "s32"])

        xp = xstage[:, 0, :]
        S.add("sp", lambda e: e.dma_start(out=xp, in_=x_prev), writes=[("xs", 0)], dma="xprev")
        xp3 = xstage[:, 0:1, :]
        norm_and_transpose(xp3, ("xs", 0), 1, g_mix, 0)
        for half in range(2):
            wcc = load_group(s_in, C_CC + half * 512)
            wxc = load_group(s_in, C_XC + half * 512)
            for cc_ in range(4):
                c = half * 4 + cc_
                b1, k1 = fm_chunk(wcc, cc_ * P, P, hT, [("hT", 0)], P)
                b2, k2 = fm_chunk(wxc, cc_ * P, P, hT, [("hT", 0)], P)
                tm_, tk_ = newtmp()
                acopy(tm_[:, 0:2], b1[:, P - 2:P], [], [k1, tk_])
                tt("dve", ucar[0][:, c, :], b2[:, P - 2:P], tm_[:, 0:2], ALU.mult, [tk_], [k2, ("uc", 0, c)])

        stg = [xstage[:, 0, :], xstage[:, 1, :],
               kdec[:].rearrange("p t n -> p (t n)").bitcast(F32),
               at_t[:].rearrange("p t n -> p (t n)").bitcast(F32)]
        stg_keys = [[("xs", 0)], [("xs", 1)], [("kdec", t_) for t_ in range(NT)], [("at", t_) for t_ in range(NT)]]

        def stage_load(blk, t):
            S.add("sp", lambda e, blk=blk, t=t: e.dma_start(out=stg[t], in_=x_own[blk * TB + t * P:blk * TB + (t + 1) * P, :]),
                  writes=stg_keys[t], dma=("xsload", t))

        def front1(blk):
            for t in range(NT):
                S.add("act", lambda e, t=t: e.activation(out=junk[:], in_=stg[t], func=AF.Square, scale=1.0 / 32.0, accum_out=stat[:, t:t + 1]),
                      reads=stg_keys[t], writes=["junk", ("ms", t)])
                rstd_from_ms(stat[:, t:t + 1], stat[:, 8 + t:9 + t], stat[:, 12 + t:13 + t], [("ms", t)], [("rstd", t)])
                stt("dve", htok[:, t, :], stg[t], stat[:, 12 + t:13 + t], g_mix, ALU.mult, ALU.mult, stg_keys[t] + [("rstd", t), "rowp"], [("htok", t)])

        def front2(blk, after_tile=None):
            for t in range(NT):
                bank, bkey = newbank()
                bv = bank[:].bitcast(BF16)
                for c in range(KC):
                    S.add("pe", lambda e, t=t, c=c, bv=bv: e.transpose(out=bv[:, c * P:(c + 1) * P], in_=htok[:, t, c * P:(c + 1) * P], identity=identb[:]),
                          reads=[("htok", t), "identb"], writes=[bkey])
                src3 = bv.rearrange("p (c n) -> p c n", c=KC)
                if t % 2 == 0:
                    acopy(hT[:, :, t * P:(t + 1) * P], src3, [], [bkey, ("hT", t)])
                else:
                    S.add("dve", lambda e, t=t, src3=src3: e.tensor_copy(out=hT[:, :, t * P:(t + 1) * P], in_=src3), reads=[], writes=[bkey, ("hT", t)])
                if after_tile is not None:
                    after_tile(t)
            gate_softplus(hT, hT_keys, sph, spl, "sp")

        def xt_load(blk):
            for t in range(NT):
                S.add("sp", lambda e, blk=blk, t=t: e.dma_start(out=xt[:, t, :], in_=x_own[blk * TB + t * P:blk * TB + (t + 1) * P, :]),
                      writes=[xkt(t)], dma=("xload", t))

        for t_ in range(NT):
            stage_load(0, t_)
        front1(0)
        front2(0)
        for blk in range(nown):
            slot = 0
            S.add("pool", lambda e, blk=blk: e.dma_start(out=ptok[0][:], in_=p_own[blk * TB:(blk + 1) * TB, :].rearrange("(t p) d -> p t d", p=P)),
                  writes=[("ptok", 0)], dma=("pload", 0))
            wkg = gate_and_state(True)

            wqg = load_group(s_in, C_Q)
            for h in range(4):
                bank, bkey = newbank()
                for t in range(NT):
                    spmm(bank[:, t * P:(t + 1) * P], True, sph, spl, "sp", t, slice(h * P, (h + 1) * P), tri_incl, True, True, bkey)
                eq_, eqk = newtmp()
                ek_, ekk = newtmp()
                act(eq_[:], bank[:, :], AF.Exp, [], [bkey, eqk])
                act(ek_[:], bank[:, :], AF.Exp, [], [bkey, ekk], scale=-1.0)
                qb, qbk = fm_chunk(wqg, h * P, P, hT, hT_keys, TB)
                stt("dve", qk[:, h, :], qb[:, :], 128.0 ** -0.5, eq_[:], ALU.mult, ALU.mult, [eqk], [qbk, ("qk", h)])
                kb, kbk = fm_chunk(wkg, h * P, P, hT, hT_keys, TB)
                tt("dve", qk[:, 4 + h, :], kb[:, :], ek_[:], ALU.mult, [ekk], [kbk, ("qk", 4 + h)])
            for t in range(NT):
                bank, bkey = newbank()
                for h in range(4):
                    mm(bank[:, h * P:(h + 1) * P], qk[:, 4 + h, t * P:(t + 1) * P], qk[:, h, t * P:(t + 1) * P], True, True, [("qk", 4 + h), ("qk", h)], bkey)
                tt("dve", at_t[:, t, :], bank[:, :], mask4, ALU.mult, ["cm32"], [bkey, ("at", t)])
            for h in range(4):
                if h % 2 == 0:
                    wzg = load_group(s_in, C_ZA + (h // 2) * 512)
                held = []
                for vh in range(2):
                    oc = h * 2 + vh
                    ob, ok = newbank()
                    for t in range(NT):
                        vch = 2 * t + h // 2
                        v0 = (h % 2) * 256 + vh * P
                        mm(ob[:, t * P:(t + 1) * P], vm[:, vch, v0:v0 + P], at_t[:, t, h * P:(h + 1) * P], True, False, [("vm", vch), ("at", t)], ok)
                        mm(ob[:, t * P:(t + 1) * P], sbf[:, t, h * 256 + vh * P:h * 256 + (vh + 1) * P],
                           qk[:, h, t * P:(t + 1) * P], False, True, [("sbf", t), ("qk", h)], ok)
                    act(sqb[:, vh, :], ob[:, :], AF.Square, [], [ok, ("sqb", vh)])
                    zb, zk = fm_chunk(wzg, (oc % 4) * P, P, hT, hT_keys, TB)
                    sz_, szk = newtmp()
                    act(sz_[:], zb[:, :], AF.Silu, [], [zk, szk])
                    held.append((ob, ok, sz_, szk))
                sbk, sbkk = newbank()
                mm(sbk[:, :], onesb[:], sqb[:, 0, :], True, False, ["onesb", ("sqb", 0)], sbkk)
                mm(sbk[:, :], onesb[:], sqb[:, 1, :], False, True, ["onesb", ("sqb", 1)], sbkk)
                ln_, lnk = newtmp()
                rs_, rsk = newtmp()
                act(ln_[:], sbk[:, :], AF.Ln, [], [sbkk, lnk], bias=EPS, scale=1.0)
                act(rs_[:], ln_[:], AF.Exp, [lnk], [rsk], scale=-0.5)
                for vh in range(2):
                    oc = h * 2 + vh
                    ob, ok, sz_, szk = held[vh]
                    tt("pool", sz_[:], sz_[:], rs_[:], ALU.mult, [szk, rsk], [szk])
                    stt("dve", ozT[:, oc, :], ob[:, :], colp_t[:, 40 + vh:41 + vh], sz_[:], ALU.mult, ALU.mult, [szk, "colp"], [ok, ("ozT", oc)] + [("n1T", t_) for t_ in range(NT)])
            us, uns = blk % 2, (blk + 1) % 2
            for half in range(2):
                wcc = load_group(s_in, C_CC + half * 512)
                wxc = load_group(s_in, C_XC + half * 512)
                wcb = load_group(s_in, C_CB + half * 512)
                wzc = load_group(s_in, C_ZC + half * 512)
                if half == 1 and blk + 1 < nown:
                    for t_ in range(NT):
                        stage_load(blk + 1, t_)
                for cc_ in range(4):
                    c = half * 4 + cc_
                    i2 = c % 2
                    ut = utmp[i2]
                    ukey = ("utmp", i2)
                    b1, k1 = fm_chunk(wcc, cc_ * P, P, hT, hT_keys, TB)
                    b2, k2 = fm_chunk(wxc, cc_ * P, P, hT, hT_keys, TB)
                    cs_, csk = newtmp()
                    acopy(cs_[:], b1[:, :], [], [k1, csk])
                    b3, k3 = fm_chunk(wcb, cc_ * P, P, hT, hT_keys, TB)
                    b4, k4 = fm_chunk(wzc, cc_ * P, P, hT, hT_keys, TB)
                    sc_, sck = newtmp()
                    act(sc_[:], b4[:, :], AF.Silu, [], [k4, sck])
                    S.add("pool", lambda e, c=c, ut=ut, us=us: e.tensor_copy(out=ut[:, 0:2], in_=ucar[us][:, c, :]), reads=[("uc", us, c)], writes=[ukey])
                    tt("dve", ut[:, 2:TB + 2], b2[:, :], cs_[:], ALU.mult, [csk], [k2, ukey])
                    S.add("pool", lambda e, c=c, ut=ut, uns=uns: e.tensor_copy(out=ucar[uns][:, c, :], in_=ut[:, TB:TB + 2]), reads=[ukey], writes=[("uc", uns, c)])
                    cw = lambda k, c=c: colp_t[:, 16 + c * 3 + k:17 + c * 3 + k]
                    y_, yk = newtmp()
                    S.add("act", lambda e, ut=ut, y_=y_, cw=cw: e.mul(out=y_[:], in_=ut[:, 2:TB + 2], mul=cw(2)), reads=[ukey, "colp"], writes=[yk])
                    for kk in (1, 0):
                        y2_, y2k = newtmp()
                        S.add("act", lambda e, ut=ut, y2_=y2_, cw=cw, kk=kk: e.mul(out=y2_[:], in_=ut[:, kk:TB + kk], mul=cw(kk)), reads=[ukey, "colp"], writes=[y2k])
                        tt("pool", y_[:], y_[:], y2_[:], ALU.add, [yk, y2k], [yk])
                    tt("dve", y_[:], b3[:, :], y_[:], ALU.mult, [yk], [k3, yk])
                    tt("pool", qk[:, c, :], y_[:], sc_[:], ALU.mult, [yk, sck], [("qk", c)])
            if blk + 1 < nown:
                front1(blk + 1)
            oz_keys = [("ozT", c) for c in range(KC)]
            cz_keys = [("qk", c) for c in range(KC)]
            for half in range(2):
                wag = load_group(s_a, half * 512)
                wgag = load_group(s_in, C_GA + half * 512)
                wcg = load_group(s_c, half * 512)
                wgcg = load_group(s_in, C_GC + half * 512)
                if half == 1:
                    xt_load(blk)
                for ee in range(4):
                    e_ = half * 4 + ee
                    ba, kba = fm_chunk(wag, ee * P, P, ozT, oz_keys, TB)
                    bg, kbg = fm_chunk(wgag, ee * P, P, hT, hT_keys, TB)
                    bc, kbc = fm_chunk(wcg, ee * P, P, qk, cz_keys, TB)
                    bh, kbh = fm_chunk(wgcg, ee * P, P, hT, hT_keys, TB)
                    sa_, sak = newtmp()
                    sc_, sck = newtmp()
                    act(sa_[:], bg[:, :], AF.Sigmoid, ["colp"], [kbg, sak], bias=colp_t[:, e_:e_ + 1], scale=1.0)
                    act(sc_[:], bh[:, :], AF.Sigmoid, ["colp"], [kbh, sck], bias=colp_t[:, 8 + e_:9 + e_], scale=1.0)
                    tt("dve", sa_[:], ba[:, :], sa_[:], ALU.mult, [sak], [kba, sak])
                    tt("dve", sc_[:], bc[:, :], sc_[:], ALU.mult, [sck], [kbc, sck])
                    tt("pool", vm[:, e_, :], sa_[:], sc_[:], ALU.add, [sak, sck], [("vm", e_)])
            m_keys = [("vm", c) for c in range(KC)]
            for half in range(2):
                wog = load_group(s_o, half * 512)
                for t in range(NT):
                    wt, wkey = wchk(wog)
                    bank, bkey = newbank()
                    for kc in range(KC):
                        mm(bank[:, :], vm[:, kc, t * P:(t + 1) * P], wt[:, kc, :], kc == 0, kc == KC - 1, [wkey] + m_keys, bkey)
                    tt("dve", xt[:, t, half * 512:(half + 1) * 512], bank[:, :], xt[:, t, half * 512:(half + 1) * 512], ALU.add, [xkt(t)], [bkey, xkt(t)])
            for t in range(NT):
                S.add("act", lambda e, t=t: e.activation(out=junk[:], in_=xt[:, t, :], func=AF.Square, scale=1.0 / 32.0, accum_out=stat[:, 4 + t:5 + t]),
                      reads=[xkt(t)], writes=["junk", ("ms", t)])
                rstd_from_ms(stat[:, 4 + t:5 + t], stat[:, 8 + t:9 + t], stat[:, 12 + t:13 + t], [("ms", t)], [("rstd", t)])

            def np_scale(t):
                stt("dve", htok[:, t, :], xt[:, t, :], stat[:, 12 + t:13 + t], g_ple, ALU.mult, ALU.mult, [xkt(t), ("rstd", t), "rowp"], [("htok", t)])
            if blk + 1 < nown:
                front2(blk + 1, after_tile=np_scale)
            else:
                for t in range(NT):
                    np_scale(t)
            n1_extra = [("ozT", c) for c in range(KC)]
            for t in range(NT):
                bank, bkey = newbank()
                bv = bank[:].bitcast(BF16)
                for c in range(KC):
                    S.add("pe", lambda e, t=t, c=c, bv=bv: e.transpose(out=bv[:, c * P:(c + 1) * P], in_=htok[:, t, c * P:(c + 1) * P], identity=identb[:]),
                          reads=[("htok", t), "identb"], writes=[bkey])
                src3 = bv.rearrange("p (c n) -> p c n", c=KC)
                if t % 2 == 0:
                    acopy(ozT[:, :, t * P:(t + 1) * P], src3, [], [bkey, ("n1T", t)] + n1_extra)
                else:
                    S.add("dve", lambda e, t=t, src3=src3: e.tensor_copy(out=ozT[:, :, t * P:(t + 1) * P], in_=src3), reads=[], writes=[bkey, ("n1T", t)] + n1_extra)
            for t in range(NT):
                bank, bkey = newbank()
                bv = bank[:].bitcast(BF16)
                for c in range(2):
                    S.add("pe", lambda e, t=t, c=c, bv=bv, slot=slot: e.transpose(out=bv[:, c * P:(c + 1) * P], in_=ptok[slot][:, t, c * P:(c + 1) * P], identity=identb[:]),
                          reads=[("ptok", slot), "identb"], writes=[bkey])
                acopy(pT[:, :, t * P:(t + 1) * P], bv[:, 0:2 * P].rearrange("p (c n) -> p c n", c=2), [], [bkey, ("pT", t)])
            for half in range(2):
                wgg = load_group(s_pg, half * 512)
                for t in range(NT):
                    wt, wkey = wchk(wgg)
                    bg, kbg = newbank()
                    for kc in range(KC):
                        mm(bg[:, :], ozT[:, kc, t * P:(t + 1) * P], wt[:, kc, :], kc == 0, kc == KC - 1, [wkey, ("n1T", t)], kbg)
                    bp, kbp = newbank()
                    for kc in range(2):
                        mm(bp[:, :], pT[:, kc, t * P:(t + 1) * P], wpe_t[:, kc, half * 512:(half + 1) * 512], kc == 0, kc == 1, ["wpe", ("pT", t)], kbp)
                    sg_, sgk = newtmp()
                    act(sg_[:], bg[:, :], AF.Sigmoid, [], [kbg, sgk])
                    tt("dve", sg_[:], bp[:, :], sg_[:], ALU.mult, [sgk], [kbp, sgk])
                    tt("pool", xt[:, t, half * 512:(half + 1) * 512], xt[:, t, half * 512:(half + 1) * 512], sg_[:], ALU.add, [xkt(t), sgk], [xkt(t)])
            for t in range(NT):
                i2 = t % 2
                S.add("act", lambda e, t=t: e.activation(out=junk[:], in_=xt[:, t, :], func=AF.Square, scale=1.0 / 32.0, accum_out=stat[:, t:t + 1]),
                      reads=[xkt(t)], writes=["junk", ("ms", t)])
                rstd_from_ms(stat[:, t:t + 1], stat[:, 8 + t:9 + t], stat[:, 12 + t:13 + t], [("ms", t)], [("rstd", t)])
                S.add("act", lambda e, t=t, i2=i2: e.mul(out=otile[i2][:], in_=xt[:, t, :], mul=stat[:, 12 + t:13 + t]),
                      reads=[xkt(t), ("rstd", t)], writes=[("otile", i2)])
                tt("pool", otile[i2][:], otile[i2][:], g_fin, ALU.mult, [("otile", i2), "rowp"], [("otile", i2)])
                S.add("pool", lambda e, t=t, i2=i2, blk=blk: e.dma_start(out=out[blk * TB + t * P:blk * TB + (t + 1) * P, :], in_=otile[i2][:]),
                      reads=[("otile", i2)], dma=("ostore", i2), out_dma=True)
        S.emit()
    return nc


def _consts():
    c = np.zeros((P, CM_W), np.float32)
    s = np.arange(P)[:, None]
    t = np.arange(P)[None, :]
    same = (s // P) == (t // P)
    c[:, 0:P] = np.eye(P, dtype=np.float32)
    c[:, P:2 * P] = 1.0 / 256.0
    c[:, 2 * P:3 * P] = np.where(same & (s <= t), -1.0 / 16.0, 0.0)
    c[:, 3 * P:4 * P] = np.where(same & (s > t), -1.0 / 16.0, 0.0)
    c[:, 4 * P:4 * P + 512] = np.tile(np.where(same & (s <= t), 1.0, 0.0).astype(np.float32), (1, 4))
    o = 4 * P + 512
    c[:, o] = np.where(np.arange(P) < 64, -1.0 / 16.0, 0.0)
    c[:, o + 1] = np.where(np.arange(P) >= 64, -1.0 / 16.0, 0.0)
    c[:, o + 2] = -1.0 / 16.0
    c[:, o + 3:o + 3 + P] = np.where(s > t, -1.0 / 16.0, 0.0)
    c[:, o + 3 + P:o + 3 + 2 * P] = -1.0 / 16.0
    return c


_NC_CACHE = {}


def kernel(x, p, norm_mix_g, w_in, b_merge, w_gk2, b_gk, gla_norm_g, conv_w, w_branch_a, w_branch_c, w_out,
           norm_ple_g, w_ple_gate, w_ple_proj, norm_final_g):
    f = lambda a: np.ascontiguousarray(np.asarray(a, dtype=np.float32))
    x, p = f(x), f(p)
    if "nc" not in _NC_CACHE:
        _NC_CACHE["nc"] = build_program()
    nc = _NC_CACHE["nc"]
    rowp = np.concatenate([f(norm_mix_g)[0], f(norm_ple_g)[0], f(norm_final_g), f(b_gk)[0]])[None, :].repeat(P, 0)
    colp = np.zeros((P, 42), np.float32)
    bm = f(b_merge)[0]
    colp[:, 0:8] = bm[:D].reshape(8, P).T
    colp[:, 8:16] = bm[D:].reshape(8, P).T
    cw = f(conv_w)[0]
    colp[:, 16:40] = cw.reshape(3, 8, P).transpose(2, 1, 0).reshape(P, 24)
    colp[:, 40:42] = f(gla_norm_g)[0].reshape(2, P).T
    cm = _consts()
    shared = {
        "w_in": f(w_in)[0], "w_gk2": f(w_gk2)[0], "w_a": f(w_branch_a)[0], "w_c": f(w_branch_c)[0],
        "w_o": f(w_out)[0], "w_pg": f(w_ple_gate)[0], "w_pe": f(w_ple_proj)[0],
        "rowp": np.ascontiguousarray(rowp), "colp": colp, "cmat": cm,
    }
    in_maps = []
    for c in range(NCORES):
        b, j = c // 4, c % 4
        s0 = j * SEG
        xpre = np.zeros((NPRE * TB, D), np.float32)
        if s0 > 0:
            xpre[NPRE * TB - s0:] = x[b, 0:s0]
        xprev = np.zeros((P, D), np.float32)
        if s0 > 0:
            xprev[:] = x[b, s0 - P:s0]
        m = dict(shared)
        m.update({"x_own": np.ascontiguousarray(x[b, s0:s0 + SEG]), "x_pre": xpre, "x_prev": xprev,
                  "p_own": np.ascontiguousarray(p[0, b, s0:s0 + SEG])})
        in_maps.append(m)
    res = run_bass_kernel_spmd(nc, in_maps, core_ids=list(range(NCORES)))
    outp = np.zeros((2, SEQ, D), np.float32)
    for c in range(NCORES):
        b, j = c // 4, c % 4
        outp[b, j * SEG:(j + 1) * SEG] = np.asarray(res.results[c]["out"], dtype=np.float32)
    return outp
```

```python
import contextlib
import numpy as np
import concourse.bass as bass
import concourse.mybir as mybir
from concourse.bass_utils import run_bass_kernel_spmd

F32 = mybir.dt.float32
BF16 = mybir.dt.bfloat16
AF = mybir.ActivationFunctionType
ALU = mybir.AluOpType

P = 128
D = 1024
KC = 8
TB = 512
NT = 4
SEQ = 8192
NCORES = 8
SEG = 2048
NOWN = SEG // TB
NPRE = (SEQ - SEG) // TB
IN_COLS = 9232
C_Q, C_K, C_V, C_ZA, C_GL, C_CB, C_CC, C_XC, C_ZC, C_GA, C_GC = (
    0, 512, 1024, 2048, 3072, 3088, 4112, 5136, 6160, 7184, 8208)
EPS = 1e-6
NWSLOT = 6
NTMP = 10
CM_W = 6 * P + 512 + 3
COMPUTE = ("pe", "act", "dve", "pool")


class Sched:
    def __init__(self, nc):
        self.nc = nc
        self.ops = []
        self.lastw = {}
        self.readers = {}
        self.dma_count = {}
        self.final_waits = []
        self.total_keys = set()

    def add(self, eng, fn, reads=(), writes=(), dma=None, out_dma=False, total=False):
        i = len(self.ops)
        deps = {}
        for k in reads:
            w = self.lastw.get(k)
            if w is not None:
                deps[w] = "raw"
        for k in writes:
            w = self.lastw.get(k)
            if w is not None and w not in deps:
                deps[w] = "waw"
            for r in self.readers.get(k, ()):
                if r not in deps:
                    deps[r] = "war"
        for k in reads:
            self.readers.setdefault(k, []).append(i)
        for k in writes:
            self.lastw[k] = i
            self.readers[k] = []
        deps.pop(i, None)
        op = dict(eng=eng, fn=fn, deps=deps, dma=dma, need_inc=False)
        if dma is not None:
            c = self.dma_count.get(dma, 0) + 1
            self.dma_count[dma] = c
            op["dma_val"] = 16 * c
            if total:
                self.total_keys.add(dma)
            if out_dma and dma not in self.final_waits:
                self.final_waits.append(dma)
        self.ops.append(op)
        return i

    def emit(self):
        nc = self.nc
        ops = self.ops
        for op in ops:
            keep = {}
            for d, kind in op["deps"].items():
                p = ops[d]
                if p["dma"] is None and op["dma"] is None and p["eng"] == op["eng"]:
                    if op["eng"] == "pe" or kind != "raw":
                        continue
                keep[d] = kind
            latest = {}
            for d in list(keep):
                p = ops[d]
                if p["dma"] is None:
                    e_ = p["eng"]
                    if e_ in latest:
                        lo_ = min(latest[e_], d)
                        latest[e_] = max(latest[e_], d)
                        keep.pop(lo_)
                    else:
                        latest[e_] = d
            op["deps"] = keep
            for d in keep:
                ops[d]["need_inc"] = True
        cnt = {e: 0 for e in COMPUTE}
        for op in ops:
            if op["dma"] is None and op["need_inc"]:
                cnt[op["eng"]] += 1
                op["ms"] = cnt[op["eng"]]
        with contextlib.ExitStack() as stack:
            sems = {e: stack.enter_context(nc.semaphore("s_" + e)) for e in COMPUTE}
            dsem = {}
            for n, k in enumerate(self.dma_count):
                dsem[k] = stack.enter_context(nc.semaphore("d%d" % n))
            block = stack.enter_context(nc.Block())

            def run(engname, handle):
                waited = {}
                for op in ops:
                    if op["eng"] != engname:
                        continue
                    need = {}
                    for d in op["deps"]:
                        p = ops[d]
                        if p["dma"] is not None:
                            s, v = dsem[p["dma"]], p["dma_val"]
                            if p["dma"] in self.total_keys:
                                v = 16 * self.dma_count[p["dma"]]
                        else:
                            s, v = sems[p["eng"]], p["ms"]
                        if need.get(id(s), (None, 0))[1] < v:
                            need[id(s)] = (s, v)
                    for key, (s, v) in need.items():
                        if waited.get(key, 0) >= v:
                            continue
                        handle.wait_ge(s, v)
                        waited[key] = v
                    inst = op["fn"](handle)
                    if op["dma"] is not None:
                        inst.then_inc(dsem[op["dma"]], 16)
                    elif op["need_inc"]:
                        inst.then_inc(sems[op["eng"]], 1)
                if engname == "sp":
                    for k in self.final_waits:
                        handle.wait_ge(dsem[k], 16 * self.dma_count[k])

            @block.sync
            def _(h):
                run("sp", h)

            @block.scalar
            def _(h):
                run("act", h)

            @block.vector
            def _(h):
                run("dve", h)

            @block.gpsimd
            def _(h):
                run("pool", h)

            @block.tensor
            def _(h):
                run("pe", h)


def build_program(npre=NPRE, nown=NOWN):
    nc = bass.Bass("TRN2", target_bir_lowering=False)
    dr = lambda name, shape, kind="ExternalInput", dt=F32: nc.dram_tensor(name, shape, dt, kind=kind).ap()
    x_own = dr("x_own", [nown * TB, D])
    x_pre = dr("x_pre", [max(npre, 1) * TB, D])
    x_prev = dr("x_prev", [P, D])
    p_own = dr("p_own", [nown * TB, 256])
    w_in = dr("w_in", [D, IN_COLS])
    w_gk2 = dr("w_gk2", [16, 512])
    w_a = dr("w_a", [D, D])
    w_c = dr("w_c", [D, D])
    w_o = dr("w_o", [D, D])
    w_pg = dr("w_pg", [D, D])
    w_pe = dr("w_pe", [256, D])
    rowp = dr("rowp", [P, 3 * D + 512])
    colp = dr("colp", [P, 42])
    cmat = dr("cmat", [P, CM_W])
    out = dr("out", [nown * TB, D], kind="ExternalOutput")
    NGRP = 26
    s_all = dr("s_all", [NGRP, P, KC * 512], kind="Internal", dt=BF16)
    s_in, s_a, s_c, s_o, s_pg = "s_in", "s_a", "s_c", "s_o", "s_pg"
    gmap = {}

    S = Sched(nc)
    S32K = [("s32", h_) for h_ in range(4)]
    MSTK = [("Mst", h_, c_) for h_ in range(4) for c_ in range(2)]
    with contextlib.ExitStack() as st:
        sb = lambda name, shape, dt=F32: st.enter_context(nc.sbuf_tensor(name, shape, dt))
        rowp_t = sb("rowp_t", [P, 3 * D])
        colp_t = sb("colp_t", [P, 42])
        cm32 = sb("cm32", [P, P + 512])
        cmb = sb("cmb", [P, 4 * P + 3], BF16)
        identb = sb("identb", [P, P], BF16)
        onesb = sb("onesb", [P, P], BF16)
        wst = sb("wst", [P, 512])
        wcat = sb("wcat", [P, 512], BF16)
        wgl3 = sb("wgl3", [P, KC, P], BF16)
        wpe_t = sb("wpe_t", [P, 2, D], BF16)
        wslot = [sb("wslot%d" % i, [P, KC, 512], BF16) for i in range(NWSLOT)]
        xtok = [sb("xtok0", [P, NT, D])]
        htok = sb("htok", [P, NT, D], BF16)
        hT = sb("hT", [P, KC, TB], BF16)
        junk = sb("junk", [P, D], BF16)
        stat = sb("stat", [P, 16])
        glcat = sb("glcat", [P, TB], BF16)
        sph = sb("sph", [P, NT, 512], BF16)
        spl = sb("spl", [P, NT, 512], BF16)
        kdec = sb("kdec", [P, NT, 512], BF16)
        vm = sb("vm", [P, 8, 512], BF16)
        qk = sb("qk", [P, 8, 512], BF16)
        dec = sb("dec", [P, 32])
        at_t = sb("at_t", [P, NT, 512], BF16)
        s32 = sb("s32", [P, 4, 256])
        big = sb("big", [P, 8 * 1024], BF16)
        sbf = big[:, 0:4096].rearrange("p (c n) -> p c n", c=4)
        sqb = sb("sqb", [P, 2, TB], BF16)
        ozT = sb("ozT", [P, KC, TB], BF16)
        utmp = [sb("utmp%d" % i, [P, TB + 2]) for i in range(2)]
        ucar = [sb("ucar%d" % i, [P, KC, 2]) for i in range(2)]
        ptok = [sb("ptok0", [P, NT, 256], BF16)]
        xstage = sb("xstage", [P, 2, D])
        pT = sb("pT", [P, 2, TB], BF16)
        _ot = big[:, 4096:8192].bitcast(F32)
        otile = [_ot[:, i * D:(i + 1) * D] for i in range(2)]
        tmps = [sb("tmp%d" % i, [P, TB]) for i in range(NTMP)]
        tctr = [0]

        def newtmp():
            i = tctr[0] % NTMP
            tctr[0] += 1
            return tmps[i], ("tmp", i)

        banks = [st.enter_context(nc.psum_tensor("bank%d" % i, [P, 512], F32)) for i in range(8)]
        bctr = [0]

        def newbank():
            i = bctr[0] % 8
            bctr[0] += 1
            return banks[i], ("bank", i)

        g_mix = rowp_t[:, 0:D]
        g_ple = rowp_t[:, D:2 * D]
        g_fin = rowp_t[:, 2 * D:3 * D]
        mask4 = cm32[:, P:P + 512]
        tri_incl = cmb[:, 0:P]
        tri_strict = cmb[:, P:2 * P]
        chunkind = cmb[:, 2 * P:2 * P + 2]
        negcol = cmb[:, 2 * P + 2:2 * P + 3]
        tri_strict128 = cmb[:, 2 * P + 3:3 * P + 3]
        allneg = cmb[:, 3 * P + 3:4 * P + 3]
        ident32 = cm32[:, 0:P]
        Mst = big[:].bitcast(F32).rearrange("p (h f) -> p h f", h=4)
        xt = xtok[0]
        xk = "xtok0"
        xkt = lambda t: ("xt", t)

        S.add("sp", lambda e: e.dma_start(out=rowp_t[:], in_=rowp[:, 0:3 * D]), writes=["rowp"], dma="const", total=True)
        S.add("sp", lambda e: e.dma_start(out=colp_t[:], in_=colp), writes=["colp"], dma="const", total=True)
        S.add("sp", lambda e: e.dma_start(out=cm32[:, 0:P], in_=cmat[:, 0:P]), writes=["cm32a"], dma="const", total=True)
        S.add("sp", lambda e: e.dma_start(out=cm32[:, P:P + 512], in_=cmat[:, 4 * P:4 * P + 512]), writes=["cm32"], dma="const", total=True)
        S.add("pool", lambda e: e.dma_start(out=cmb[:, 0:2 * P], in_=cmat[:, 2 * P:4 * P]), writes=["cmb"], dma="const2", total=True)
        S.add("pool", lambda e: e.dma_start(out=cmb[:, 2 * P:4 * P + 3], in_=cmat[:, 4 * P + 512:CM_W]), writes=["cmb2"], dma="const2", total=True)
        S.add("dve", lambda e: e.memset(wst[:], 0.0), writes=["wst"])
        for g0 in (0, 32, 64):
            S.add("sp", lambda e, g0=g0: e.dma_start(out=wst[g0:g0 + 16, :], in_=w_gk2), reads=["wst"], writes=[("wstw", g0)], dma="const3", total=True)
        for g0 in (0, 64):
            S.add("sp", lambda e, g0=g0: e.dma_start(out=wst[g0 + 16:g0 + 17, :], in_=rowp[0:1, 3 * D:3 * D + 512]), reads=["wst"], writes=[("wstb", g0)], dma="const3", total=True)
        wst_keys = [("wstw", 0), ("wstw", 32), ("wstw", 64), ("wstb", 0), ("wstb", 64)]
        S.add("dve", lambda e: e.tensor_copy(out=wcat[:], in_=wst[:]), reads=wst_keys, writes=["wcat"])
        S.add("dve", lambda e: e.tensor_tensor(out=wcat[64:96, :], in0=wst[64:96, :], in1=wcat[64:96, :], op=ALU.subtract), reads=wst_keys + ["wcat"], writes=["wcat"])
        S.add("dve", lambda e: e.memset(glcat[0:32, :], 1.0), writes=["glcat"])
        S.add("dve", lambda e: e.memset(glcat[32:64, :], 0.0), writes=["glcat"])
        S.add("dve", lambda e: e.memset(glcat[64:128, :], 1.0), writes=["glcat"])
        S.add("dve", lambda e: e.memset(wgl3[:], 0.0), writes=["wgl0"])
        S.add("dve", lambda e: e.memset(Mst, 0.0), writes=MSTK)
        S.add("pool", lambda e: e.dma_start(out=identb[:], in_=cmat[:, 0:P]), writes=["identb"], dma="const2", total=True)
        S.add("pool", lambda e: e.dma_start(out=onesb[:], in_=cmat[:, P:2 * P]), writes=["onesb"], dma="const2", total=True)
        wview = lambda w, c0, n: w[:, c0:c0 + n].rearrange("(kc p) n -> p kc n", p=P)
        for g0 in (0, 32, 64):
            S.add("pool", lambda e, g0=g0: e.dma_start(out=wgl3[:, :, g0:g0 + 16], in_=wview(w_in, C_GL, 16)), reads=["wgl0"], writes=[("wgl", g0)], dma="const4", total=True)
        S.add("pool", lambda e: e.dma_start(out=wpe_t[:], in_=w_pe.rearrange("(kc p) n -> p kc n", p=P)), writes=["wpe"], dma="const2", total=True)

        conv_keys = {"A": [], "B": []}
        conv_todo = []

        def conv_piece(dst, src, c0, grp):
            for r0 in (0, 512):
                key = ("wconv", grp, len(conv_keys[grp]))
                conv_keys[grp].append(key)
                gid = gmap.setdefault((dst, c0), len(gmap))
                k0 = r0 // P
                conv_todo.append((lambda e, r0=r0, gid=gid, k0=k0: e.dma_start(
                    out=s_all[gid][:, k0 * 512:(k0 + 4) * 512].rearrange("p (kc n) -> p kc n", kc=4),
                    in_=src[r0:r0 + 512, c0:c0 + 512].rearrange("(kc p) n -> p kc n", p=P)), key, grp))
        grpA_cols = (C_V, C_V + 512, C_CC, C_CC + 512, C_XC, C_XC + 512, C_K)
        for c0 in grpA_cols:
            conv_piece(s_in, w_in, c0, "A")
        for c0 in [C_Q, C_ZA, C_ZA + 512, C_CB, C_CB + 512, C_ZC, C_ZC + 512] + [C_GA + 512 * i for i in range(4)]:
            conv_piece(s_in, w_in, c0, "B")
        for dst, src in ((s_a, w_a), (s_c, w_c), (s_o, w_o), (s_pg, w_pg)):
            for c0 in (0, 512):
                conv_piece(dst, src, c0, "B")

        def emit_conv(n):
            for _ in range(min(n, len(conv_todo))):
                fn, key, grp = conv_todo.pop(0)
                S.add("pool", fn, writes=[key], dma="wconv" + grp, total=True)

        S.add("dve", lambda e: e.memset(s32[:], 0.0), writes=S32K)

        wctr = [0]

        def load_group(src, c0):
            seq = wctr[0]
            i = seq % NWSLOT
            wctr[0] += 1
            key = ("wslot", i)
            grp = "A" if (src == s_in and c0 in grpA_cols) else "B"
            gid = gmap[(src, c0)]
            S.add("sp", lambda e: e.dma_start(out=wslot[i][:], in_=s_all[gid].rearrange("p (kc n) -> p kc n", kc=KC)),
                  reads=conv_keys[grp], writes=[key], dma=key)
            return (wslot[i], key, seq)

        def load_group_direct(src32, c0):
            seq = wctr[0]
            i = seq % NWSLOT
            wctr[0] += 1
            key = ("wslot", i)
            S.add("pool", lambda e: e.dma_start(out=wslot[i][:], in_=src32[:, c0:c0 + 512].rearrange("(kc p) n -> p kc n", p=P)),
                  writes=[key], dma=("wslot_sw", i))
            return (wslot[i], key, seq)

        def wchk(w):
            assert wctr[0] - w[2] <= NWSLOT, "weight slot reused while still needed"
            return w[0], w[1]

        def mm(out_ap, lhsT, rhs, start, stop, reads, bkey):
            S.add("pe", lambda e: e.matmul(out_ap, lhsT=lhsT, rhs=rhs, start=start, stop=stop), reads=reads, writes=[bkey])

        def fm_chunk(w, j0, m, act_t, akeys, ntok):
            wt, wkey = wchk(w)
            bank, bkey = newbank()
            for kc in range(KC):
                mm(bank[0:m, 0:ntok], wt[:, kc, j0:j0 + m], act_t[:, kc, 0:ntok], kc == 0, kc == KC - 1, [wkey] + akeys, bkey)
            return bank, bkey

        def tm_bank(wt, wkey, c0, act_t, akey, t, nk=KC):
            bank, bkey = newbank()
            for kc in range(nk):
                mm(bank[:, :], act_t[:, kc, t * P:(t + 1) * P], wt[:, kc, c0:c0 + 512], kc == 0, kc == nk - 1, [wkey, akey], bkey)
            return bank, bkey

        def act(out_ap, in_ap, func, reads, writes, **kw):
            S.add("act", lambda e: e.activation(out=out_ap, in_=in_ap, func=func, **kw), reads=reads, writes=writes)

        def acopy(out_ap, in_ap, reads, writes):
            S.add("act", lambda e: e.copy(out=out_ap, in_=in_ap), reads=reads, writes=writes)

        def tt(eng, out_ap, a, b, op, reads, writes):
            S.add(eng, lambda e: e.tensor_tensor(out=out_ap, in0=a, in1=b, op=op), reads=reads, writes=writes)

        def stt(eng, out_ap, a, scalar, b, op0, op1, reads, writes):
            S.add(eng, lambda e: e.scalar_tensor_tensor(out=out_ap, in0=a, scalar=scalar, in1=b, op0=op0, op1=op1), reads=reads, writes=writes)

        def rstd_from_ms(ms_ap, tmp_ap, out_ap, reads, writes):
            act(tmp_ap, ms_ap, AF.Ln, reads, [("lnstat", writes[0])], bias=EPS, scale=1.0)
            act(out_ap, tmp_ap, AF.Exp, [("lnstat", writes[0])], writes, scale=-0.5)

        def norm_and_transpose(src_tile, skey, nt, gain, stat_c0, ht=None, hkey="htok", dT=None, dkey="hT"):
            ht = htok if ht is None else ht
            dT = hT if dT is None else dT
            sk = skey if callable(skey) else (lambda t: skey)
            for t in range(nt):
                S.add("act", lambda e, t=t: e.activation(out=junk[:], in_=src_tile[:, t, :], func=AF.Square, scale=1.0 / 32.0,
                                                        accum_out=stat[:, stat_c0 + t:stat_c0 + t + 1]),
                      reads=[sk(t)], writes=["junk", ("ms", t)])
                rstd_from_ms(stat[:, stat_c0 + t:stat_c0 + t + 1], stat[:, 8 + t:9 + t], stat[:, 12 + t:13 + t], [("ms", t)], [("rstd", t)])
            for t in range(nt):
                hs = t
                stt("dve", ht[:, hs, :], src_tile[:, t, :], stat[:, 12 + t:13 + t], gain, ALU.mult, ALU.mult,
                    [sk(t), ("rstd", t), "rowp"], [(hkey, hs)])
                bank, bkey = newbank()
                bv = bank[:].bitcast(BF16)
                for c in range(KC):
                    S.add("pe", lambda e, hs=hs, c=c, bv=bv: e.transpose(out=bv[:, c * P:(c + 1) * P], in_=ht[:, hs, c * P:(c + 1) * P], identity=identb[:]),
                          reads=[(hkey, hs), "identb"], writes=[bkey])
                src3 = bv.rearrange("p (c n) -> p c n", c=KC)
                extra = [("ozT", c) for c in range(KC)] if dkey == "n1T" else []
                if t % 2 == 0:
                    acopy(dT[:, :, t * P:(t + 1) * P], src3, [], [bkey, (dkey, t)] + extra)
                else:
                    S.add("dve", lambda e, t=t, src3=src3: e.tensor_copy(out=dT[:, :, t * P:(t + 1) * P], in_=src3), reads=[], writes=[bkey, (dkey, t)] + extra)

        hT_keys = [("hT", t) for t in range(NT)]

        CMB = ["cmb", "cmb2"]

        def gate_softplus(actT, akeys, sh, sl, skey):
            bank, bkey = newbank()
            wglk = [("wgl", 0), ("wgl", 32), ("wgl", 64)]
            for kc in range(KC):
                mm(bank[:, 0:TB], wgl3[:, kc, :], actT[:, kc, :], kc == 0, kc == KC - 1, wglk + akeys, bkey)
            acopy(glcat[0:16, :], bank[0:16, 0:TB], [], [bkey, "glcat"])
            acopy(glcat[64:80, :], bank[64:80, 0:TB], [], [bkey, "glcat"])
            S.add("dve", lambda e, bank=bank: e.tensor_copy(out=glcat[32:48, :], in_=bank[32:48, 0:TB]), reads=[], writes=[bkey, "glcat"])
            tt("dve", glcat[32:48, :], bank[32:48, 0:TB], glcat[32:48, :], ALU.subtract, ["glcat"], [bkey, "glcat"])
            for t in range(NT):
                bank, bkey = newbank()
                cs = slice(t * P, (t + 1) * P)
                mm(bank[:, :], glcat[:, cs], wcat[:, :], True, True, ["glcat", "wcat"], bkey)
                et, ekey = newtmp()
                st_, stkey = newtmp()
                act(et[:], bank[:, :], AF.Exp, [], [bkey, ekey], scale=-1.0)
                act(st_[:], et[:], AF.Ln, [ekey], [stkey], bias=1.0, scale=1.0)
                S.add("pool", lambda e, t=t, st_=st_: e.tensor_copy(out=sh[:, t, :], in_=st_[:]), reads=[stkey], writes=[(skey + "h", t)])
                tt("pool", sl[:, t, :], st_[:], sh[:, t, :], ALU.subtract, [stkey, (skey + "h", t)], [(skey + "l", t)])

        def spmm(out_ap, use_sp_as_lhsT, sh, sl, skey, t, sp_cols, other, start, stop, bkey):
            for i, (sx, sfx) in enumerate(((sh, "h"), (sl, "l"))):
                a_ = sx[:, t, sp_cols]
                if use_sp_as_lhsT:
                    mm(out_ap, a_, other, start and i == 0, stop and i == 1, [(skey + sfx, t)] + CMB, bkey)
                else:
                    mm(out_ap, other, a_, start and i == 0, stop and i == 1, [(skey + sfx, t)] + CMB, bkey)

        def gate_and_state(with_sbf):
            wkg = load_group(s_in, C_K)
            wvg = [load_group(s_in, C_V), load_group(s_in, C_V + 512)]
            bankd, bdkey = newbank()
            for h in range(4):
                for t in range(NT):
                    spmm(bankd[:, h * 4 + t:h * 4 + t + 1], True, sph, spl, "sp", t, slice(h * P, (h + 1) * P), negcol, True, True, bdkey)
            act(dec[:, 0:16], bankd[:, 0:16], AF.Exp, [], [bdkey, "dec"])
            for t in range(NT):
                bank, bkey = newbank()
                spmm(bank[:, :], False, sph, spl, "sp", t, slice(0, 512), tri_strict, True, True, bkey)
                ed, edkey = newtmp()
                act(ed[:], bank[:, :], AF.Exp, [], [bkey, edkey])
                wt, wkey = wchk(wkg)
                kb, kbkey = tm_bank(wt, wkey, 0, hT, ("hT", t), t)
                tt("dve", kdec[:, t, :], kb[:, :], ed[:], ALU.mult, [edkey], [kbkey, ("kdec", t)])
                for half in range(2):
                    wt, wkey = wchk(wvg[half])
                    vb, vbkey = tm_bank(wt, wkey, 0, hT, ("hT", t), t)
                    ch = 2 * t + half
                    if half == 0:
                        acopy(vm[:, ch, :], vb[:, :], [], [vbkey, ("vm", ch)])
                    else:
                        S.add("dve", lambda e, ch=ch, vb=vb: e.tensor_copy(out=vm[:, ch, :], in_=vb[:, :]), reads=[], writes=[vbkey, ("vm", ch)])
            s32f = s32[:].rearrange("p h v -> p (h v)")
            if with_sbf:
                acopy(sbf[:, 0, :], s32f, S32K, [("sbf", 0)] + MSTK)
            for c in range(NT):
                t = c
                for hp in range(2):
                    bank, bkey = newbank()
                    for hh in range(2):
                        h = hp * 2 + hh
                        mm(bank[:, hh * 256:(hh + 1) * 256], kdec[:, t, h * P:(h + 1) * P],
                           vm[:, 2 * t + h // 2, (h % 2) * 256:(h % 2 + 1) * 256],
                           True, True, [("kdec", t), ("vm", 2 * t + h // 2)], bkey)
                    for hh in range(2):
                        h = hp * 2 + hh
                        stt("dve", s32[:, h, :], s32[:, h, :], dec[:, h * 4 + c:h * 4 + c + 1], bank[:, hh * 256:(hh + 1) * 256], ALU.mult, ALU.add,
                            [("s32", h), "dec"], [bkey, ("s32", h)])
                if with_sbf and c < NT - 1:
                    acopy(sbf[:, c + 1, :], s32f, S32K, [("sbf", c + 1)] + MSTK)
            return wkg

        htokB = qk[:].rearrange("p c n -> p (c n)").rearrange("p (t d) -> p t d", t=NT)
        psets = [dict(ht=htok, hk="htok", dT=hT, dk="hT", sh=sph, sl=spl, sk="sp"),
                 dict(ht=htokB, hk="htokB", dT=ozT, dk="hTB", sh=vm[:, 0:4, :], sl=vm[:, 4:8, :], sk="spB")]

        def pre_L(pb):
            for t in range(NT):
                S.add("sp", lambda e, pb=pb, t=t: e.dma_start(out=xt[:, t, :], in_=x_pre[pb * TB + t * P:pb * TB + (t + 1) * P, :]),
                      writes=[xkt(t)], dma=("xload", t))
            for t in range(NT):
                S.add("act", lambda e, t=t: e.activation(out=junk[:], in_=xt[:, t, :], func=AF.Square, scale=1.0 / 32.0, accum_out=stat[:, t:t + 1]),
                      reads=[xkt(t)], writes=["junk", ("ms", t)])
                rstd_from_ms(stat[:, t:t + 1], stat[:, 8 + t:9 + t], stat[:, 12 + t:13 + t], [("ms", t)], [("rstd", t)])

        def pre_H(pb, t):
            ps = psets[pb % 2]
            stt("dve", ps["ht"][:, t, :], xt[:, t, :], stat[:, 12 + t:13 + t], g_mix, ALU.mult, ALU.mult, [xkt(t), ("rstd", t), "rowp"], [(ps["hk"], t)])

        def pre_T(pb):
            ps = psets[pb % 2]
            for t in range(NT):
                bank, bkey = newbank()
                bv = bank[:].bitcast(BF16)
                for c in range(KC):
                    S.add("pe", lambda e, t=t, c=c, bv=bv, ps=ps: e.transpose(out=bv[:, c * P:(c + 1) * P], in_=ps["ht"][:, t, c * P:(c + 1) * P], identity=identb[:]),
                          reads=[(ps["hk"], t), "identb"], writes=[bkey])
                src3 = bv.rearrange("p (c n) -> p c n", c=KC)
                if t % 2 == 0:
                    acopy(ps["dT"][:, :, t * P:(t + 1) * P], src3, [], [bkey, (ps["dk"], t)])
                else:
                    S.add("dve", lambda e, t=t, src3=src3, ps=ps: e.tensor_copy(out=ps["dT"][:, :, t * P:(t + 1) * P], in_=src3), reads=[], writes=[bkey, (ps["dk"], t)])
            dkeys = [(ps["dk"], t) for t in range(NT)]
            gate_softplus(ps["dT"], dkeys, ps["sh"], ps["sl"], ps["sk"])

        def pre_B(pb, after_all=None):
            ps = psets[pb % 2]
            sh, sl, sk_ = ps["sh"], ps["sl"], ps["sk"]
            wkg = wk_pre
            bankd, bdkey = newbank()
            for h in range(4):
                for t in range(NT):
                    spmm(bankd[:, h:h + 1], True, sh, sl, sk_, t, slice(h * P, (h + 1) * P), negcol, t == 0, t == NT - 1, bdkey)
            act(dec[:, 0:4], bankd[:, 0:4], AF.Exp, [], [bdkey, "dec"])
            eds = []
            for t in range(NT):
                bank, bkey = newbank()
                spmm(bank[:, :], False, sh, sl, sk_, t, slice(0, 512), tri_strict128, True, t == NT - 1, bkey)
                for u in range(t + 1, NT):
                    spmm(bank[:, :], False, sh, sl, sk_, u, slice(0, 512), allneg, False, u == NT - 1, bkey)
                ed, edkey = newtmp()
                act(ed[:], bank[:, :], AF.Exp, [], [bkey, edkey])
                eds.append((ed, edkey))
            for t in range(NT):
                wt, wkey = wchk(wkg)
                kb, kbkey = tm_bank(wt, wkey, 0, ps["dT"], (ps["dk"], t), t)
                ed, edkey = eds[t]
                tt("dve", kdec[:, t, :], kb[:, :], ed[:], ALU.mult, [edkey], [kbkey, ("kdec", t)])
            for h in range(4):
                for half in range(2):
                    bank, bkey = newbank()
                    for t in range(NT):
                        mm(bank[:, :], kdec[:, t, h * P:(h + 1) * P], ps["ht"][:, t, half * 512:(half + 1) * 512], t == 0, t == NT - 1, [("kdec", t), (ps["hk"], t)], bkey)
                    stt("dve", Mst[:, h, half * 512:(half + 1) * 512], Mst[:, h, half * 512:(half + 1) * 512], dec[:, h:h + 1], bank[:, :],
                        ALU.mult, ALU.add, [("Mst", h, half), "dec"], [bkey, ("Mst", h, half)])
            if after_all is not None:
                after_all()

        wk_pre = load_group_direct(w_in, C_K) if npre > 0 else None
        for pb in range(min(2, npre)):
            pre_L(pb)
            for t in range(NT):
                pre_H(pb, t)
            pre_T(pb)
        for pb in range(npre):
            nb = pb + 2
            if nb < npre:
                pre_L(nb)
                pre_B(pb, after_all=lambda nb=nb: [pre_H(nb, t) for t in range(NT)])
                pre_T(nb)
            else:
                pre_B(pb)
            emit_conv(4)
        emit_conv(len(conv_todo))
        if npre > 1:
            fence_r = [(k, t) for k in ("htokB", "hTB", "spBh", "spBl") for t in range(NT)]
            fence_w = [(k, c) for k in ("qk", "ozT", "vm") for c in range(KC)]
            S.add("dve", lambda e: e.memset(stat[:, 0:1], 0.0), reads=fence_r, writes=fence_w + [("ms", 0)])
        if npre > 0:
            wvg = [load_group(s_in, C_V), load_group(s_in, C_V + 512)]
            for h in range(4):
                tb = [newbank(), newbank()]
                for c in range(KC):
                    bnk, bk_ = tb[c // 4]
                    S.add("pe", lambda e, h=h, c=c, bnk=bnk: e.transpose(out=bnk[:, (c % 4) * P:(c % 4 + 1) * P], in_=Mst[:, h, c * P:(c + 1) * P], identity=ident32),
                          reads=[("Mst", h, c // 4), "cm32a"], writes=[bk_])
                hi_, hik = newtmp()
                lo_, lok = newtmp()
                hib = hi_[:].bitcast(BF16)
                lob = lo_[:].bitcast(BF16)
                for g2 in range(2):
                    bnk, bk_ = tb[g2]
                    acopy(hib[:, g2 * 512:(g2 + 1) * 512], bnk[:, :], [], [bk_, hik])
                    tt("dve", lob[:, g2 * 512:(g2 + 1) * 512], bnk[:, :], hib[:, g2 * 512:(g2 + 1) * 512], ALU.subtract, [hik], [bk_, lok])
                wt, wkey = wchk(wvg[h // 2])
                sbk, sbkk = newbank()
                n = 0
                for src_, skey_ in ((hib, hik), (lob, lok)):
                    for c in range(KC):
                        mm(sbk[:, 0:256], src_[:, c * P:(c + 1) * P], wt[:, c, (h % 2) * 256:(h % 2 + 1) * 256], n == 0, n == 2 * KC - 1, [skey_, wkey], sbkk)
                        n += 1
                acopy(s32[:, h, :], sbk[:, 0:256], [], [sbkk, ("s32", h)])

        xp = xstage[:, 0, :]
        S.add("sp", lambda e: e.dma_start(out=xp, in_=x_prev), writes=[("xs", 0)], dma="xprev")
        xp3 = xstage[:, 0:1, :]
        norm_and_transpose(xp3, ("xs", 0), 1, g_mix, 0)
        for half in range(2):
            wcc = load_group(s_in, C_CC + half * 512)
            wxc = load_group(s_in, C_XC + half * 512)
            for cc_ in range(4):
                c = half * 4 + cc_
                b1, k1 = fm_chunk(wcc, cc_ * P, P, hT, [("hT", 0)], P)
                b2, k2 = fm_chunk(wxc, cc_ * P, P, hT, [("hT", 0)], P)
                tm_, tk_ = newtmp()
                acopy(tm_[:, 0:2], b1[:, P - 2:P], [], [k1, tk_])
                tt("dve", ucar[0][:, c, :], b2[:, P - 2:P], tm_[:, 0:2], ALU.mult, [tk_], [k2, ("uc", 0, c)])

        stg = [xstage[:, 0, :], xstage[:, 1, :],
               kdec[:].rearrange("p t n -> p (t n)").bitcast(F32),
               at_t[:].rearrange("p t n -> p (t n)").bitcast(F32)]
        stg_keys = [[("xs", 0)], [("xs", 1)], [("kdec", t_) for t_ in range(NT)], [("at", t_) for t_ in range(NT)]]

        def stage_load(blk, t):
            S.add("sp", lambda e, blk=blk, t=t: e.dma_start(out=stg[t], in_=x_own[blk * TB + t * P:blk * TB + (t + 1) * P, :]),
                  writes=stg_keys[t], dma=("xsload", t))

        def front1(blk):
            for t in range(NT):
                S.add("act", lambda e, t=t: e.activation(out=junk[:], in_=stg[t], func=AF.Square, scale=1.0 / 32.0, accum_out=stat[:, t:t + 1]),
                      reads=stg_keys[t], writes=["junk", ("ms", t)])
                rstd_from_ms(stat[:, t:t + 1], stat[:, 8 + t:9 + t], stat[:, 12 + t:13 + t], [("ms", t)], [("rstd", t)])
                stt("dve", htok[:, t, :], stg[t], stat[:, 12 + t:13 + t], g_mix, ALU.mult, ALU.mult, stg_keys[t] + [("rstd", t), "rowp"], [("htok", t)])

        def front2(blk, after_tile=None):
            for t in range(NT):
                bank, bkey = newbank()
                bv = bank[:].bitcast(BF16)
                for c in range(KC):
                    S.add("pe", lambda e, t=t, c=c, bv=bv: e.transpose(out=bv[:, c * P:(c + 1) * P], in_=htok[:, t, c * P:(c + 1) * P], identity=identb[:]),
                          reads=[("htok", t), "identb"], writes=[bkey])
                src3 = bv.rearrange("p (c n) -> p c n", c=KC)
                if t % 2 == 0:
                    acopy(hT[:, :, t * P:(t + 1) * P], src3, [], [bkey, ("hT", t)])
                else:
                    S.add("dve", lambda e, t=t, src3=src3: e.tensor_copy(out=hT[:, :, t * P:(t + 1) * P], in_=src3), reads=[], writes=[bkey, ("hT", t)])
                if after_tile is not None:
                    after_tile(t)
            gate_softplus(hT, hT_keys, sph, spl, "sp")

        def xt_load(blk):
            for t in range(NT):
                S.add("sp", lambda e, blk=blk, t=t: e.dma_start(out=xt[:, t, :], in_=x_own[blk * TB + t * P:blk * TB + (t + 1) * P, :]),
                      writes=[xkt(t)], dma=("xload", t))

        for t_ in range(NT):
            stage_load(0, t_)
        front1(0)
        front2(0)
        for blk in range(nown):
            slot = 0
            S.add("pool", lambda e, blk=blk: e.dma_start(out=ptok[0][:], in_=p_own[blk * TB:(blk + 1) * TB, :].rearrange("(t p) d -> p t d", p=P)),
                  writes=[("ptok", 0)], dma=("pload", 0))
            wkg = gate_and_state(True)

            wqg = load_group(s_in, C_Q)
            for h in range(4):
                bank, bkey = newbank()
                for t in range(NT):
                    spmm(bank[:, t * P:(t + 1) * P], True, sph, spl, "sp", t, slice(h * P, (h + 1) * P), tri_incl, True, True, bkey)
                eq_, eqk = newtmp()
                ek_, ekk = newtmp()
                act(eq_[:], bank[:, :], AF.Exp, [], [bkey, eqk])
                act(ek_[:], bank[:, :], AF.Exp, [], [bkey, ekk], scale=-1.0)
                qb, qbk = fm_chunk(wqg, h * P, P, hT, hT_keys, TB)
                stt("dve", qk[:, h, :], qb[:, :], 128.0 ** -0.5, eq_[:], ALU.mult, ALU.mult, [eqk], [qbk, ("qk", h)])
                kb, kbk = fm_chunk(wkg, h * P, P, hT, hT_keys, TB)
                tt("dve", qk[:, 4 + h, :], kb[:, :], ek_[:], ALU.mult, [ekk], [kbk, ("qk", 4 + h)])
            for t in range(NT):
                bank, bkey = newbank()
                for h in range(4):
                    mm(bank[:, h * P:(h + 1) * P], qk[:, 4 + h, t * P:(t + 1) * P], qk[:, h, t * P:(t + 1) * P], True, True, [("qk", 4 + h), ("qk", h)], bkey)
                tt("dve", at_t[:, t, :], bank[:, :], mask4, ALU.mult, ["cm32"], [bkey, ("at", t)])
            for h in range(4):
                if h % 2 == 0:
                    wzg = load_group(s_in, C_ZA + (h // 2) * 512)
                held = []
                for vh in range(2):
                    oc = h * 2 + vh
                    ob, ok = newbank()
                    for t in range(NT):
                        vch = 2 * t + h // 2
                        v0 = (h % 2) * 256 + vh * P
                        mm(ob[:, t * P:(t + 1) * P], vm[:, vch, v0:v0 + P], at_t[:, t, h * P:(h + 1) * P], True, False, [("vm", vch), ("at", t)], ok)
                        mm(ob[:, t * P:(t + 1) * P], sbf[:, t, h * 256 + vh * P:h * 256 + (vh + 1) * P],
                           qk[:, h, t * P:(t + 1) * P], False, True, [("sbf", t), ("qk", h)], ok)
                    act(sqb[:, vh, :], ob[:, :], AF.Square, [], [ok, ("sqb", vh)])
                    zb, zk = fm_chunk(wzg, (oc % 4) * P, P, hT, hT_keys, TB)
                    sz_, szk = newtmp()
                    act(sz_[:], zb[:, :], AF.Silu, [], [zk, szk])
                    held.append((ob, ok, sz_, szk))
                sbk, sbkk = newbank()
                mm(sbk[:, :], onesb[:], sqb[:, 0, :], True, False, ["onesb", ("sqb", 0)], sbkk)
                mm(sbk[:, :], onesb[:], sqb[:, 1, :], False, True, ["onesb", ("sqb", 1)], sbkk)
                ln_, lnk = newtmp()
                rs_, rsk = newtmp()
                act(ln_[:], sbk[:, :], AF.Ln, [], [sbkk, lnk], bias=EPS, scale=1.0)
                act(rs_[:], ln_[:], AF.Exp, [lnk], [rsk], scale=-0.5)
                for vh in range(2):
                    oc = h * 2 + vh
                    ob, ok, sz_, szk = held[vh]
                    tt("pool", sz_[:], sz_[:], rs_[:], ALU.mult, [szk, rsk], [szk])
                    stt("dve", ozT[:, oc, :], ob[:, :], colp_t[:, 40 + vh:41 + vh], sz_[:], ALU.mult, ALU.mult, [szk, "colp"], [ok, ("ozT", oc)] + [("n1T", t_) for t_ in range(NT)])
            us, uns = blk % 2, (blk + 1) % 2
            for half in range(2):
                wcc = load_group(s_in, C_CC + half * 512)
                wxc = load_group(s_in, C_XC + half * 512)
                wcb = load_group(s_in, C_CB + half * 512)
                wzc = load_group(s_in, C_ZC + half * 512)
                if half == 1 and blk + 1 < nown:
                    for t_ in range(NT):
                        stage_load(blk + 1, t_)
                for cc_ in range(4):
                    c = half * 4 + cc_
                    i2 = c % 2
                    ut = utmp[i2]
                    ukey = ("utmp", i2)
                    b1, k1 = fm_chunk(wcc, cc_ * P, P, hT, hT_keys, TB)
                    b2, k2 = fm_chunk(wxc, cc_ * P, P, hT, hT_keys, TB)
                    cs_, csk = newtmp()
                    acopy(cs_[:], b1[:, :], [], [k1, csk])
                    S.add("pool", lambda e, c=c, ut=ut, us=us: e.tensor_copy(out=ut[:, 0:2], in_=ucar[us][:, c, :]), reads=[("uc", us, c)], writes=[ukey])
                    tt("dve", ut[:, 2:TB + 2], b2[:, :], cs_[:], ALU.mult, [csk], [k2, ukey])
                    S.add("pool", lambda e, c=c, ut=ut, uns=uns: e.tensor_copy(out=ucar[uns][:, c, :], in_=ut[:, TB:TB + 2]), reads=[ukey], writes=[("uc", uns, c)])
                    cw = lambda k, c=c: colp_t[:, 16 + c * 3 + k:17 + c * 3 + k]
                    y_, yk = newtmp()
                    S.add("act", lambda e, ut=ut, y_=y_, cw=cw: e.mul(out=y_[:], in_=ut[:, 2:TB + 2], mul=cw(2)), reads=[ukey, "colp"], writes=[yk])
                    for kk in (1, 0):
                        y2_, y2k = newtmp()
                        S.add("act", lambda e, ut=ut, y2_=y2_, cw=cw, kk=kk: e.mul(out=y2_[:], in_=ut[:, kk:TB + kk], mul=cw(kk)), reads=[ukey, "colp"], writes=[y2k])
                        tt("pool", y_[:], y_[:], y2_[:], ALU.add, [yk, y2k], [yk])
                    b3, k3 = fm_chunk(wcb, cc_ * P, P, hT, hT_keys, TB)
                    b4, k4 = fm_chunk(wzc, cc_ * P, P, hT, hT_keys, TB)
                    tt("dve", y_[:], b3[:, :], y_[:], ALU.mult, [yk], [k3, yk])
                    sc_, sck = newtmp()
                    act(sc_[:], b4[:, :], AF.Silu, [], [k4, sck])
                    tt("pool", qk[:, c, :], y_[:], sc_[:], ALU.mult, [yk, sck], [("qk", c)])
            if blk + 1 < nown:
                front1(blk + 1)
            oz_keys = [("ozT", c) for c in range(KC)]
            cz_keys = [("qk", c) for c in range(KC)]
            for half in range(2):
                wag = load_group(s_a, half * 512)
                wgag = load_group(s_in, C_GA + half * 512)
                wcg = load_group(s_c, half * 512)
                wgcg = load_group(s_in, C_GC + half * 512)
                if half == 1:
                    xt_load(blk)
                for ee in range(4):
                    e_ = half * 4 + ee
                    ba, kba = fm_chunk(wag, ee * P, P, ozT, oz_keys, TB)
                    bg, kbg = fm_chunk(wgag, ee * P, P, hT, hT_keys, TB)
                    bc, kbc = fm_chunk(wcg, ee * P, P, qk, cz_keys, TB)
                    bh, kbh = fm_chunk(wgcg, ee * P, P, hT, hT_keys, TB)
                    sa_, sak = newtmp()
                    sc_, sck = newtmp()
                    act(sa_[:], bg[:, :], AF.Sigmoid, ["colp"], [kbg, sak], bias=colp_t[:, e_:e_ + 1], scale=1.0)
                    act(sc_[:], bh[:, :], AF.Sigmoid, ["colp"], [kbh, sck], bias=colp_t[:, 8 + e_:9 + e_], scale=1.0)
                    tt("dve", sa_[:], ba[:, :], sa_[:], ALU.mult, [sak], [kba, sak])
                    tt("dve", sc_[:], bc[:, :], sc_[:], ALU.mult, [sck], [kbc, sck])
                    tt("pool", vm[:, e_, :], sa_[:], sc_[:], ALU.add, [sak, sck], [("vm", e_)])
            m_keys = [("vm", c) for c in range(KC)]
            for half in range(2):
                wog = load_group(s_o, half * 512)
                for t in range(NT):
                    wt, wkey = wchk(wog)
                    bank, bkey = newbank()
                    for kc in range(KC):
                        mm(bank[:, :], vm[:, kc, t * P:(t + 1) * P], wt[:, kc, :], kc == 0, kc == KC - 1, [wkey] + m_keys, bkey)
                    tt("dve", xt[:, t, half * 512:(half + 1) * 512], bank[:, :], xt[:, t, half * 512:(half + 1) * 512], ALU.add, [xkt(t)], [bkey, xkt(t)])
            for t in range(NT):
                S.add("act", lambda e, t=t: e.activation(out=junk[:], in_=xt[:, t, :], func=AF.Square, scale=1.0 / 32.0, accum_out=stat[:, 4 + t:5 + t]),
                      reads=[xkt(t)], writes=["junk", ("ms", t)])
                rstd_from_ms(stat[:, 4 + t:5 + t], stat[:, 8 + t:9 + t], stat[:, 12 + t:13 + t], [("ms", t)], [("rstd", t)])

            def np_scale(t):
                stt("dve", htok[:, t, :], xt[:, t, :], stat[:, 12 + t:13 + t], g_ple, ALU.mult, ALU.mult, [xkt(t), ("rstd", t), "rowp"], [("htok", t)])
            if blk + 1 < nown:
                front2(blk + 1, after_tile=np_scale)
            else:
                for t in range(NT):
                    np_scale(t)
            n1_extra = [("ozT", c) for c in range(KC)]
            for t in range(NT):
                bank, bkey = newbank()
                bv = bank[:].bitcast(BF16)
                for c in range(KC):
                    S.add("pe", lambda e, t=t, c=c, bv=bv: e.transpose(out=bv[:, c * P:(c + 1) * P], in_=htok[:, t, c * P:(c + 1) * P], identity=identb[:]),
                          reads=[("htok", t), "identb"], writes=[bkey])
                src3 = bv.rearrange("p (c n) -> p c n", c=KC)
                if t % 2 == 0:
                    acopy(ozT[:, :, t * P:(t + 1) * P], src3, [], [bkey, ("n1T", t)] + n1_extra)
                else:
                    S.add("dve", lambda e, t=t, src3=src3: e.tensor_copy(out=ozT[:, :, t * P:(t + 1) * P], in_=src3), reads=[], writes=[bkey, ("n1T", t)] + n1_extra)
            for t in range(NT):
                bank, bkey = newbank()
                bv = bank[:].bitcast(BF16)
                for c in range(2):
                    S.add("pe", lambda e, t=t, c=c, bv=bv, slot=slot: e.transpose(out=bv[:, c * P:(c + 1) * P], in_=ptok[slot][:, t, c * P:(c + 1) * P], identity=identb[:]),
                          reads=[("ptok", slot), "identb"], writes=[bkey])
                acopy(pT[:, :, t * P:(t + 1) * P], bv[:, 0:2 * P].rearrange("p (c n) -> p c n", c=2), [], [bkey, ("pT", t)])
            for half in range(2):
                wgg = load_group(s_pg, half * 512)
                for t in range(NT):
                    wt, wkey = wchk(wgg)
                    bg, kbg = newbank()
                    for kc in range(KC):
                        mm(bg[:, :], ozT[:, kc, t * P:(t + 1) * P], wt[:, kc, :], kc == 0, kc == KC - 1, [wkey, ("n1T", t)], kbg)
                    bp, kbp = newbank()
                    for kc in range(2):
                        mm(bp[:, :], pT[:, kc, t * P:(t + 1) * P], wpe_t[:, kc, half * 512:(half + 1) * 512], kc == 0, kc == 1, ["wpe", ("pT", t)], kbp)
                    sg_, sgk = newtmp()
                    act(sg_[:], bg[:, :], AF.Sigmoid, [], [kbg, sgk])
                    tt("dve", sg_[:], bp[:, :], sg_[:], ALU.mult, [sgk], [kbp, sgk])
                    tt("pool", xt[:, t, half * 512:(half + 1) * 512], xt[:, t, half * 512:(half + 1) * 512], sg_[:], ALU.add, [xkt(t), sgk], [xkt(t)])
            for t in range(NT):
                i2 = t % 2
                S.add("act", lambda e, t=t: e.activation(out=junk[:], in_=xt[:, t, :], func=AF.Square, scale=1.0 / 32.0, accum_out=stat[:, t:t + 1]),
                      reads=[xkt(t)], writes=["junk", ("ms", t)])
                rstd_from_ms(stat[:, t:t + 1], stat[:, 8 + t:9 + t], stat[:, 12 + t:13 + t], [("ms", t)], [("rstd", t)])
                S.add("act", lambda e, t=t, i2=i2: e.mul(out=otile[i2][:], in_=xt[:, t, :], mul=stat[:, 12 + t:13 + t]),
                      reads=[xkt(t), ("rstd", t)], writes=[("otile", i2)])
                tt("pool", otile[i2][:], otile[i2][:], g_fin, ALU.mult, [("otile", i2), "rowp"], [("otile", i2)])
                S.add("pool", lambda e, t=t, i2=i2, blk=blk: e.dma_start(out=out[blk * TB + t * P:blk * TB + (t + 1) * P, :], in_=otile[i2][:]),
                      reads=[("otile", i2)], dma=("ostore", i2), out_dma=True)
        S.emit()
    return nc


def _consts():
    c = np.zeros((P, CM_W), np.float32)
    s = np.arange(P)[:, None]
    t = np.arange(P)[None, :]
    same = (s // P) == (t // P)
    c[:, 0:P] = np.eye(P, dtype=np.float32)
    c[:, P:2 * P] = 1.0 / 256.0
    c[:, 2 * P:3 * P] = np.where(same & (s <= t), -1.0 / 16.0, 0.0)
    c[:, 3 * P:4 * P] = np.where(same & (s > t), -1.0 / 16.0, 0.0)
    c[:, 4 * P:4 * P + 512] = np.tile(np.where(same & (s <= t), 1.0, 0.0).astype(np.float32), (1, 4))
    o = 4 * P + 512
    c[:, o] = np.where(np.arange(P) < 64, -1.0 / 16.0, 0.0)
    c[:, o + 1] = np.where(np.arange(P) >= 64, -1.0 / 16.0, 0.0)
    c[:, o + 2] = -1.0 / 16.0
    c[:, o + 3:o + 3 + P] = np.where(s > t, -1.0 / 16.0, 0.0)
    c[:, o + 3 + P:o + 3 + 2 * P] = -1.0 / 16.0
    return c


_NC_CACHE = {}


def kernel(x, p, norm_mix_g, w_in, b_merge, w_gk2, b_gk, gla_norm_g, conv_w, w_branch_a, w_branch_c, w_out,
           norm_ple_g, w_ple_gate, w_ple_proj, norm_final_g):
    f = lambda a: np.ascontiguousarray(np.asarray(a, dtype=np.float32))
    x, p = f(x), f(p)
    if "nc" not in _NC_CACHE:
        _NC_CACHE["nc"] = build_program()
    nc = _NC_CACHE["nc"]
    rowp = np.concatenate([f(norm_mix_g)[0], f(norm_ple_g)[0], f(norm_final_g), f(b_gk)[0]])[None, :].repeat(P, 0)
    colp = np.zeros((P, 42), np.float32)
    bm = f(b_merge)[0]
    colp[:, 0:8] = bm[:D].reshape(8, P).T
    colp[:, 8:16] = bm[D:].reshape(8, P).T
    cw = f(conv_w)[0]
    colp[:, 16:40] = cw.reshape(3, 8, P).transpose(2, 1, 0).reshape(P, 24)
    colp[:, 40:42] = f(gla_norm_g)[0].reshape(2, P).T
    cm = _consts()
    shared = {
        "w_in": f(w_in)[0], "w_gk2": f(w_gk2)[0], "w_a": f(w_branch_a)[0], "w_c": f(w_branch_c)[0],
        "w_o": f(w_out)[0], "w_pg": f(w_ple_gate)[0], "w_pe": f(w_ple_proj)[0],
        "rowp": np.ascontiguousarray(rowp), "colp": colp, "cmat": cm,
    }
    in_maps = []
    for c in range(NCORES):
        b, j = c // 4, c % 4
        s0 = j * SEG
        xpre = np.zeros((NPRE * TB, D), np.float32)
        if s0 > 0:
            xpre[NPRE * TB - s0:] = x[b, 0:s0]
        xprev = np.zeros((P, D), np.float32)
        if s0 > 0:
            xprev[:] = x[b, s0 - P:s0]
        m = dict(shared)
        m.update({"x_own": np.ascontiguousarray(x[b, s0:s0 + SEG]), "x_pre": xpre, "x_prev": xprev,
                  "p_own": np.ascontiguousarray(p[0, b, s0:s0 + SEG])})
        in_maps.append(m)
    res = run_bass_kernel_spmd(nc, in_maps, core_ids=list(range(NCORES)))
    outp = np.zeros((2, SEQ, D), np.float32)
    for c in range(NCORES):
        b, j = c // 4, c % 4
        outp[b, j * SEG:(j + 1) * SEG] = np.asarray(res.results[c]["out"], dtype=np.float32)
    return outp
```
